# Optimizing a Trainium2 kernel written in Bass

```python
import jax, jax.numpy as jnp
from jax import lax
import numpy as np

D_MODEL = 1024
BATCH = 8
SEQ = 4096
DEPTH = 2

HEAD_DIM = 64
SGU_WIDTH = 3 * D_MODEL // 8
CONV_WIDTH = 3 * D_MODEL // 8
POOL_WIDTH = D_MODEL - SGU_WIDTH - CONV_WIDTH
SGU_HEADS = SGU_WIDTH // HEAD_DIM
CHUNK = 128
CONV_K = 31
POOL_WINDOWS = (2, 4, 8, 16)
POOL_GROUPS = len(POOL_WINDOWS)
POOL_GDIM = POOL_WIDTH // POOL_GROUPS
IN_WIDTH = 2 * SGU_WIDTH + 2 * CONV_WIDTH + POOL_WIDTH
D_FF = ((8 * D_MODEL // 3 + 127) // 128) * 128
FFN_CONV_K = 3
N_MOD = 6
EPS = 1e-6
MOD_INIT_STD = 0.02

kernel_name = "hybrid_sgu_conformer_pool_convffn_block"


def rms_norm(x, g):
    xf = x.astype(jnp.float32)
    y = xf * lax.rsqrt(jnp.mean(xf * xf, axis=-1, keepdims=True) + EPS)
    return (y * g.astype(jnp.float32)).astype(x.dtype)


def layer_norm(x, g, b):
    xf = x.astype(jnp.float32)
    mu = jnp.mean(xf, axis=-1, keepdims=True)
    var = jnp.mean(jnp.square(xf - mu), axis=-1, keepdims=True)
    y = (xf - mu) * lax.rsqrt(var + EPS)
    return (y * g.astype(jnp.float32) + b.astype(jnp.float32)).astype(x.dtype)


def causal_depthwise_conv(x, w, b):
    k, ch = w.shape
    y = lax.conv_general_dilated(
        x, w[:, None, :].astype(x.dtype), window_strides=(1,), padding=[(k - 1, 0)],
        dimension_numbers=("NWC", "WIO", "NWC"), feature_group_count=ch)
    return y + b


def spatial_gating(z, norm_g, norm_b, w_s, b_s):
    bsz, s, _ = z.shape
    u, v = jnp.split(z, 2, axis=-1)
    v = v.reshape(bsz, s // CHUNK, CHUNK, SGU_HEADS, HEAD_DIM)
    v = layer_norm(v, norm_g.reshape(SGU_HEADS, HEAD_DIM), norm_b.reshape(SGU_HEADS, HEAD_DIM))
    mask = jnp.tril(jnp.ones((CHUNK, CHUNK), dtype=bool))
    w = jnp.where(mask[None], w_s, jnp.zeros_like(w_s))
    f = jnp.einsum("hts,bnshd->bnthd", w, v) + b_s.T[None, None, :, :, None]
    return u * f.reshape(bsz, s, SGU_WIDTH)


def conformer_conv(z, conv_w, conv_b, norm_g, norm_b):
    a, g = jnp.split(z, 2, axis=-1)
    h = a * jax.nn.sigmoid(g)
    h = causal_depthwise_conv(h, conv_w, conv_b)
    return jax.nn.silu(layer_norm(h, norm_g, norm_b))


def multiscale_pool(z, pool_w, pool_scale):
    bsz, s, _ = z.shape
    zf = z.astype(jnp.float32)
    cs = jnp.pad(jnp.cumsum(zf, axis=1), ((0, 0), (1, 0), (0, 0)))
    pos1 = jnp.arange(1, s + 1, dtype=jnp.int32)
    means = []
    for gi, win in enumerate(POOL_WINDOWS):
        sl = slice(gi * POOL_GDIM, (gi + 1) * POOL_GDIM)
        hi = cs[:, 1:, sl]
        lo = jnp.pad(cs[:, : s + 1 - win, sl], ((0, 0), (win - 1, 0), (0, 0)))
        count = jnp.minimum(pos1, win).astype(jnp.float32)[None, :, None]
        means.append((hi - lo) / count)
    d = (jnp.concatenate(means, axis=-1) - zf).astype(z.dtype)
    d = d.reshape(bsz, s, POOL_GROUPS, POOL_GDIM)
    y = jnp.einsum("bsgc,gcd->bsgd", d, pool_w).reshape(bsz, s, POOL_WIDTH)
    return y * pool_scale


def setup_inputs(seed: int = 0) -> dict:
    key = jax.random.key(seed)
    ks = jax.random.split(key, 32)
    L, D = DEPTH, D_MODEL
    nrm = lambda k, shape, std: (jax.random.normal(k, shape, jnp.float32) * std)
    gain = lambda k, shape: 1.0 + 0.05 * jax.random.normal(k, shape, jnp.float32)
    return {
        "x": jax.random.normal(ks[0], (BATCH, SEQ, D), jnp.float32),
        "c": jax.random.normal(ks[1], (BATCH, D), jnp.float32),
        "mod_w": nrm(ks[2], (L, D, N_MOD * D), MOD_INIT_STD),
        "mod_b": nrm(ks[3], (L, N_MOD * D), 0.01),
        "mix_pre_g": gain(ks[4], (L, D)),
        "mix_post_g": gain(ks[5], (L, D)),
        "w_in": nrm(ks[6], (L, D, IN_WIDTH), D ** -0.5),
        "sgu_norm_g": gain(ks[7], (L, SGU_WIDTH)),
        "sgu_norm_b": nrm(ks[8], (L, SGU_WIDTH), 0.02),
        "sgu_w": nrm(ks[9], (L, SGU_HEADS, CHUNK, CHUNK), CHUNK ** -0.5),
        "sgu_b": gain(ks[10], (L, SGU_HEADS, CHUNK)),
        "conv_w": nrm(ks[11], (L, CONV_K, CONV_WIDTH), CONV_K ** -0.5),
        "conv_b": nrm(ks[12], (L, CONV_WIDTH), 0.02),
        "conv_norm_g": gain(ks[13], (L, CONV_WIDTH)),
        "conv_norm_b": nrm(ks[14], (L, CONV_WIDTH), 0.02),
        "pool_w": nrm(ks[15], (L, POOL_GROUPS, POOL_GDIM, POOL_GDIM), POOL_GDIM ** -0.5),
        "pool_scale": gain(ks[16], (L, POOL_WIDTH)),
        "branch_g": gain(ks[17], (L, D)),
        "w_out": nrm(ks[18], (L, D, D), D ** -0.5),
        "ffn_pre_g": gain(ks[19], (L, D)),
        "ffn_post_g": gain(ks[20], (L, D)),
        "ffn_up": nrm(ks[21], (L, D, 2 * D_FF), D ** -0.5),
        "ffn_conv_w": nrm(ks[22], (L, FFN_CONV_K, 2 * D_FF), FFN_CONV_K ** -0.5),
        "ffn_conv_b": nrm(ks[23], (L, 2 * D_FF), 0.02),
        "ffn_down": nrm(ks[24], (L, D_FF, D), D_FF ** -0.5),
    }


def reference(x, c, mod_w, mod_b, mix_pre_g, mix_post_g, w_in, sgu_norm_g, sgu_norm_b, sgu_w, sgu_b,
              conv_w, conv_b, conv_norm_g, conv_norm_b, pool_w, pool_scale, branch_g, w_out,
              ffn_pre_g, ffn_post_g, ffn_up, ffn_conv_w, ffn_conv_b, ffn_down):
    sc = jax.nn.silu(c)
    for l in range(DEPTH):
        mod = sc @ mod_w[l] + mod_b[l]
        sh1, sc1, g1, sh2, sc2, g2 = [m[:, None, :] for m in jnp.split(mod, N_MOD, axis=-1)]

        h = rms_norm(x, mix_pre_g[l]) * (1.0 + sc1) + sh1
        z = h @ w_in[l]
        z_a, z_b, z_c = jnp.split(z, [2 * SGU_WIDTH, 2 * SGU_WIDTH + 2 * CONV_WIDTH], axis=-1)
        y_a = spatial_gating(jax.nn.gelu(z_a), sgu_norm_g[l], sgu_norm_b[l], sgu_w[l], sgu_b[l])
        y_b = conformer_conv(z_b, conv_w[l], conv_b[l], conv_norm_g[l], conv_norm_b[l])
        y_c = multiscale_pool(z_c, pool_w[l], pool_scale[l])
        ga, gb, gc = jnp.split(branch_g[l], [SGU_WIDTH, SGU_WIDTH + CONV_WIDTH])
        y = jnp.concatenate([rms_norm(y_a, ga), rms_norm(y_b, gb), rms_norm(y_c, gc)], axis=-1)
        y = y @ w_out[l]
        x = x + g1 * rms_norm(y, mix_post_g[l])

        h = rms_norm(x, ffn_pre_g[l]) * (1.0 + sc2) + sh2
        u = causal_depthwise_conv(h @ ffn_up[l], ffn_conv_w[l], ffn_conv_b[l])
        ug, uv = jnp.split(u, 2, axis=-1)
        y = (jax.nn.gelu(ug) * uv) @ ffn_down[l]
        x = x + g2 * rms_norm(y, ffn_post_g[l])
    return x
```

```python
import contextlib
import numpy as np
import concourse.bass as bass
import concourse.mybir as mybir
from concourse.bass_utils import run_bass_kernel_spmd

F32 = mybir.dt.float32
BF16 = mybir.dt.bfloat16
AF = mybir.ActivationFunctionType
ALU = mybir.AluOpType
AX = mybir.AxisListType

D_MODEL = 1024
BATCH = 8
SEQ = 4096
DEPTH = 2
T = 512
NT = SEQ // T
D_FF = 2816
NJ = D_FF // 128
EPS = 1e-6
NPV = 374
IN_ORDER = [9, 6, 10, 7, 11, 8, 3, 4, 5, 12, 13, 0, 1, 2]

PV_PRE, PV_POST, PV_BR, PV_FPRE, PV_FPOST = 0, 8, 16, 24, 32
PV_SG, PV_SB, PV_CB, PV_CNG, PV_CNB, PV_PS = 40, 43, 46, 49, 52, 55
PV_CW, PV_FW, PV_FB, PV_MB = 57, 150, 282, 326


class Buf:
    __slots__ = ("name", "w", "r", "al", "excl")

    def __init__(self, name):
        self.name = name
        self.w = None
        self.r = []
        self.al = ()
        self.excl = False


class Eng:
    def __init__(self, name, attr):
        self.name = name
        self.attr = attr
        self.ops = []
        self.count = 0
        self.waited = {}


class Prog:
    def __init__(self, nc):
        self.nc = nc
        self.es = contextlib.ExitStack()
        self.pe = Eng("pe", "tensor")
        self.act = Eng("act", "scalar")
        self.dve = Eng("dve", "vector")
        self.pool = Eng("pool", "gpsimd")
        self.sp = Eng("sp", "sync")
        self.engs = [self.pe, self.act, self.dve, self.pool, self.sp]
        self.sems = {}
        for e in self.engs:
            self.sems[e.name] = self.es.enter_context(nc.semaphore("s_" + e.name))
        self.dma_counts = {}
        self.nbuf = 0

    def sbuf(self, name, shape, dt):
        return self.es.enter_context(self.nc.sbuf_tensor(name, list(shape), dt))

    def psum(self, name, shape, dt=F32):
        return self.es.enter_context(self.nc.psum_tensor(name, list(shape), dt))

    def buf(self, name=None):
        self.nbuf += 1
        return Buf(name or "b%d" % self.nbuf)

    def bufs(self, n, name=None):
        return [self.buf(None if name is None else "%s%d" % (name, i)) for i in range(n)]

    def dma_sem(self, name):
        key = "d_" + name
        self.sems[key] = self.es.enter_context(self.nc.semaphore(key))
        self.dma_counts[key] = 0
        return key

    def _collect(self, eng, reads, writes):
        waits = {}

        def need(tok, skip_same):
            if tok is None:
                return
            k, v = tok
            if skip_same and k == eng.name:
                return
            if eng.waited.get(k, 0) >= v:
                return
            if waits.get(k, 0) < v:
                waits[k] = v

        pe_same = eng.name == "pe"
        for b0 in reads:
            for b in (b0,) + tuple(b0.al):
                need(b.w, False)
                if b.excl:
                    for t in b.r:
                        need(t, True)
        for b0 in writes:
            for b in (b0,) + tuple(b0.al):
                need(b.w, pe_same)
                for t in b.r:
                    need(t, pe_same)
        for k, v in waits.items():
            eng.waited[k] = v
        return sorted(waits.items())

    def op(self, eng, fn, reads=(), writes=(), signal=True):
        waits = self._collect(eng, reads, writes)
        tok = (eng.name, eng.count + 1)
        for b in reads:
            b.r.append(tok)
        for b in writes:
            b.w = tok
            b.r = []
        inc = None
        if signal:
            eng.count += 1
            inc = (eng.name, 1)
        eng.ops.append((waits, fn, inc))

    def dma(self, eng, semkey, out, in_, reads=(), writes=(), **kw):
        waits = self._collect(eng, reads, writes)
        self.dma_counts[semkey] += 16
        tok = (semkey, self.dma_counts[semkey])
        for b in reads:
            b.r.append(tok)
        for b in writes:
            b.w = tok
            b.r = []

        def fn(e, out=out, in_=in_, kw=kw):
            return e.dma_start(out=out, in_=in_, **kw)
        eng.ops.append((waits, fn, (semkey, 16)))
        return tok

    def wait_tok(self, eng, tok):
        k, v = tok
        if eng.waited.get(k, 0) >= v:
            return
        eng.waited[k] = v
        eng.ops.append(([(k, v)], None, None))

    def barrier(self):
        toks = [(e.name, e.count) for e in self.engs if e.count > 0]
        toks += [(k, v) for k, v in self.dma_counts.items() if v > 0]
        for e in self.engs:
            for t in toks:
                if t[0] != e.name:
                    self.wait_tok(e, t)

    def emit(self):
        nc = self.nc
        with nc.Block() as block:
            for e in self.engs:
                if not e.ops:
                    continue

                def body(h, e=e):
                    for waits, fn, inc in e.ops:
                        for k, v in waits:
                            h.wait_ge(self.sems[k], v)
                        if fn is not None:
                            ins = fn(h)
                            if inc is not None:
                                ins.then_inc(self.sems[inc[0]], inc[1])
                getattr(block, e.attr)(body)
        self.es.close()


class Rot:
    def __init__(self, items):
        self.items = items
        self.i = 0

    def get(self):
        it = self.items[self.i % len(self.items)]
        self.i += 1
        return it


class _Stop(Exception):
    pass


def build(layers, ntiles=NT, dbg=(), stop_after=None):
    nc = bass.Bass("TRN2", target_bir_lowering=False)
    P = Prog(nc)
    L = len(layers)
    dbg = set(dbg)
    dbg_out = {}

    def din(name, shape, dt=F32):
        return nc.dram_tensor(name, list(shape), dt, kind="ExternalInput").ap()

    x_fm = din("x_fm", [ntiles, 128, 8, T])
    out_fm = nc.dram_tensor("out_fm", [ntiles, 128, 8, T], F32, kind="ExternalOutput").ap()
    c_fm = din("c_fm", [128, 8])
    pv_d = din("pv", [L, 128, NPV])
    mod_d = din("mod_c", [L, 12, 128, 4, 8, 128])
    win_d = din("w_in_c", [L, 14, 128, 8, 128])
    wout_d = din("w_out_c", [L, 8, 128, 8, 128])
    up_d = din("up_c", [L, NJ, 128, 8, 256])
    dn_d = din("down_c", [L, 8, 128, NJ, 128])
    wst_d = din("wst", [L, 128, 6, 128])
    sgub_d = din("sgub", [L, 6, 128])
    pwbd_d = din("pwbd", [L, 128, 2, 128])
    ident_d = din("identh", [128, 128])
    identf_d = din("identf", [128, 128])
    tri_d = din("tri", [128, 128])
    ic0_d = din("ic0", [128, 2, 16])
    invw_d = din("invw", [128, 2])

    def dscr(name, shape):
        return nc.dram_tensor(name, list(shape), BF16, kind="Internal").ap()
    win_b = dscr("w_in_b", [L, 14, 128, 8, 128])
    wout_b = dscr("w_out_b", [L, 8, 128, 8, 128])
    up_b = dscr("up_b", [L, NJ, 128, 8, 256])
    dn_b = dscr("down_b", [L, 8, 128, NJ, 128])
    b_win = P.bufs(L); b_wout = P.bufs(L); b_up = P.bufs(L); b_dn = P.bufs(L)
    dg_b = dscr("dg_b", [L, 3, 128, 31, 128]); b_dgb = P.bufs(L, "dgb")

    PV = P.sbuf("PV", [128, L, NPV], F32); b_PV = P.buf("PV")
    MOD = P.sbuf("MOD", [128, L, 48], F32); b_MOD = P.buf("MOD")
    DV = P.sbuf("DV", [128, L, 40], F32); b_DV = P.buf("DV")
    CF = P.sbuf("CF", [128, 8], F32); b_CF = P.buf("CF")
    CF2 = P.sbuf("CF2", [128, 8], F32); b_CF2 = P.buf("CF2")
    SCb = P.sbuf("SCb", [128, 8], BF16); b_SCb = P.buf("SCb")
    WST = P.sbuf("WST", [128, L, 6, 128], BF16); b_WST = P.buf("WST")
    CC = P.sbuf("CC", [128, L, 3, 128], F32); b_CC = P.buf("CC")
    PWb = P.sbuf("PWb", [128, L, 2, 128], BF16); b_PWb = P.buf("PWb")
    onesb = P.sbuf("onesb", [128, 128], BF16); b_onesb = P.buf("onesb")
    onesf = P.sbuf("onesf", [128, 128], F32); b_onesf = P.buf("onesf")
    identh = P.sbuf("identh_s", [128, 128], BF16); b_identh = P.buf("identh")
    identf = P.sbuf("identf_s", [128, 128], F32); b_identf = P.buf("identf")
    NH4 = P.sbuf("NH4", [128, 4], F32); b_NH4 = P.buf("NH4")
    S4 = P.sbuf("S4", [128, 8, 4], F32); r_S4 = Rot([(S4[:, i, :], P.buf("S4_%d" % i)) for i in range(8)])
    D4 = P.sbuf("D4", [128, 1, 4, 128], F32); r_D4 = Rot([(D4[:, i], P.buf("D4_%d" % i)) for i in range(1)])
    IC0 = P.sbuf("IC0", [128, 2, 16], F32); b_IC0 = P.buf("IC0")
    INVW = P.sbuf("INVW", [128, 2], F32); b_INVW = P.buf("INVW")
    TRI = P.sbuf("TRI", [128, 128], F32); b_TRI = P.buf("TRI")
    HC = P.sbuf("HC", [128, L, 3, 30 + T], BF16); b_HC = [P.bufs(3) for _ in range(L)]
    ZC = P.sbuf("ZC", [128, L, 2, 15 + T], F32); b_ZC = [P.bufs(2) for _ in range(L)]
    ZT = P.sbuf("ZT", [128, L, NJ, 2, 2], F32); b_ZT = P.bufs(L)
    CORR = P.sbuf("CORR", [128, NJ, 2, 2], F32); b_CORR = P.buf("CORR")
    CT = P.sbuf("CT", [128, NJ, 2, 2], F32); b_CT = P.buf("CT")
    XT = P.sbuf("XT", [128, 8, T], F32); b_XT = P.bufs(8, "XT")
    HT = P.sbuf("HT", [128, 8, T], BF16); b_HT = P.bufs(8, "HT")
    Y32 = P.sbuf("Y32", [128, 8, T], F32); b_Y32 = P.bufs(8, "Y32")
    YT = P.sbuf("YT", [128, 8, T], BF16); b_YT = P.bufs(8, "YT")
    SQ = P.sbuf("SQ", [128, 3, T], BF16); r_SQ = Rot([(SQ[:, i, :], P.buf("SQ%d" % i)) for i in range(3)])
    TMP = P.sbuf("TMP", [128, 3, T], F32); r_TMP = Rot([(TMP[:, i, :], P.buf("TMP%d" % i)) for i in range(3)])
    SS = P.sbuf("SS", [128, 3, T], F32); r_SS = Rot([(SS[:, i, :], P.buf("SS%d" % i)) for i in range(3)])
    AR = P.sbuf("AR", [128, NJ * T], BF16)
    AT = AR[:].rearrange("p (j t) -> p j t", j=NJ); b_AT = P.bufs(NJ, "AT")
    ARf = AR[:].bitcast(F32)
    UU = ARf[:, 0:1536].rearrange("p (c t) -> p c t", c=3); b_UU = P.bufs(3, "UU")
    TH = ARf[:, 1536:2560].rearrange("p (c t) -> p c t", c=2); r_TH = Rot([(TH[:, i, :], P.buf("TH%d" % i)) for i in range(2)])
    PA = ARf[:, 2560:2560 + 2 * 527].rearrange("p (c t) -> p c t", c=2); b_PA = P.buf("PA")
    PB = ARf[:, 3616:3616 + 2 * 527].rearrange("p (c t) -> p c t", c=2); b_PB = P.buf("PB")
    DD = AR[:, 2 * 4672:2 * 4672 + 2 * T].rearrange("p (c t) -> p c t", c=2); b_DD = P.bufs(2, "DD")
    mix_scr = b_UU + [it[1] for it in r_TH.items] + [b_PA, b_PB] + b_DD
    for b in mix_scr:
        b.al = tuple(b_AT)
    for b in b_AT:
        b.al = tuple(mix_scr)
    VV4 = P.sbuf("VV4", [128, 4, 384], F32); b_VV4 = P.bufs(4, "VV4")
    VSQ = P.sbuf("VSQ", [128, 384], F32); b_VSQ = P.buf("VSQ")
    VST = P.sbuf("VST", [128, 5, 24], F32); b_VST = P.buf("VST"); b_VST2 = P.buf("VST2"); b_VST3 = P.buf("VST3")
    NH24 = P.sbuf("NH24", [128, 24], F32); b_NH24 = P.buf("NH24")
    VN = P.sbuf("VN", [128, 4, 384], BF16); b_VN = P.bufs(4, "VN")
    ACC = P.sbuf("ACC", [128, 5, 2, T], F32); r_ACC = Rot([(ACC[:, i], P.buf("ACC%d" % i)) for i in range(5)])
    GL = P.sbuf("GL", [128, 3, T], F32); r_GL = Rot([(GL[:, i, :], P.buf("GL%d" % i)) for i in range(3)])
    DG = P.sbuf("DG", [128, 3, 31, 128], BF16); b_DG = P.bufs(3, "DG")
    WA = P.sbuf("WA", [128, 4, 8, 128], BF16); b_WA = P.bufs(4, "WA")
    WU = P.sbuf("WU", [128, 3, 8, 256], BF16); b_WU = P.bufs(3, "WU")
    WD = P.sbuf("WD", [128, 2, NJ, 128], BF16); b_WD = P.bufs(2, "WD")
    PSG = P.psum("PSG", [128, 6, T]); b_PSG = P.bufs(6, "PSG")
    PST = P.psum("PST", [128, 2, T]); b_PST = P.bufs(2, "PST")
    for b in b_PSG + b_PST:
        b.excl = True
    gen_i = [0]

    def gbank():
        i = gen_i[0] % 6
        gen_i[0] += 1
        return i

    pair_i = [0]

    def gpair():
        i = pair_i[0] % 3
        pair_i[0] += 1
        return 2 * i
    st_i = [0]

    def sbank():
        i = st_i[0] % 2
        st_i[0] += 1
        return i

    def ACT(out, in_, func, reads, writes, scale=1.0, bias=0.0):
        P.op(P.act, lambda e: e.activation(out=out, in_=in_, func=func, scale=scale, bias=bias), reads, writes)

    def TT(eng, out, in0, in1, op, reads, writes):
        P.op(eng, lambda e: e.tensor_tensor(out=out, in0=in0, in1=in1, op=op), reads, writes)

    def STT(out, in0, scalar, in1, op0, op1, reads, writes):
        P.op(P.dve, lambda e: e.scalar_tensor_tensor(out=out, in0=in0, scalar=scalar, in1=in1, op0=op0, op1=op1), reads, writes)

    def TS(out, in0, s1, s2, op0, op1, reads, writes):
        if s2 is None:
            P.op(P.dve, lambda e: e.tensor_scalar(out=out, in0=in0, scalar1=s1, scalar2=None, op0=op0), reads, writes)
        else:
            P.op(P.dve, lambda e: e.tensor_scalar(out=out, in0=in0, scalar1=s1, scalar2=s2, op0=op0, op1=op1), reads, writes)

    def CP(eng, out, in_, reads, writes):
        P.op(eng, lambda e: e.tensor_copy(out=out, in_=in_), reads, writes)

    def MM(out, lhsT, rhs, start, stop, reads, writes, signal=True):
        P.op(P.pe, lambda e: e.matmul(out, lhsT=lhsT, rhs=rhs, start=start, stop=stop), reads, writes, signal=signal)

    dsem = {}

    def DMA(eng, sem, out, in_, reads=(), writes=()):
        if sem not in dsem:
            dsem[sem] = P.dma_sem(sem)
        return P.dma(eng, dsem[sem], out, in_, reads, writes)

    def stage(name):
        if stop_after == name:
            raise _Stop()

    def dump(name, ap, bufs, shape):
        if name in dbg and name not in dbg_out:
            t = nc.dram_tensor("dbg_" + name, list(shape), ap.dtype, kind="ExternalOutput").ap()
            dbg_out[name] = t
            DMA(P.sp, "dbg_" + name, t, ap, reads=bufs)

    try:
        DMA(P.sp, "c0", PV[:], pv_d.rearrange("l p n -> p l n"), writes=[b_PV])
        DMA(P.sp, "c1", CF[:], c_fm, writes=[b_CF])
        DMA(P.sp, "c2", TRI[:], tri_d, writes=[b_TRI])
        DMA(P.sp, "c3", IC0[:], ic0_d, writes=[b_IC0])
        DMA(P.sp, "c4", INVW[:], invw_d, writes=[b_INVW])
        P.op(P.dve, lambda e: e.memset(onesb[:], 1.0), writes=[b_onesb])
        P.op(P.dve, lambda e: e.memset(onesf[:], 1.0), writes=[b_onesf])
        P.op(P.dve, lambda e: e.memset(NH4[:], -0.5), writes=[b_NH4])
        DMA(P.sp, "c5", identf[:], identf_d, writes=[b_identf])
        P.op(P.dve, lambda e: e.memset(NH24[:], -0.5), writes=[b_NH24])
        P.op(P.dve, lambda e: e.memset(HC[:].rearrange("p l c t -> p (l c t)"), 0.0), writes=[b for bl in b_HC for b in bl])
        P.op(P.dve, lambda e: e.memset(ZC[:].rearrange("p l c t -> p (l c t)"), 0.0), writes=[b for bl in b_ZC for b in bl])
        P.op(P.dve, lambda e: e.memset(ZT[:].rearrange("p l j a b -> p (l j a b)"), 0.0), writes=b_ZT)
        ACT(CF2[:], CF[:], AF.Tanh, [b_CF], [b_CF2], scale=0.5)
        STT(CF2[:], CF2[:], 1.0, CF[:], ALU.add, ALU.mult, [b_CF2, b_CF], [b_CF2])
        TS(SCb[:], CF2[:], 0.5, None, ALU.mult, None, [b_CF2], [b_SCb])
        stage('s_consts')
        MS = Y32[:].rearrange("p c t -> p (c t)").bitcast(BF16).rearrange("p (s j k n) -> p s j k n", s=2, j=4, k=8)
        b_MS = P.bufs(2, "MS")
        for b in b_MS:
            b.al = tuple(b_Y32)
        for li in range(L):
            for g in range(12):
                s = (li * 12 + g) % 2
                DMA(P.pool, "ms%d" % s, MS[:, s], mod_d[li, g], writes=[b_MS[s]])
                for jj in range(4):
                    j = g * 4 + jj
                    for kc in range(8):
                        MM(PST[:, 0, li * 48 + j:li * 48 + j + 1], MS[:, s, jj, kc, :], SCb[:, kc:kc + 1], kc == 0, kc == 7,
                           [b_MS[s], b_SCb], [b_PST[0]], signal=(kc == 7))
        for li in range(L):
            TT(P.dve, MOD[:, li, :], PST[:, 0, li * 48:(li + 1) * 48], PV[:, li, PV_MB:PV_MB + 48], ALU.add, [b_PST[0], b_PV], [b_MOD])
        stage('s_mod')
        DMA(P.pool, "ci", identh[:], ident_d, writes=[b_identh])
        DMA(P.pool, "cp", PWb[:], pwbd_d.rearrange("l p c n -> p l c n"), writes=[b_PWb])
        for li in range(L):
            DMA(P.pool, "cv_in%d" % li, win_b[li], win_d[li], writes=[b_win[li]])
            DMA(P.pool, "cv_out%d" % li, wout_b[li], wout_d[li], writes=[b_wout[li]])
            DMA(P.pool, "cv_up%d" % li, up_b[li], up_d[li], writes=[b_up[li]])
            DMA(P.pool, "cv_dn%d" % li, dn_b[li], dn_d[li], writes=[b_dn[li]])
        stage('s_casts')
        for li in range(L):
            for c in range(3):
                TT(P.dve, DG[:, c], identh[:].unsqueeze(1).broadcast_to([128, 31, 128]),
                   PV[:, li, PV_CW + 31 * c:PV_CW + 31 * c + 31].unsqueeze(2).broadcast_to([128, 31, 128]), ALU.mult,
                   [b_identh, b_PV], [b_DG[c]])
                DMA(P.sp, "dgw", dg_b[li, c], DG[:, c], reads=[b_DG[c]], writes=[b_dgb[li]])
        for li in range(L):
            STT(DV[:, li, 0:8], MOD[:, li, 8:16], 1.0, PV[:, li, PV_PRE:PV_PRE + 8], ALU.add, ALU.mult, [b_MOD, b_PV], [b_DV])
            TT(P.dve, DV[:, li, 8:16], MOD[:, li, 16:24], PV[:, li, PV_POST:PV_POST + 8], ALU.mult, [b_MOD, b_PV], [b_DV])
            STT(DV[:, li, 16:24], MOD[:, li, 32:40], 1.0, PV[:, li, PV_FPRE:PV_FPRE + 8], ALU.add, ALU.mult, [b_MOD, b_PV], [b_DV])
            TT(P.dve, DV[:, li, 24:32], MOD[:, li, 40:48], PV[:, li, PV_FPOST:PV_FPOST + 8], ALU.mult, [b_MOD, b_PV], [b_DV])
            TS(DV[:, li, 32:38], PV[:, li, PV_CNG:PV_CNG + 6], 0.5, None, ALU.mult, None, [b_PV], [b_DV])
        stage('s_derived')
        WSF = TMP[:].rearrange("p a t -> p (a t)")[:, 0:768].rearrange("p (h t) -> p h t", h=6)
        b_WSF = P.buf("WSF")
        b_WSF.al = tuple(it[1] for it in r_TMP.items)
        SBB = SS[:].rearrange("p a t -> p (a t)")[:, 0:384].rearrange("p (c t) -> p c t", c=3)
        b_SBB = P.buf("SBB")
        b_SBB.al = tuple(it[1] for it in r_SS.items)
        for li in range(L):
            DMA(P.sp, "wsf", WSF, wst_d[li], writes=[b_WSF])
            for c in range(3):
                DMA(P.sp, "sbb", SBB[0:64, c, :], sgub_d[li, 2 * c:2 * c + 1, :].broadcast_to([64, 128]), writes=[b_SBB])
                DMA(P.sp, "sbb", SBB[64:128, c, :], sgub_d[li, 2 * c + 1:2 * c + 2, :].broadcast_to([64, 128]), writes=[b_SBB])
            TT(P.dve, WSF, WSF, TRI[:].unsqueeze(1).broadcast_to([128, 6, 128]), ALU.mult, [b_WSF, b_TRI], [b_WSF])
            CP(P.dve, WST[:, li], WSF, [b_WSF], [b_WST])
            for h in range(6):
                MM(PSG[:, h // 4, (h % 4) * 128:(h % 4 + 1) * 128], onesf[:], WSF[:, h, :], True, True, [b_onesf, b_WSF], [b_PSG[h // 4]])
            for c in range(3):
                for hh in range(2):
                    h = 2 * c + hh
                    sl = slice(64 * hh, 64 * hh + 64)
                    STT(CC[sl, li, c, :], PSG[sl, h // 4, (h % 4) * 128:(h % 4 + 1) * 128], PV[sl, li, PV_SB + c:PV_SB + c + 1], SBB[sl, c, :],
                        ALU.mult, ALU.add, [b_PSG[h // 4], b_PV, b_SBB], [b_CC])
        stage('s_sgu')
        P.barrier()

        stream = []
        for ti in range(ntiles):
            for li in range(L):
                for j in range(14):
                    stream.append(("A", win_b[li, j], b_win[li]))
                for m in range(8):
                    stream.append(("A", wout_b[li, m], b_wout[li]))
                for j in range(NJ):
                    stream.append(("U", up_b[li, j], b_up[li]))
                for m in range(8):
                    stream.append(("D", dn_b[li, m], b_dn[li]))
        rings = {"A": (WA, b_WA, 4), "U": (WU, b_WU, 3), "D": (WD, b_WD, 2)}
        issued = {"A": 0, "U": 0, "D": 0}
        consumed = {"A": 0, "U": 0, "D": 0}
        nxt = [0]

        def pump():
            while nxt[0] < len(stream):
                r, src, sb = stream[nxt[0]]
                ten, bl, ns = rings[r]
                if issued[r] - consumed[r] >= ns:
                    break
                s = issued[r] % ns
                DMA(P.sp, "w%s%d" % (r, s), ten[:, s], src, reads=[sb], writes=[bl[s]])
                issued[r] += 1
                nxt[0] += 1

        def wslot(r):
            ten, bl, ns = rings[r]
            assert consumed[r] < issued[r], "weight chunk not issued"
            s = consumed[r] % ns
            return ten[:, s], bl[s]

        def wdone(r):
            consumed[r] += 1
            pump()

        sc_i = [0]

        def tm_sum(chunks, fp32=False):
            col = 4 * (sc_i[0] % 16)
            sc_i[0] += 1
            oc = onesf[:, 0:1] if fp32 else onesb[:, 0:1]
            bo = b_onesf if fp32 else b_onesb
            n = len(chunks)
            for tb in range(4):
                for i, (ap, b) in enumerate(chunks):
                    MM(PST[:, 0, col + tb:col + tb + 1], ap[:, tb * 128:(tb + 1) * 128], oc, i == 0, i == n - 1, [b, bo], [b_PST[0]],
                       signal=(tb == 3 and i == n - 1))
            return col

        def small(col, scale, bias):
            a, ba = r_S4.get()
            ACT(a, PST[:, 0, col:col + 4], AF.Identity, [b_PST[0]], [ba], scale=scale, bias=bias)
            return a, ba

        def rpow(a, ba):
            r, br = r_S4.get()
            TT(P.pool, r, a, NH4[:], ALU.pow, [ba, b_NH4], [br])
            return r, br

        def bcast(a, ba):
            d4, bd4 = r_D4.get()
            TT(P.dve, d4, identf[:].unsqueeze(1).broadcast_to([128, 4, 128]), a.unsqueeze(2).broadcast_to([128, 4, 128]), ALU.mult,
               [b_identf, ba], [bd4])
            pb = gbank()
            MM(PSG[:, pb, :], onesf[:], d4.rearrange("p a b -> p (a b)"), True, True, [b_onesf, bd4], [b_PSG[pb]])
            return PSG[:, pb, :], b_PSG[pb]

        def rstd_of(chunks, scale, eps):
            n = len(chunks)
            for i, (ap, b) in enumerate(chunks):
                MM(PST[:, 1, :], onesb[:], ap, i == 0, i == n - 1, [b_onesb, b], [b_PST[1]], signal=(i == n - 1))
            a, ba = r_SS.get()
            ACT(a, PST[:, 1, :], AF.Sqrt, [b_PST[1]], [ba], scale=scale, bias=eps)
            r, br = r_SS.get()
            P.op(P.dve, lambda e: e.reciprocal(out=r, in_=a), [ba], [br])
            return r, br

        def prenorm(li, gcol, shcol, have_sq):
            SQ8 = YT
            if not have_sq:
                for c in range(8):
                    ACT(SQ8[:, c, :], XT[:, c, :], AF.Square, [b_XT[c]], [b_YT[c]])
            r, br = rstd_of([(SQ8[:, c, :], b_YT[c]) for c in range(8)], 1.0 / D_MODEL, EPS)
            for c in range(8):
                t, bt = r_TMP.get()
                TT(P.dve, t, XT[:, c, :], r, ALU.mult, [b_XT[c], br], [bt])
                ACT(HT[:, c, :], t, AF.Identity, [bt, b_DV, b_MOD], [b_HT[c]],
                    scale=DV[:, li, gcol + c:gcol + c + 1], bias=MOD[:, li, shcol + c:shcol + c + 1])

        def postnorm_residual(li, ggcol, eps, produce, sq_next):
            SQ8 = HT
            for m in range(8):
                pb = produce(m)
                CP(P.dve, Y32[:, m, :], PSG[:, pb, :], [b_PSG[pb]], [b_Y32[m]])
                ACT(SQ8[:, m, :], Y32[:, m, :], AF.Square, [b_Y32[m]], [b_HT[m]])
            r, br = rstd_of([(SQ8[:, m, :], b_HT[m]) for m in range(8)], 1.0 / D_MODEL, eps)
            for m in range(8):
                t, bt = r_TMP.get()
                STT(t, Y32[:, m, :], DV[:, li, ggcol + m:ggcol + m + 1], r, ALU.mult, ALU.mult, [b_Y32[m], b_DV, br], [bt])
                TT(P.dve, XT[:, m, :], XT[:, m, :], t, ALU.add, [b_XT[m], bt], [b_XT[m]])
                if sq_next:
                    ACT(YT[:, m, :], XT[:, m, :], AF.Square, [b_XT[m]], [b_YT[m]])

        def mixer(ti, li, have_sq):
            prenorm(li, 0, 0, have_sq)
            dump("h1", HT[:], b_HT, [128, 8, T])
            hcl = HC[:, li]
            zcl = ZC[:, li]
            stage('m_prenorm')
            if ti > 0:
                ACT(hcl[:, :, 0:30], hcl[:, :, T:T + 30], AF.Copy, b_HC[li], b_HC[li])
                ACT(zcl[:, :, 0:15], zcl[:, :, T:T + 15], AF.Copy, b_ZC[li], b_ZC[li])
            dgs = []
            for c in range(3):
                DMA(P.sp, "dg%d" % c, DG[:, c], dg_b[li, c], reads=[b_dgb[li]], writes=[b_DG[c]])
                dgs.append((DG[:, c], b_DG[c]))
            def inproj_fm():
                w, bw = wslot("A")
                pb = gbank()
                for kc in range(8):
                    MM(PSG[:, pb, :], w[:, kc, :], HT[:, kc, :], kc == 0, kc == 7, [bw, b_HT[kc]], [b_PSG[pb]], signal=(kc == 7))
                wdone("A")
                return pb

            csq = []

            def conv_chunk(c):
                dg, bdg = dgs[c]
                pb = gbank()
                for k in range(31):
                    MM(PSG[:, pb, :], dg[:, k, :], hcl[:, c, k:k + T], k == 0, k == 30, [bdg, b_HC[li][c]], [b_PSG[pb]], signal=(k == 30))
                ACT(Y32[:, 3 + c, :], PSG[:, pb, :], AF.Identity, [b_PSG[pb], b_PV], [b_Y32[3 + c]], bias=PV[:, li, PV_CB + c:PV_CB + c + 1])
                q, bq = r_SQ.get()
                ACT(q, Y32[:, 3 + c, :], AF.Square, [b_Y32[3 + c]], [bq])
                csq.append((q, bq))

            def v_path_and_pool():
                def pool_sums():
                    TT(P.dve, PA[:, :, 1:527], zcl[:, :, 1:527], zcl[:, :, 0:526], ALU.add, b_ZC[li], [b_PA])
                    TT(P.dve, PB[64:128, 0, 3:527], PA[64:128, 0, 3:527], PA[64:128, 0, 1:525], ALU.add, [b_PA], [b_PB])
                    TT(P.dve, PB[:, 1, 3:527], PA[:, 1, 3:527], PA[:, 1, 1:525], ALU.add, [b_PA], [b_PB])
                    TT(P.dve, PA[:, 1, 7:527], PB[:, 1, 7:527], PB[:, 1, 3:523], ALU.add, [b_PB, b_PA], [b_PA])
                    TT(P.dve, PB[64:128, 1, 15:527], PA[64:128, 1, 15:527], PA[64:128, 1, 7:519], ALU.add, [b_PA, b_PB], [b_PB])
                st = VST[:]
                for tb in range(4):
                    v, bv = vvs[tb]
                    v3 = v.rearrange("p (h d) -> p h d", h=6)
                    P.op(P.dve, lambda e, v3=v3, tb=tb: e.tensor_reduce(out=st[:, 0, tb * 6:tb * 6 + 6], in_=v3, axis=AX.X, op=ALU.add), [bv], [b_VST])
                    ACT(VSQ[:], v, AF.Square, [bv], [b_VSQ])
                    P.op(P.dve, lambda e, tb=tb: e.tensor_reduce(out=st[:, 1, tb * 6:tb * 6 + 6], in_=VSQ[:].rearrange("p (h d) -> p h d", h=6), axis=AX.X,
                                                                  op=ALU.add), [b_VSQ], [b_VST])
                TS(st[:, 2, :], st[:, 0, :], 1.0 / 64, None, ALU.mult, None, [b_VST], [b_VST])
                TT(P.dve, st[:, 3, :], st[:, 2, :], st[:, 2, :], ALU.mult, [b_VST], [b_VST])
                STT(st[:, 3, :], st[:, 1, :], 1.0 / 64, st[:, 3, :], ALU.mult, ALU.subtract, [b_VST], [b_VST])
                TS(st[:, 3, :], st[:, 3, :], EPS, None, ALU.add, None, [b_VST], [b_VST])
                TT(P.pool, st[:, 4, :], st[:, 3, :], NH24[:], ALU.pow, [b_VST, b_NH24], [b_VST2])
                pool_sums()
                TT(P.dve, st[:, 0, :], st[:, 2, :], st[:, 4, :], ALU.mult, [b_VST, b_VST2], [b_VST3])
                TS(st[:, 0, :], st[:, 0, :], -1.0, None, ALU.mult, None, [b_VST3], [b_VST3])
                for tb in range(4):
                    v, bv = vvs[tb]
                    for h in range(6):
                        ACT(VN[:, tb, h * 64:(h + 1) * 64], v[:, h * 64:(h + 1) * 64], AF.Identity, [bv, b_VST2, b_VST3], [b_VN[tb]],
                            scale=st[:, 4, tb * 6 + h:tb * 6 + h + 1], bias=st[:, 0, tb * 6 + h:tb * 6 + h + 1])
                stage('m_v')
                srcs = [(PA, 0, 0), (PB, 0, 1), (PA, 1, 0), (PB, 1, 1)]
                for g in range(4):
                    ten, c, hh = srcs[g]
                    sl = slice(64 * hh, 64 * hh + 64)
                    STT(DD[sl, c, :], ten[sl, c, 15:527], INVW[sl, c:c + 1], zcl[sl, c, 15:527], ALU.mult, ALU.subtract,
                        [b_PA, b_PB, b_INVW, b_ZC[li][c]], [b_DD[c]])
                    if ti == 0:
                        t, bt = r_TMP.get()
                        TT(P.dve, t[sl, 0:16], ten[sl, c, 15:31], IC0[sl, c, :], ALU.mult, [b_PA, b_PB, b_IC0], [bt])
                        TT(P.dve, DD[sl, c, 0:16], t[sl, 0:16], zcl[sl, c, 15:31], ALU.subtract, [bt, b_ZC[li][c]], [b_DD[c]])

            def pool_mm():
                for c in range(2):
                    pb = gbank()
                    MM(PSG[:, pb, :], PWb[:, li, c, :], DD[:, c, :], True, True, [b_PWb, b_DD[c]], [b_PSG[pb]])
                    ACT(Y32[:, 6 + c, :], PSG[:, pb, :], AF.Identity, [b_PSG[pb], b_PV], [b_Y32[6 + c]], scale=PV[:, li, PV_PS + c:PV_PS + c + 1])
                dump("yc", Y32[:, 6:8, :], b_Y32[6:8], [128, 2, T])
                stage('m_pool')

            def sgu_mix():
                for c in range(3):
                    pb = gbank()
                    for tb in range(4):
                        for hh in range(2):
                            h = 2 * c + hh
                            MM(PSG[64 * hh:64 * hh + 64, pb, tb * 128:(tb + 1) * 128], VN[:, tb, h * 64:(h + 1) * 64], WST[:, li, h, :], True, True,
                               [b_VN[tb], b_WST], [b_PSG[pb]], signal=(tb == 3 and hh == 1))
                    t, bt = r_TMP.get()
                    STT(t.rearrange("p (b t) -> p b t", b=4), PSG[:, pb, :].rearrange("p (b t) -> p b t", b=4), PV[:, li, PV_SG + c:PV_SG + c + 1],
                        CC[:, li, c, :].unsqueeze(1).broadcast_to([128, 4, 128]), ALU.mult, ALU.add, [b_PSG[pb], b_PV, b_CC], [bt])
                    TT(P.dve, Y32[:, c, :], UU[:, c, :], t, ALU.mult, [b_UU[c], bt], [b_Y32[c]])
                dump("ya", Y32[:, 0:3, :], b_Y32[0:3], [128, 3, T])
                stage('m_sgu')

            for c in range(3):
                pg = inproj_fm()
                th, bth = r_TH.get()
                ACT(th, PSG[:, pg, :], AF.Tanh, [b_PSG[pg]], [bth], scale=0.5)
                pa = inproj_fm()
                STT(hcl[:, c, 30:30 + T], th, 1.0, PSG[:, pa, :], ALU.add, ALU.mult, [bth, b_PSG[pa]], [b_HC[li][c]])
            stage('m_glu')
            vb = [gbank() for _ in range(4)]
            for jv in range(3):
                w, bw = wslot("A")
                for tb in range(4):
                    for kc in range(8):
                        MM(PSG[:, vb[tb], jv * 128:(jv + 1) * 128], HT[:, kc, tb * 128:(tb + 1) * 128], w[:, kc, :], kc == 0, kc == 7,
                           [bw, b_HT[kc]], [b_PSG[vb[tb]]], signal=(kc == 7))
                wdone("A")
            vvs = []
            for tb in range(4):
                v = VV4[:, tb, :]
                bv = b_VV4[tb]
                ACT(v, PSG[:, vb[tb], 0:384], AF.Gelu, [b_PSG[vb[tb]]], [bv])
                vvs.append((v, bv))
            for c in range(2):
                pb = inproj_fm()
                ACT(zcl[:, c, 15:15 + T], PSG[:, pb, :], AF.Copy, [b_PSG[pb]], [b_ZC[li][c]])
            stage('m_inproj')
            conv_chunk(0)
            v_path_and_pool()
            conv_chunk(1)
            conv_chunk(2)
            for c in range(3):
                pb = inproj_fm()
                ACT(UU[:, c, :], PSG[:, pb, :], AF.Gelu, [b_PSG[pb]], [b_UU[c]])
            stage('m_u')
            colm = tm_sum([(Y32[:, 3 + c, :], b_Y32[3 + c]) for c in range(3)], fp32=True)
            colq = tm_sum(csq)
            pool_mm()
            sgu_mix()
            mean4, bmean4 = small(colm, 1.0 / 384, 0.0)
            var4, bvar4 = small(colq, 1.0 / 384, EPS)
            m24, bm24 = r_S4.get()
            TT(P.dve, m24, mean4, mean4, ALU.mult, [bmean4], [bm24])
            TT(P.dve, var4, var4, m24, ALU.subtract, [bvar4, bm24], [bvar4])
            rs4, brs4 = rpow(var4, bvar4)
            mean, bmean = bcast(mean4, bmean4)
            rs, brs = bcast(rs4, brs4)

            def branch_norm(c0, c1, n, eps):
                sqs = []
                for c in range(c0, c1):
                    q, bq = r_SQ.get()
                    ACT(q, Y32[:, c, :], AF.Square, [b_Y32[c]], [bq])
                    sqs.append((q, bq))
                r, br = rstd_of(sqs, 1.0 / n, eps)
                for c in range(c0, c1):
                    STT(YT[:, c, :], Y32[:, c, :], PV[:, li, PV_BR + c:PV_BR + c + 1], r, ALU.mult, ALU.mult, [b_Y32[c], b_PV, br], [b_YT[c]])
            for c in range(3):
                t, bt = r_TMP.get()
                TT(P.dve, t, Y32[:, 3 + c, :], mean, ALU.subtract, [b_Y32[3 + c], bmean], [bt])
                TT(P.dve, t, t, rs, ALU.mult, [bt, brs], [bt])
                th, bth = r_TH.get()
                ACT(th, t, AF.Tanh, [bt, b_DV], [bth], scale=DV[:, li, 32 + c:33 + c], bias=DV[:, li, 35 + c:36 + c])
                l_, bl_ = r_SS.get()
                ACT(l_, t, AF.Identity, [bt, b_PV], [bl_], scale=PV[:, li, PV_CNG + c:PV_CNG + c + 1], bias=PV[:, li, PV_CNB + c:PV_CNB + c + 1])
                STT(Y32[:, 3 + c, :], th, 1.0, l_, ALU.add, ALU.mult, [bth, bl_], [b_Y32[3 + c]])
            dump("yb", Y32[:, 3:6, :], b_Y32[3:6], [128, 3, T])
            stage('m_conv')
            branch_norm(3, 6, 384, 4 * EPS)
            branch_norm(6, 8, 256, EPS)
            branch_norm(0, 3, 384, EPS)
            dump("yT", YT[:], b_YT, [128, 8, T])
            stage('m_brnorm')

            def produce(m):
                w, bw = wslot("A")
                pb = gbank()
                for kc in range(8):
                    MM(PSG[:, pb, :], w[:, kc, :], YT[:, kc, :], kc == 0, kc == 7, [bw, b_YT[kc]], [b_PSG[pb]], signal=(kc == 7))
                wdone("A")
                return pb
            postnorm_residual(li, 8, EPS, produce, True)
            stage('m_out')
            dump("x1", XT[:], b_XT, [128, 8, T])

        def ffn(ti, li, last):
            prenorm(li, 16, 24, True)
            dump("h2", HT[:], b_HT, [128, 8, T])
            ztl = ZT[:, li]
            fw = lambda k, j: PV[:, li, PV_FW + 44 * k + j:PV_FW + 44 * k + j + 1]
            fwv = lambda k: PV[:, li, PV_FW + 44 * k:PV_FW + 44 * k + 44].rearrange("p (g j) -> p j g", g=2)
            TT(P.dve, CORR[:, :, :, 0], ztl[:, :, :, 1], fwv(1), ALU.mult, [b_ZT[li], b_PV], [b_CORR])
            TT(P.dve, CT[:, :, :, 0], ztl[:, :, :, 0], fwv(0), ALU.mult, [b_ZT[li], b_PV], [b_CT])
            TT(P.dve, CORR[:, :, :, 0], CORR[:, :, :, 0], CT[:, :, :, 0], ALU.add, [b_CORR, b_CT], [b_CORR])
            TT(P.dve, CORR[:, :, :, 1], ztl[:, :, :, 1], fwv(0), ALU.mult, [b_ZT[li], b_PV], [b_CORR])
            def fin_act(p):
                gl, bgl = r_GL.get()
                ACT(gl, p[1][:, 0, :], AF.Gelu, [p[2]], [bgl])
                return gl, bgl

            def fin_dve(p, glp):
                TT(P.dve, AT[:, p[0], :], glp[0], p[1][:, 1, :], ALU.mult, [glp[1], p[2]], [b_AT[p[0]]])
            pend = None
            for j in range(NJ):
                w, bw = wslot("U")
                p0 = gpair()
                for g in range(2):
                    for kc in range(8):
                        MM(PSG[:, p0 + g, :], w[:, kc, g * 128:(g + 1) * 128], HT[:, kc, :], kc == 0, kc == 7, [bw, b_HT[kc]], [b_PSG[p0 + g]],
                           signal=(kc == 7))
                wdone("U")
                acc, bacc = r_ACC.get()
                pbs = [b_PSG[p0], b_PSG[p0 + 1]]
                for g in range(2):
                    jj = j + 22 * g
                    ACT(acc[:, g, :], PSG[:, p0 + g, :], AF.Identity, [b_PSG[p0 + g], b_PV], [bacc], scale=fw(2, jj),
                        bias=PV[:, li, PV_FB + jj:PV_FB + jj + 1])
                if pend is not None:
                    glp = fin_act(pend)
                for g in range(2):
                    jj = j + 22 * g
                    STT(acc[:, g, 1:T], PSG[:, p0 + g, 0:T - 1], fw(1, jj), acc[:, g, 1:T], ALU.mult, ALU.add, [b_PSG[p0 + g], b_PV, bacc], [bacc])
                    STT(acc[:, g, 2:T], PSG[:, p0 + g, 0:T - 2], fw(0, jj), acc[:, g, 2:T], ALU.mult, ALU.add, [b_PSG[p0 + g], b_PV, bacc], [bacc])
                CP(P.dve, ztl[:, j, :, :], PSG[:, p0:p0 + 2, T - 2:T], pbs, [b_ZT[li]])
                TT(P.dve, acc[:, :, 0:2], acc[:, :, 0:2], CORR[:, j, :, :], ALU.add, [bacc, b_CORR], [bacc])
                if pend is not None:
                    fin_dve(pend, glp)
                pend = (j, acc, bacc)
            glp = fin_act(pend)
            fin_dve(pend, glp)
            stage('f_up')
            dump("aT", AT[:, 0:4, :], b_AT[0:4], [128, 4, T])

            def produce(m):
                w, bw = wslot("D")
                pb = gbank()
                for kc in range(NJ):
                    MM(PSG[:, pb, :], w[:, kc, :], AT[:, kc, :], kc == 0, kc == NJ - 1, [bw, b_AT[kc]], [b_PSG[pb]], signal=(kc == NJ - 1))
                wdone("D")
                return pb
            postnorm_residual(li, 24, EPS, produce, not last)
            dump("x2", XT[:], b_XT, [128, 8, T])

        pump()
        for ti in range(ntiles):
            DMA(P.sp, "xin", XT[:], x_fm[ti], writes=b_XT)
            for li in range(L):
                mixer(ti, li, li > 0)
                ffn(ti, li, li == L - 1)
            DMA(P.sp, "xout", out_fm[ti], XT[:], reads=b_XT)

    except _Stop:
        pass
    P.barrier()
    print('sbuf bytes remaining', nc.sbuf_bytes_remaining)
    P.emit()
    return nc, dbg_out


def _fm(v, n):
    return np.ascontiguousarray(np.asarray(v, np.float32).reshape(n, 128).T)


def prep_shared(inp, layers):
    g = lambda k: np.asarray(inp[k], np.float32)
    pv, modc, winc, woutc, upc, dnc, wst, sgub, pwbd = [], [], [], [], [], [], [], [], []
    for l in layers:
        cols = [_fm(g("mix_pre_g")[l], 8), _fm(g("mix_post_g")[l], 8), _fm(g("branch_g")[l], 8), _fm(g("ffn_pre_g")[l], 8),
                _fm(g("ffn_post_g")[l], 8), _fm(g("sgu_norm_g")[l], 3), _fm(g("sgu_norm_b")[l], 3), _fm(g("conv_b")[l], 3),
                _fm(g("conv_norm_g")[l], 3), _fm(g("conv_norm_b")[l], 3), _fm(g("pool_scale")[l], 2)]
        cw = g("conv_w")[l].reshape(31, 3, 128).transpose(2, 1, 0).reshape(128, 93)
        fw = g("ffn_conv_w")[l].reshape(3, 44, 128).transpose(2, 0, 1).reshape(128, 132)
        cols += [cw, fw, _fm(g("ffn_conv_b")[l], 44), _fm(g("mod_b")[l], 48)]
        p = np.concatenate(cols, axis=1)
        assert p.shape == (128, NPV)
        pv.append(p)
        modc.append(g("mod_w")[l].reshape(8, 128, 12, 4, 128).transpose(2, 1, 3, 0, 4))
        winc.append(g("w_in")[l].reshape(8, 128, 14, 128).transpose(2, 1, 0, 3)[IN_ORDER])
        woutc.append(g("w_out")[l].reshape(8, 128, 8, 128).transpose(2, 1, 0, 3))
        upc.append(g("ffn_up")[l].reshape(8, 128, 2, NJ, 128).transpose(3, 1, 0, 2, 4).reshape(NJ, 128, 8, 256))
        dnc.append(g("ffn_down")[l].reshape(NJ, 128, 8, 128).transpose(2, 1, 0, 3))
        wst.append(g("sgu_w")[l].transpose(2, 0, 1))
        sgub.append(g("sgu_b")[l])
        pw = g("pool_w")[l]
        bd = np.zeros((128, 2, 128), np.float32)
        for c in range(2):
            bd[0:64, c, 0:64] = pw[2 * c]
            bd[64:128, c, 64:128] = pw[2 * c + 1]
        pwbd.append(bd)
    ca = lambda lst: np.ascontiguousarray(np.stack(lst, 0), dtype=np.float32)
    s_idx = np.arange(128)
    tri = (s_idx[None, :] >= s_idx[:, None]).astype(np.float32)
    wins = np.array([2, 4, 8, 16], np.float32)
    ic0 = np.zeros((128, 2, 16), np.float32)
    invw = np.zeros((128, 2), np.float32)
    for c in range(2):
        for hh in range(2):
            w_ = wins[2 * c + hh]
            ic0[64 * hh:64 * hh + 64, c, :] = 1.0 / np.minimum(np.arange(1, 17, dtype=np.float32), w_)
            invw[64 * hh:64 * hh + 64, c] = 1.0 / w_
    return {"pv": ca(pv), "mod_c": ca(modc), "w_in_c": ca(winc), "w_out_c": ca(woutc), "up_c": ca(upc), "down_c": ca(dnc),
            "wst": ca(wst), "sgub": ca(sgub), "pwbd": ca(pwbd), "identh": (0.5 * np.eye(128)).astype(np.float32), "identf": np.eye(128, dtype=np.float32), "tri": tri,
            "ic0": ic0, "invw": invw}


def x_to_fm(xb, ntiles=NT):
    return np.ascontiguousarray(np.asarray(xb, np.float32).reshape(ntiles, T, 8, 128).transpose(0, 3, 2, 1))


def fm_to_x(o):
    nt = o.shape[0]
    return np.ascontiguousarray(o.transpose(0, 3, 2, 1).reshape(nt * T, D_MODEL))


_CACHE = {}


def _get_prog(layers):
    key = tuple(layers)
    if key not in _CACHE:
        _CACHE[key] = build(list(range(len(layers))))[0]
    return _CACHE[key]


def run_layers(x, inputs, layers):
    shared = prep_shared(inputs, layers)
    c = np.asarray(inputs["c"], np.float32)
    nc = _get_prog(layers)
    in_maps = []
    for b in range(BATCH):
        m = dict(shared)
        m["x_fm"] = x_to_fm(x[b])
        m["c_fm"] = _fm(c[b], 8)
        in_maps.append(m)
    res = run_bass_kernel_spmd(nc, in_maps, core_ids=list(range(BATCH)))
    return np.stack([fm_to_x(res.results[b]["out_fm"]) for b in range(BATCH)], 0)


FUSED = True


def kernel(**inputs):
    x = np.asarray(inputs["x"], np.float32)
    if FUSED:
        out = run_layers(x, inputs, list(range(DEPTH)))
    else:
        out = x
        for l in range(DEPTH):
            out = run_layers(out, inputs, [l])
    return out.astype(np.float32)
```

```python
import contextlib
import numpy as np
import concourse.bass as bass
import concourse.mybir as mybir
from concourse.bass_utils import run_bass_kernel_spmd

F32 = mybir.dt.float32
BF16 = mybir.dt.bfloat16
AF = mybir.ActivationFunctionType
ALU = mybir.AluOpType
AX = mybir.AxisListType

D_MODEL = 1024
BATCH = 8
SEQ = 4096
DEPTH = 2
T = 512
NT = SEQ // T
D_FF = 2816
NJ = D_FF // 128
EPS = 1e-6
NPV = 374
IN_ORDER = [9, 6, 10, 7, 11, 8, 3, 4, 5, 12, 13, 0, 1, 2]

PV_PRE, PV_POST, PV_BR, PV_FPRE, PV_FPOST = 0, 8, 16, 24, 32
PV_SG, PV_SB, PV_CB, PV_CNG, PV_CNB, PV_PS = 40, 43, 46, 49, 52, 55
PV_CW, PV_FW, PV_FB, PV_MB = 57, 150, 282, 326


class Buf:
    __slots__ = ("name", "w", "r", "al", "excl")

    def __init__(self, name):
        self.name = name
        self.w = None
        self.r = []
        self.al = ()
        self.excl = False


class Eng:
    def __init__(self, name, attr):
        self.name = name
        self.attr = attr
        self.ops = []
        self.count = 0
        self.waited = {}


class Prog:
    def __init__(self, nc):
        self.nc = nc
        self.es = contextlib.ExitStack()
        self.pe = Eng("pe", "tensor")
        self.act = Eng("act", "scalar")
        self.dve = Eng("dve", "vector")
        self.pool = Eng("pool", "gpsimd")
        self.sp = Eng("sp", "sync")
        self.engs = [self.pe, self.act, self.dve, self.pool, self.sp]
        self.sems = {}
        for e in self.engs:
            self.sems[e.name] = self.es.enter_context(nc.semaphore("s_" + e.name))
        self.dma_counts = {}
        self.nbuf = 0

    def sbuf(self, name, shape, dt):
        return self.es.enter_context(self.nc.sbuf_tensor(name, list(shape), dt))

    def psum(self, name, shape, dt=F32):
        return self.es.enter_context(self.nc.psum_tensor(name, list(shape), dt))

    def buf(self, name=None):
        self.nbuf += 1
        return Buf(name or "b%d" % self.nbuf)

    def bufs(self, n, name=None):
        return [self.buf(None if name is None else "%s%d" % (name, i)) for i in range(n)]

    def dma_sem(self, name):
        key = "d_" + name
        self.sems[key] = self.es.enter_context(self.nc.semaphore(key))
        self.dma_counts[key] = 0
        return key

    def _collect(self, eng, reads, writes):
        waits = {}

        def need(tok, skip_same):
            if tok is None:
                return
            k, v = tok
            if skip_same and k == eng.name:
                return
            if eng.waited.get(k, 0) >= v:
                return
            if waits.get(k, 0) < v:
                waits[k] = v

        pe_same = eng.name == "pe"
        for b0 in reads:
            for b in (b0,) + tuple(b0.al):
                need(b.w, False)
                if b.excl:
                    for t in b.r:
                        need(t, True)
        for b0 in writes:
            for b in (b0,) + tuple(b0.al):
                need(b.w, pe_same)
                for t in b.r:
                    need(t, pe_same)
        for k, v in waits.items():
            eng.waited[k] = v
        return sorted(waits.items())

    def op(self, eng, fn, reads=(), writes=(), signal=True):
        waits = self._collect(eng, reads, writes)
        tok = (eng.name, eng.count + 1)
        for b in reads:
            b.r.append(tok)
        for b in writes:
            b.w = tok
            b.r = []
        inc = None
        if signal:
            eng.count += 1
            inc = (eng.name, 1)
        eng.ops.append((waits, fn, inc))

    def dma(self, eng, semkey, out, in_, reads=(), writes=(), **kw):
        waits = self._collect(eng, reads, writes)
        self.dma_counts[semkey] += 16
        tok = (semkey, self.dma_counts[semkey])
        for b in reads:
            b.r.append(tok)
        for b in writes:
            b.w = tok
            b.r = []

        def fn(e, out=out, in_=in_, kw=kw):
            return e.dma_start(out=out, in_=in_, **kw)
        eng.ops.append((waits, fn, (semkey, 16)))
        return tok

    def wait_tok(self, eng, tok):
        k, v = tok
        if eng.waited.get(k, 0) >= v:
            return
        eng.waited[k] = v
        eng.ops.append(([(k, v)], None, None))

    def barrier(self):
        toks = [(e.name, e.count) for e in self.engs if e.count > 0]
        toks += [(k, v) for k, v in self.dma_counts.items() if v > 0]
        for e in self.engs:
            for t in toks:
                if t[0] != e.name:
                    self.wait_tok(e, t)

    def emit(self):
        nc = self.nc
        with nc.Block() as block:
            for e in self.engs:
                if not e.ops:
                    continue

                def body(h, e=e):
                    for waits, fn, inc in e.ops:
                        for k, v in waits:
                            h.wait_ge(self.sems[k], v)
                        if fn is not None:
                            ins = fn(h)
                            if inc is not None:
                                ins.then_inc(self.sems[inc[0]], inc[1])
                getattr(block, e.attr)(body)
        self.es.close()


class Rot:
    def __init__(self, items):
        self.items = items
        self.i = 0

    def get(self):
        it = self.items[self.i % len(self.items)]
        self.i += 1
        return it


class _Stop(Exception):
    pass


def build(layers, ntiles=NT, dbg=(), stop_after=None):
    nc = bass.Bass("TRN2", target_bir_lowering=False)
    P = Prog(nc)
    L = len(layers)
    dbg = set(dbg)
    dbg_out = {}

    def din(name, shape, dt=F32):
        return nc.dram_tensor(name, list(shape), dt, kind="ExternalInput").ap()

    x_fm = din("x_fm", [ntiles, 128, 8, T])
    out_fm = nc.dram_tensor("out_fm", [ntiles, 128, 8, T], F32, kind="ExternalOutput").ap()
    c_fm = din("c_fm", [128, 8])
    pv_d = din("pv", [L, 128, NPV])
    mod_d = din("mod_c", [L, 12, 128, 4, 8, 128])
    win_d = din("w_in_c", [L, 14, 128, 8, 128])
    wout_d = din("w_out_c", [L, 8, 128, 8, 128])
    up_d = din("up_c", [L, NJ, 128, 8, 256])
    dn_d = din("down_c", [L, 8, 128, NJ, 128])
    wst_d = din("wst", [L, 128, 6, 128])
    sgub_d = din("sgub", [L, 6, 128])
    pwbd_d = din("pwbd", [L, 128, 2, 128])
    ident_d = din("identh", [128, 128])
    identf_d = din("identf", [128, 128])
    tri_d = din("tri", [128, 128])
    ic0_d = din("ic0", [128, 2, 16])
    invw_d = din("invw", [128, 2])

    def dscr(name, shape):
        return nc.dram_tensor(name, list(shape), BF16, kind="Internal").ap()
    win_b = dscr("w_in_b", [L, 14, 128, 8, 128])
    wout_b = dscr("w_out_b", [L, 8, 128, 8, 128])
    up_b = dscr("up_b", [L, NJ, 128, 8, 256])
    dn_b = dscr("down_b", [L, 8, 128, NJ, 128])
    b_win = [P.bufs(14) for _ in range(L)]; b_wout = [P.bufs(8) for _ in range(L)]
    b_up = [P.bufs(NJ) for _ in range(L)]; b_dn = [P.bufs(8) for _ in range(L)]
    dg_b = dscr("dg_b", [L, 3, 128, 31, 128]); b_dgb = [P.bufs(3, "dgb%d_" % l) for l in range(L)]

    PV = P.sbuf("PV", [128, L, NPV], F32); b_PV = P.buf("PV")
    MOD = P.sbuf("MOD", [128, L, 48], F32); b_MOD = P.buf("MOD")
    DV = P.sbuf("DV", [128, L, 40], F32); b_DV = P.buf("DV")
    CF = P.sbuf("CF", [128, 8], F32); b_CF = P.buf("CF")
    CF2 = P.sbuf("CF2", [128, 8], F32); b_CF2 = P.buf("CF2")
    SCb = P.sbuf("SCb", [128, 8], BF16); b_SCb = P.buf("SCb")
    WST = P.sbuf("WST", [128, L, 6, 128], BF16); b_WSTl = P.bufs(L, "WST")
    CC = P.sbuf("CC", [128, L, 3, 128], F32); b_CCl = P.bufs(L, "CC")
    PWb = P.sbuf("PWb", [128, L, 2, 128], BF16); b_PWb = P.buf("PWb")
    onesb = P.sbuf("onesb", [128, 128], BF16); b_onesb = P.buf("onesb")
    onesf = P.sbuf("onesf", [128, 128], F32); b_onesf = P.buf("onesf")
    identh = P.sbuf("identh_s", [128, 128], BF16); b_identh = P.buf("identh")
    identf = P.sbuf("identf_s", [128, 128], F32); b_identf = P.buf("identf")
    NH4 = P.sbuf("NH4", [128, 4], F32); b_NH4 = P.buf("NH4")
    S4 = P.sbuf("S4", [128, 8, 4], F32); r_S4 = Rot([(S4[:, i, :], P.buf("S4_%d" % i)) for i in range(8)])
    D4 = P.sbuf("D4", [128, 1, 4, 128], F32); r_D4 = Rot([(D4[:, i], P.buf("D4_%d" % i)) for i in range(1)])
    IC0 = P.sbuf("IC0", [128, 2, 16], F32); b_IC0 = P.buf("IC0")
    INVW = P.sbuf("INVW", [128, 2], F32); b_INVW = P.buf("INVW")
    TRI = P.sbuf("TRI", [128, 128], F32); b_TRI = P.buf("TRI")
    HC = P.sbuf("HC", [128, L, 3, 30 + T], BF16); b_HC = [P.bufs(3) for _ in range(L)]
    ZC = P.sbuf("ZC", [128, L, 2, 15 + T], F32); b_ZC = [P.bufs(2) for _ in range(L)]
    ZT = P.sbuf("ZT", [128, L, NJ, 2, 2], F32); b_ZT = P.bufs(L)
    CORR = P.sbuf("CORR", [128, NJ, 2, 2], F32); b_CORR = P.buf("CORR")
    CT = P.sbuf("CT", [128, NJ, 2, 2], F32); b_CT = P.buf("CT")
    XT = P.sbuf("XT", [128, 8, T], F32); b_XT = P.bufs(8, "XT")
    HT = P.sbuf("HT", [128, 8, T], BF16); b_HT = P.bufs(8, "HT")
    Y32 = P.sbuf("Y32", [128, 8, T], F32); b_Y32 = P.bufs(8, "Y32")
    YT = P.sbuf("YT", [128, 8, T], BF16); b_YT = P.bufs(8, "YT")
    SQ = P.sbuf("SQ", [128, 3, T], BF16); r_SQ = Rot([(SQ[:, i, :], P.buf("SQ%d" % i)) for i in range(3)])
    TMP = P.sbuf("TMP", [128, 3, T], F32); r_TMP = Rot([(TMP[:, i, :], P.buf("TMP%d" % i)) for i in range(3)])
    SS = P.sbuf("SS", [128, 3, T], F32); r_SS = Rot([(SS[:, i, :], P.buf("SS%d" % i)) for i in range(3)])
    AR = P.sbuf("AR", [128, NJ * T], BF16)
    AT = AR[:].rearrange("p (j t) -> p j t", j=NJ); b_AT = P.bufs(NJ, "AT")
    ARf = AR[:].bitcast(F32)
    UU = ARf[:, 0:1536].rearrange("p (c t) -> p c t", c=3); b_UU = P.bufs(3, "UU")
    TH = ARf[:, 1536:2560].rearrange("p (c t) -> p c t", c=2); r_TH = Rot([(TH[:, i, :], P.buf("TH%d" % i)) for i in range(2)])
    PA = ARf[:, 2560:2560 + 2 * 527].rearrange("p (c t) -> p c t", c=2); b_PA = P.buf("PA")
    PB = ARf[:, 3616:3616 + 2 * 527].rearrange("p (c t) -> p c t", c=2); b_PB = P.buf("PB")
    DD = AR[:, 2 * 4672:2 * 4672 + 2 * T].rearrange("p (c t) -> p c t", c=2); b_DD = P.bufs(2, "DD")
    mix_scr = b_UU + [it[1] for it in r_TH.items] + [b_PA, b_PB] + b_DD
    for b in mix_scr:
        b.al = tuple(b_AT)
    for b in b_AT:
        b.al = tuple(mix_scr)
    VV4 = P.sbuf("VV4", [128, 4, 384], F32); b_VV4 = P.bufs(4, "VV4")
    VSQ = P.sbuf("VSQ", [128, 384], F32); b_VSQ = P.buf("VSQ")
    VST = P.sbuf("VST", [128, 5, 24], F32); b_VST = P.buf("VST"); b_VST2 = P.buf("VST2"); b_VST3 = P.buf("VST3")
    NH24 = P.sbuf("NH24", [128, 24], F32); b_NH24 = P.buf("NH24")
    VN = P.sbuf("VN", [128, 4, 384], BF16); b_VN = P.bufs(4, "VN")
    ACC = P.sbuf("ACC", [128, 5, 2, T], F32); r_ACC = Rot([(ACC[:, i], P.buf("ACC%d" % i)) for i in range(5)])
    GL = P.sbuf("GL", [128, 3, T], F32); r_GL = Rot([(GL[:, i, :], P.buf("GL%d" % i)) for i in range(3)])
    DG = P.sbuf("DG", [128, 3, 31, 128], BF16); b_DG = P.bufs(3, "DG")
    WA = P.sbuf("WA", [128, 4, 8, 128], BF16); b_WA = P.bufs(4, "WA")
    WU = P.sbuf("WU", [128, 3, 8, 256], BF16); b_WU = P.bufs(3, "WU")
    WD = P.sbuf("WD", [128, 2, NJ, 128], BF16); b_WD = P.bufs(2, "WD")
    PSG = P.psum("PSG", [128, 6, T]); b_PSG = P.bufs(6, "PSG")
    PST = P.psum("PST", [128, 2, T]); b_PST = P.bufs(2, "PST")
    for b in b_PSG + b_PST:
        b.excl = True
    gen_i = [0]

    def gbank():
        i = gen_i[0] % 6
        gen_i[0] += 1
        return i

    pair_i = [0]

    def gpair():
        i = pair_i[0] % 3
        pair_i[0] += 1
        return 2 * i
    st_i = [0]

    def sbank():
        i = st_i[0] % 2
        st_i[0] += 1
        return i

    def ACT(out, in_, func, reads, writes, scale=1.0, bias=0.0):
        P.op(P.act, lambda e: e.activation(out=out, in_=in_, func=func, scale=scale, bias=bias), reads, writes)

    def TT(eng, out, in0, in1, op, reads, writes):
        P.op(eng, lambda e: e.tensor_tensor(out=out, in0=in0, in1=in1, op=op), reads, writes)

    def STT(out, in0, scalar, in1, op0, op1, reads, writes):
        P.op(P.dve, lambda e: e.scalar_tensor_tensor(out=out, in0=in0, scalar=scalar, in1=in1, op0=op0, op1=op1), reads, writes)

    def TS(out, in0, s1, s2, op0, op1, reads, writes):
        if s2 is None:
            P.op(P.dve, lambda e: e.tensor_scalar(out=out, in0=in0, scalar1=s1, scalar2=None, op0=op0), reads, writes)
        else:
            P.op(P.dve, lambda e: e.tensor_scalar(out=out, in0=in0, scalar1=s1, scalar2=s2, op0=op0, op1=op1), reads, writes)

    def CP(eng, out, in_, reads, writes):
        P.op(eng, lambda e: e.tensor_copy(out=out, in_=in_), reads, writes)

    def MM(out, lhsT, rhs, start, stop, reads, writes, signal=True):
        P.op(P.pe, lambda e: e.matmul(out, lhsT=lhsT, rhs=rhs, start=start, stop=stop), reads, writes, signal=signal)

    dsem = {}

    def DMA(eng, sem, out, in_, reads=(), writes=()):
        if sem not in dsem:
            dsem[sem] = P.dma_sem(sem)
        return P.dma(eng, dsem[sem], out, in_, reads, writes)

    def stage(name):
        if stop_after == name:
            raise _Stop()

    def dump(name, ap, bufs, shape):
        if name in dbg and name not in dbg_out:
            t = nc.dram_tensor("dbg_" + name, list(shape), ap.dtype, kind="ExternalOutput").ap()
            dbg_out[name] = t
            DMA(P.sp, "dbg_" + name, t, ap, reads=bufs)

    try:
        DMA(P.sp, "c0", PV[:], pv_d.rearrange("l p n -> p l n"), writes=[b_PV])
        DMA(P.sp, "c1", CF[:], c_fm, writes=[b_CF])
        DMA(P.sp, "c2", TRI[:], tri_d, writes=[b_TRI])
        DMA(P.sp, "c3", IC0[:], ic0_d, writes=[b_IC0])
        DMA(P.sp, "c4", INVW[:], invw_d, writes=[b_INVW])
        P.op(P.dve, lambda e: e.memset(onesb[:], 1.0), writes=[b_onesb])
        P.op(P.dve, lambda e: e.memset(onesf[:], 1.0), writes=[b_onesf])
        P.op(P.dve, lambda e: e.memset(NH4[:], -0.5), writes=[b_NH4])
        DMA(P.sp, "c5", identf[:], identf_d, writes=[b_identf])
        P.op(P.dve, lambda e: e.memset(NH24[:], -0.5), writes=[b_NH24])
        P.op(P.dve, lambda e: e.memset(HC[:].rearrange("p l c t -> p (l c t)"), 0.0), writes=[b for bl in b_HC for b in bl])
        P.op(P.dve, lambda e: e.memset(ZC[:].rearrange("p l c t -> p (l c t)"), 0.0), writes=[b for bl in b_ZC for b in bl])
        P.op(P.dve, lambda e: e.memset(ZT[:].rearrange("p l j a b -> p (l j a b)"), 0.0), writes=b_ZT)
        ACT(CF2[:], CF[:], AF.Tanh, [b_CF], [b_CF2], scale=0.5)
        STT(CF2[:], CF2[:], 1.0, CF[:], ALU.add, ALU.mult, [b_CF2, b_CF], [b_CF2])
        TS(SCb[:], CF2[:], 0.5, None, ALU.mult, None, [b_CF2], [b_SCb])
        stage('s_consts')
        MS = Y32[:].rearrange("p c t -> p (c t)").bitcast(BF16).rearrange("p (s j k n) -> p s j k n", s=2, j=4, k=8)
        b_MS = P.bufs(2, "MS")
        for b in b_MS:
            b.al = tuple(b_Y32)
        for b in b_Y32:
            b.al = tuple(b_MS)
        for li in range(L):
            for g in range(12):
                s = (li * 12 + g) % 2
                DMA(P.pool, "ms%d" % s, MS[:, s], mod_d[li, g], writes=[b_MS[s]])
                for jj in range(4):
                    j = g * 4 + jj
                    for kc in range(8):
                        MM(PST[:, 0, li * 48 + j:li * 48 + j + 1], MS[:, s, jj, kc, :], SCb[:, kc:kc + 1], kc == 0, kc == 7,
                           [b_MS[s], b_SCb], [b_PST[0]], signal=(kc == 7))
        for li in range(L):
            TT(P.dve, MOD[:, li, :], PST[:, 0, li * 48:(li + 1) * 48], PV[:, li, PV_MB:PV_MB + 48], ALU.add, [b_PST[0], b_PV], [b_MOD])
        stage('s_mod')
        DMA(P.pool, "ci", identh[:], ident_d, writes=[b_identh])
        DMA(P.pool, "cp", PWb[:], pwbd_d.rearrange("l p c n -> p l c n"), writes=[b_PWb])
        stage('s_casts')
        for li in range(L):
            STT(DV[:, li, 0:8], MOD[:, li, 8:16], 1.0, PV[:, li, PV_PRE:PV_PRE + 8], ALU.add, ALU.mult, [b_MOD, b_PV], [b_DV])
            TT(P.dve, DV[:, li, 8:16], MOD[:, li, 16:24], PV[:, li, PV_POST:PV_POST + 8], ALU.mult, [b_MOD, b_PV], [b_DV])
            STT(DV[:, li, 16:24], MOD[:, li, 32:40], 1.0, PV[:, li, PV_FPRE:PV_FPRE + 8], ALU.add, ALU.mult, [b_MOD, b_PV], [b_DV])
            TT(P.dve, DV[:, li, 24:32], MOD[:, li, 40:48], PV[:, li, PV_FPOST:PV_FPOST + 8], ALU.mult, [b_MOD, b_PV], [b_DV])
            TS(DV[:, li, 32:38], PV[:, li, PV_CNG:PV_CNG + 6], 0.5, None, ALU.mult, None, [b_PV], [b_DV])
        stage('s_derived')
        WSF = TMP[:].rearrange("p a t -> p (a t)")[:, 0:768].rearrange("p (h t) -> p h t", h=6)
        b_WSF = P.buf("WSF")
        b_WSF.al = tuple(it[1] for it in r_TMP.items)
        for it in r_TMP.items:
            it[1].al = (b_WSF,)
        SBB = SS[:].rearrange("p a t -> p (a t)")[:, 0:384].rearrange("p (c t) -> p c t", c=3)
        b_SBB = P.buf("SBB")
        b_SBB.al = tuple(it[1] for it in r_SS.items)
        for it in r_SS.items:
            it[1].al = (b_SBB,)
        for li in range(L):
            DMA(P.sp, "wsf", WSF, wst_d[li], writes=[b_WSF])
            for c in range(3):
                DMA(P.sp, "sbb", SBB[0:64, c, :], sgub_d[li, 2 * c:2 * c + 1, :].broadcast_to([64, 128]), writes=[b_SBB])
                DMA(P.sp, "sbb", SBB[64:128, c, :], sgub_d[li, 2 * c + 1:2 * c + 2, :].broadcast_to([64, 128]), writes=[b_SBB])
            TT(P.dve, WSF, WSF, TRI[:].unsqueeze(1).broadcast_to([128, 6, 128]), ALU.mult, [b_WSF, b_TRI], [b_WSF])
            CP(P.dve, WST[:, li], WSF, [b_WSF], [b_WSTl[li]])
            for h in range(6):
                MM(PSG[:, h // 4, (h % 4) * 128:(h % 4 + 1) * 128], onesf[:], WSF[:, h, :], True, True, [b_onesf, b_WSF], [b_PSG[h // 4]])
            for c in range(3):
                for hh in range(2):
                    h = 2 * c + hh
                    sl = slice(64 * hh, 64 * hh + 64)
                    STT(CC[sl, li, c, :], PSG[sl, h // 4, (h % 4) * 128:(h % 4 + 1) * 128], PV[sl, li, PV_SB + c:PV_SB + c + 1], SBB[sl, c, :],
                        ALU.mult, ALU.add, [b_PSG[h // 4], b_PV, b_SBB], [b_CCl[li]])
        stage('s_sgu')

        stream = []
        for ti in range(ntiles):
            for li in range(L):
                for j in range(14):
                    stream.append(("A", win_d[li, j], win_b[li, j], b_win[li][j], ti == 0))
                for m in range(8):
                    stream.append(("A", wout_d[li, m], wout_b[li, m], b_wout[li][m], ti == 0))
                for j in range(NJ):
                    stream.append(("U", up_d[li, j], up_b[li, j], b_up[li][j], ti == 0))
                for m in range(8):
                    stream.append(("D", dn_d[li, m], dn_b[li, m], b_dn[li][m], ti == 0))
        rings = {"A": (WA, b_WA, 4), "U": (WU, b_WU, 3), "D": (WD, b_WD, 2)}
        issued = {"A": 0, "U": 0, "D": 0}
        consumed = {"A": 0, "U": 0, "D": 0}
        nxt = [0]

        def pump():
            while nxt[0] < len(stream):
                r, src32, scr, sb, first = stream[nxt[0]]
                ten, bl, ns = rings[r]
                if issued[r] - consumed[r] >= ns:
                    break
                s = issued[r] % ns
                if first:
                    DMA(P.pool, "wc%s%d" % (r, s), ten[:, s], src32, writes=[bl[s]])
                    if ntiles > 1:
                        DMA(P.sp, "ws%s%d" % (r, s), scr, ten[:, s], reads=[bl[s]], writes=[sb])
                else:
                    DMA(P.sp, "w%s%d" % (r, s), ten[:, s], scr, reads=[sb], writes=[bl[s]])
                issued[r] += 1
                nxt[0] += 1

        def wslot(r):
            ten, bl, ns = rings[r]
            assert consumed[r] < issued[r], "weight chunk not issued"
            s = consumed[r] % ns
            return ten[:, s], bl[s]

        def wdone(r):
            consumed[r] += 1
            pump()

        sc_i = [0]

        def tm_sum(chunks, fp32=False):
            col = 4 * (sc_i[0] % 16)
            sc_i[0] += 1
            oc = onesf[:, 0:1] if fp32 else onesb[:, 0:1]
            bo = b_onesf if fp32 else b_onesb
            n = len(chunks)
            for tb in range(4):
                for i, (ap, b) in enumerate(chunks):
                    MM(PST[:, 0, col + tb:col + tb + 1], ap[:, tb * 128:(tb + 1) * 128], oc, i == 0, i == n - 1, [b, bo], [b_PST[0]],
                       signal=(tb == 3 and i == n - 1))
            return col

        def small(col, scale, bias):
            a, ba = r_S4.get()
            ACT(a, PST[:, 0, col:col + 4], AF.Identity, [b_PST[0]], [ba], scale=scale, bias=bias)
            return a, ba

        def rpow(a, ba):
            r, br = r_S4.get()
            TT(P.pool, r, a, NH4[:], ALU.pow, [ba, b_NH4], [br])
            return r, br

        def bcast(a, ba):
            d4, bd4 = r_D4.get()
            TT(P.dve, d4, identf[:].unsqueeze(1).broadcast_to([128, 4, 128]), a.unsqueeze(2).broadcast_to([128, 4, 128]), ALU.mult,
               [b_identf, ba], [bd4])
            pb = gbank()
            MM(PSG[:, pb, :], onesf[:], d4.rearrange("p a b -> p (a b)"), True, True, [b_onesf, bd4], [b_PSG[pb]])
            return PSG[:, pb, :], b_PSG[pb]

        def rstd_of(chunks, scale, eps):
            n = len(chunks)
            for i, (ap, b) in enumerate(chunks):
                MM(PST[:, 1, :], onesb[:], ap, i == 0, i == n - 1, [b_onesb, b], [b_PST[1]], signal=(i == n - 1))
            a, ba = r_SS.get()
            ACT(a, PST[:, 1, :], AF.Sqrt, [b_PST[1]], [ba], scale=scale, bias=eps)
            r, br = r_SS.get()
            P.op(P.dve, lambda e: e.reciprocal(out=r, in_=a), [ba], [br])
            return r, br

        def prenorm(li, gcol, shcol, have_sq):
            SQ8 = YT
            if not have_sq:
                for c in range(8):
                    ACT(SQ8[:, c, :], XT[:, c, :], AF.Square, [b_XT[c]], [b_YT[c]])
            r, br = rstd_of([(SQ8[:, c, :], b_YT[c]) for c in range(8)], 1.0 / D_MODEL, EPS)
            for c in range(8):
                t, bt = r_TMP.get()
                TT(P.dve, t, XT[:, c, :], r, ALU.mult, [b_XT[c], br], [bt])
                ACT(HT[:, c, :], t, AF.Identity, [bt, b_DV, b_MOD], [b_HT[c]],
                    scale=DV[:, li, gcol + c:gcol + c + 1], bias=MOD[:, li, shcol + c:shcol + c + 1])

        def postnorm_residual(li, ggcol, eps, produce, sq_next):
            SQ8 = HT
            for m in range(8):
                pb = produce(m)
                CP(P.dve, Y32[:, m, :], PSG[:, pb, :], [b_PSG[pb]], [b_Y32[m]])
                ACT(SQ8[:, m, :], Y32[:, m, :], AF.Square, [b_Y32[m]], [b_HT[m]])
            r, br = rstd_of([(SQ8[:, m, :], b_HT[m]) for m in range(8)], 1.0 / D_MODEL, eps)
            for m in range(8):
                t, bt = r_TMP.get()
                STT(t, Y32[:, m, :], DV[:, li, ggcol + m:ggcol + m + 1], r, ALU.mult, ALU.mult, [b_Y32[m], b_DV, br], [bt])
                TT(P.dve, XT[:, m, :], XT[:, m, :], t, ALU.add, [b_XT[m], bt], [b_XT[m]])
                if sq_next:
                    ACT(YT[:, m, :], XT[:, m, :], AF.Square, [b_XT[m]], [b_YT[m]])

        def mixer(ti, li, have_sq):
            prenorm(li, 0, 0, have_sq)
            dump("h1", HT[:], b_HT, [128, 8, T])
            hcl = HC[:, li]
            zcl = ZC[:, li]
            stage('m_prenorm')
            if ti > 0:
                ACT(hcl[:, :, 0:30], hcl[:, :, T:T + 30], AF.Copy, b_HC[li], b_HC[li])
                ACT(zcl[:, :, 0:15], zcl[:, :, T:T + 15], AF.Copy, b_ZC[li], b_ZC[li])
            dgs = []
            for c in range(3):
                if ti == 0:
                    TT(P.dve, DG[:, c], identh[:].unsqueeze(1).broadcast_to([128, 31, 128]),
                       PV[:, li, PV_CW + 31 * c:PV_CW + 31 * c + 31].unsqueeze(2).broadcast_to([128, 31, 128]), ALU.mult,
                       [b_identh, b_PV], [b_DG[c]])
                    if ntiles > 1:
                        DMA(P.sp, "dgw%d" % c, dg_b[li, c], DG[:, c], reads=[b_DG[c]], writes=[b_dgb[li][c]])
                else:
                    DMA(P.sp, "dg%d" % c, DG[:, c], dg_b[li, c], reads=[b_dgb[li][c]], writes=[b_DG[c]])
                dgs.append((DG[:, c], b_DG[c]))
            def inproj_fm():
                w, bw = wslot("A")
                pb = gbank()
                for kc in range(8):
                    MM(PSG[:, pb, :], w[:, kc, :], HT[:, kc, :], kc == 0, kc == 7, [bw, b_HT[kc]], [b_PSG[pb]], signal=(kc == 7))
                wdone("A")
                return pb

            csq = []

            def conv_chunk(c):
                dg, bdg = dgs[c]
                pb = gbank()
                for k in range(31):
                    MM(PSG[:, pb, :], dg[:, k, :], hcl[:, c, k:k + T], k == 0, k == 30, [bdg, b_HC[li][c]], [b_PSG[pb]], signal=(k == 30))
                ACT(Y32[:, 3 + c, :], PSG[:, pb, :], AF.Identity, [b_PSG[pb], b_PV], [b_Y32[3 + c]], bias=PV[:, li, PV_CB + c:PV_CB + c + 1])
                q, bq = r_SQ.get()
                ACT(q, Y32[:, 3 + c, :], AF.Square, [b_Y32[3 + c]], [bq])
                csq.append((q, bq))

            def v_path_and_pool():
                def pool_sums():
                    TT(P.dve, PA[:, :, 1:527], zcl[:, :, 1:527], zcl[:, :, 0:526], ALU.add, b_ZC[li], [b_PA])
                    TT(P.dve, PB[64:128, 0, 3:527], PA[64:128, 0, 3:527], PA[64:128, 0, 1:525], ALU.add, [b_PA], [b_PB])
                    TT(P.dve, PB[:, 1, 3:527], PA[:, 1, 3:527], PA[:, 1, 1:525], ALU.add, [b_PA], [b_PB])
                    TT(P.dve, PA[:, 1, 7:527], PB[:, 1, 7:527], PB[:, 1, 3:523], ALU.add, [b_PB, b_PA], [b_PA])
                    TT(P.dve, PB[64:128, 1, 15:527], PA[64:128, 1, 15:527], PA[64:128, 1, 7:519], ALU.add, [b_PA, b_PB], [b_PB])
                st = VST[:]
                for tb in range(4):
                    v, bv = vvs[tb]
                    v3 = v.rearrange("p (h d) -> p h d", h=6)
                    P.op(P.dve, lambda e, v3=v3, tb=tb: e.tensor_reduce(out=st[:, 0, tb * 6:tb * 6 + 6], in_=v3, axis=AX.X, op=ALU.add), [bv], [b_VST])
                    ACT(VSQ[:], v, AF.Square, [bv], [b_VSQ])
                    P.op(P.dve, lambda e, tb=tb: e.tensor_reduce(out=st[:, 1, tb * 6:tb * 6 + 6], in_=VSQ[:].rearrange("p (h d) -> p h d", h=6), axis=AX.X,
                                                                  op=ALU.add), [b_VSQ], [b_VST])
                TS(st[:, 2, :], st[:, 0, :], 1.0 / 64, None, ALU.mult, None, [b_VST], [b_VST])
                TT(P.dve, st[:, 3, :], st[:, 2, :], st[:, 2, :], ALU.mult, [b_VST], [b_VST])
                STT(st[:, 3, :], st[:, 1, :], 1.0 / 64, st[:, 3, :], ALU.mult, ALU.subtract, [b_VST], [b_VST])
                TS(st[:, 3, :], st[:, 3, :], EPS, None, ALU.add, None, [b_VST], [b_VST])
                TT(P.pool, st[:, 4, :], st[:, 3, :], NH24[:], ALU.pow, [b_VST, b_NH24], [b_VST2])
                pool_sums()
                TT(P.dve, st[:, 0, :], st[:, 2, :], st[:, 4, :], ALU.mult, [b_VST, b_VST2], [b_VST3])
                TS(st[:, 0, :], st[:, 0, :], -1.0, None, ALU.mult, None, [b_VST3], [b_VST3])
                for tb in range(4):
                    v, bv = vvs[tb]
                    for h in range(6):
                        ACT(VN[:, tb, h * 64:(h + 1) * 64], v[:, h * 64:(h + 1) * 64], AF.Identity, [bv, b_VST2, b_VST3], [b_VN[tb]],
                            scale=st[:, 4, tb * 6 + h:tb * 6 + h + 1], bias=st[:, 0, tb * 6 + h:tb * 6 + h + 1])
                stage('m_v')
                srcs = [(PA, 0, 0), (PB, 0, 1), (PA, 1, 0), (PB, 1, 1)]
                for g in range(4):
                    ten, c, hh = srcs[g]
                    sl = slice(64 * hh, 64 * hh + 64)
                    STT(DD[sl, c, :], ten[sl, c, 15:527], INVW[sl, c:c + 1], zcl[sl, c, 15:527], ALU.mult, ALU.subtract,
                        [b_PA, b_PB, b_INVW, b_ZC[li][c]], [b_DD[c]])
                    if ti == 0:
                        t, bt = r_TMP.get()
                        TT(P.dve, t[sl, 0:16], ten[sl, c, 15:31], IC0[sl, c, :], ALU.mult, [b_PA, b_PB, b_IC0], [bt])
                        TT(P.dve, DD[sl, c, 0:16], t[sl, 0:16], zcl[sl, c, 15:31], ALU.subtract, [bt, b_ZC[li][c]], [b_DD[c]])

            def pool_mm():
                for c in range(2):
                    pb = gbank()
                    MM(PSG[:, pb, :], PWb[:, li, c, :], DD[:, c, :], True, True, [b_PWb, b_DD[c]], [b_PSG[pb]])
                    ACT(Y32[:, 6 + c, :], PSG[:, pb, :], AF.Identity, [b_PSG[pb], b_PV], [b_Y32[6 + c]], scale=PV[:, li, PV_PS + c:PV_PS + c + 1])
                dump("yc", Y32[:, 6:8, :], b_Y32[6:8], [128, 2, T])
                stage('m_pool')

            def sgu_mix():
                for c in range(3):
                    pb = gbank()
                    for tb in range(4):
                        for hh in range(2):
                            h = 2 * c + hh
                            MM(PSG[64 * hh:64 * hh + 64, pb, tb * 128:(tb + 1) * 128], VN[:, tb, h * 64:(h + 1) * 64], WST[:, li, h, :], True, True,
                               [b_VN[tb], b_WSTl[li]], [b_PSG[pb]], signal=(tb == 3 and hh == 1))
                    t, bt = r_TMP.get()
                    STT(t.rearrange("p (b t) -> p b t", b=4), PSG[:, pb, :].rearrange("p (b t) -> p b t", b=4), PV[:, li, PV_SG + c:PV_SG + c + 1],
                        CC[:, li, c, :].unsqueeze(1).broadcast_to([128, 4, 128]), ALU.mult, ALU.add, [b_PSG[pb], b_PV, b_CCl[li]], [bt])
                    TT(P.dve, Y32[:, c, :], UU[:, c, :], t, ALU.mult, [b_UU[c], bt], [b_Y32[c]])
                dump("ya", Y32[:, 0:3, :], b_Y32[0:3], [128, 3, T])
                stage('m_sgu')

            for c in range(3):
                pg = inproj_fm()
                th, bth = r_TH.get()
                ACT(th, PSG[:, pg, :], AF.Tanh, [b_PSG[pg]], [bth], scale=0.5)
                pa = inproj_fm()
                STT(hcl[:, c, 30:30 + T], th, 1.0, PSG[:, pa, :], ALU.add, ALU.mult, [bth, b_PSG[pa]], [b_HC[li][c]])
            stage('m_glu')
            vb = [gbank() for _ in range(4)]
            for jv in range(3):
                w, bw = wslot("A")
                for tb in range(4):
                    for kc in range(8):
                        MM(PSG[:, vb[tb], jv * 128:(jv + 1) * 128], HT[:, kc, tb * 128:(tb + 1) * 128], w[:, kc, :], kc == 0, kc == 7,
                           [bw, b_HT[kc]], [b_PSG[vb[tb]]], signal=(kc == 7))
                wdone("A")
            vvs = []
            for tb in range(4):
                v = VV4[:, tb, :]
                bv = b_VV4[tb]
                ACT(v, PSG[:, vb[tb], 0:384], AF.Gelu, [b_PSG[vb[tb]]], [bv])
                vvs.append((v, bv))
            for c in range(2):
                pb = inproj_fm()
                ACT(zcl[:, c, 15:15 + T], PSG[:, pb, :], AF.Copy, [b_PSG[pb]], [b_ZC[li][c]])
            stage('m_inproj')
            conv_chunk(0)
            v_path_and_pool()
            conv_chunk(1)
            conv_chunk(2)
            for c in range(3):
                pb = inproj_fm()
                ACT(UU[:, c, :], PSG[:, pb, :], AF.Gelu, [b_PSG[pb]], [b_UU[c]])
            stage('m_u')
            colm = tm_sum([(Y32[:, 3 + c, :], b_Y32[3 + c]) for c in range(3)], fp32=True)
            colq = tm_sum(csq)
            pool_mm()
            sgu_mix()
            mean4, bmean4 = small(colm, 1.0 / 384, 0.0)
            var4, bvar4 = small(colq, 1.0 / 384, EPS)
            m24, bm24 = r_S4.get()
            TT(P.dve, m24, mean4, mean4, ALU.mult, [bmean4], [bm24])
            TT(P.dve, var4, var4, m24, ALU.subtract, [bvar4, bm24], [bvar4])
            rs4, brs4 = rpow(var4, bvar4)
            mean, bmean = bcast(mean4, bmean4)
            rs, brs = bcast(rs4, brs4)

            def branch_norm(c0, c1, n, eps):
                sqs = []
                for c in range(c0, c1):
                    q, bq = r_SQ.get()
                    ACT(q, Y32[:, c, :], AF.Square, [b_Y32[c]], [bq])
                    sqs.append((q, bq))
                r, br = rstd_of(sqs, 1.0 / n, eps)
                for c in range(c0, c1):
                    STT(YT[:, c, :], Y32[:, c, :], PV[:, li, PV_BR + c:PV_BR + c + 1], r, ALU.mult, ALU.mult, [b_Y32[c], b_PV, br], [b_YT[c]])
            for c in range(3):
                t, bt = r_TMP.get()
                TT(P.dve, t, Y32[:, 3 + c, :], mean, ALU.subtract, [b_Y32[3 + c], bmean], [bt])
                TT(P.dve, t, t, rs, ALU.mult, [bt, brs], [bt])
                th, bth = r_TH.get()
                ACT(th, t, AF.Tanh, [bt, b_DV], [bth], scale=DV[:, li, 32 + c:33 + c], bias=DV[:, li, 35 + c:36 + c])
                l_, bl_ = r_SS.get()
                ACT(l_, t, AF.Identity, [bt, b_PV], [bl_], scale=PV[:, li, PV_CNG + c:PV_CNG + c + 1], bias=PV[:, li, PV_CNB + c:PV_CNB + c + 1])
                STT(Y32[:, 3 + c, :], th, 1.0, l_, ALU.add, ALU.mult, [bth, bl_], [b_Y32[3 + c]])
            dump("yb", Y32[:, 3:6, :], b_Y32[3:6], [128, 3, T])
            stage('m_conv')
            branch_norm(3, 6, 384, 4 * EPS)
            branch_norm(6, 8, 256, EPS)
            branch_norm(0, 3, 384, EPS)
            dump("yT", YT[:], b_YT, [128, 8, T])
            stage('m_brnorm')

            def produce(m):
                w, bw = wslot("A")
                pb = gbank()
                for kc in range(8):
                    MM(PSG[:, pb, :], w[:, kc, :], YT[:, kc, :], kc == 0, kc == 7, [bw, b_YT[kc]], [b_PSG[pb]], signal=(kc == 7))
                wdone("A")
                return pb
            postnorm_residual(li, 8, EPS, produce, True)
            stage('m_out')
            dump("x1", XT[:], b_XT, [128, 8, T])

        def ffn(ti, li, last):
            prenorm(li, 16, 24, True)
            dump("h2", HT[:], b_HT, [128, 8, T])
            ztl = ZT[:, li]
            fw = lambda k, j: PV[:, li, PV_FW + 44 * k + j:PV_FW + 44 * k + j + 1]
            fwv = lambda k: PV[:, li, PV_FW + 44 * k:PV_FW + 44 * k + 44].rearrange("p (g j) -> p j g", g=2)
            TT(P.dve, CORR[:, :, :, 0], ztl[:, :, :, 1], fwv(1), ALU.mult, [b_ZT[li], b_PV], [b_CORR])
            TT(P.dve, CT[:, :, :, 0], ztl[:, :, :, 0], fwv(0), ALU.mult, [b_ZT[li], b_PV], [b_CT])
            TT(P.dve, CORR[:, :, :, 0], CORR[:, :, :, 0], CT[:, :, :, 0], ALU.add, [b_CORR, b_CT], [b_CORR])
            TT(P.dve, CORR[:, :, :, 1], ztl[:, :, :, 1], fwv(0), ALU.mult, [b_ZT[li], b_PV], [b_CORR])
            def fin_act(p):
                gl, bgl = r_GL.get()
                ACT(gl, p[1][:, 0, :], AF.Gelu, [p[2]], [bgl])
                return gl, bgl

            def fin_dve(p, glp):
                TT(P.dve, AT[:, p[0], :], glp[0], p[1][:, 1, :], ALU.mult, [glp[1], p[2]], [b_AT[p[0]]])
            pend = None
            for j in range(NJ):
                w, bw = wslot("U")
                p0 = gpair()
                for g in range(2):
                    for kc in range(8):
                        MM(PSG[:, p0 + g, :], w[:, kc, g * 128:(g + 1) * 128], HT[:, kc, :], kc == 0, kc == 7, [bw, b_HT[kc]], [b_PSG[p0 + g]],
                           signal=(kc == 7))
                wdone("U")
                acc, bacc = r_ACC.get()
                pbs = [b_PSG[p0], b_PSG[p0 + 1]]
                for g in range(2):
                    jj = j + 22 * g
                    ACT(acc[:, g, :], PSG[:, p0 + g, :], AF.Identity, [b_PSG[p0 + g], b_PV], [bacc], scale=fw(2, jj),
                        bias=PV[:, li, PV_FB + jj:PV_FB + jj + 1])
                if pend is not None:
                    glp = fin_act(pend)
                for g in range(2):
                    jj = j + 22 * g
                    STT(acc[:, g, 1:T], PSG[:, p0 + g, 0:T - 1], fw(1, jj), acc[:, g, 1:T], ALU.mult, ALU.add, [b_PSG[p0 + g], b_PV, bacc], [bacc])
                    STT(acc[:, g, 2:T], PSG[:, p0 + g, 0:T - 2], fw(0, jj), acc[:, g, 2:T], ALU.mult, ALU.add, [b_PSG[p0 + g], b_PV, bacc], [bacc])
                CP(P.dve, ztl[:, j, :, :], PSG[:, p0:p0 + 2, T - 2:T], pbs, [b_ZT[li]])
                TT(P.dve, acc[:, :, 0:2], acc[:, :, 0:2], CORR[:, j, :, :], ALU.add, [bacc, b_CORR], [bacc])
                if pend is not None:
                    fin_dve(pend, glp)
                pend = (j, acc, bacc)
            glp = fin_act(pend)
            fin_dve(pend, glp)
            stage('f_up')
            dump("aT", AT[:, 0:4, :], b_AT[0:4], [128, 4, T])

            def produce(m):
                w, bw = wslot("D")
                pb = gbank()
                for kc in range(NJ):
                    MM(PSG[:, pb, :], w[:, kc, :], AT[:, kc, :], kc == 0, kc == NJ - 1, [bw, b_AT[kc]], [b_PSG[pb]], signal=(kc == NJ - 1))
                wdone("D")
                return pb
            postnorm_residual(li, 24, EPS, produce, not last)
            dump("x2", XT[:], b_XT, [128, 8, T])

        pump()
        for ti in range(ntiles):
            DMA(P.sp, "xin", XT[:], x_fm[ti], writes=b_XT)
            for li in range(L):
                mixer(ti, li, li > 0)
                ffn(ti, li, li == L - 1)
            DMA(P.sp, "xout", out_fm[ti], XT[:], reads=b_XT)

    except _Stop:
        pass
    P.barrier()
    print('sbuf bytes remaining', nc.sbuf_bytes_remaining)
    P.emit()
    return nc, dbg_out


def _fm(v, n):
    return np.ascontiguousarray(np.asarray(v, np.float32).reshape(n, 128).T)


def prep_shared(inp, layers):
    g = lambda k: np.asarray(inp[k], np.float32)
    pv, modc, winc, woutc, upc, dnc, wst, sgub, pwbd = [], [], [], [], [], [], [], [], []
    for l in layers:
        cols = [_fm(g("mix_pre_g")[l], 8), _fm(g("mix_post_g")[l], 8), _fm(g("branch_g")[l], 8), _fm(g("ffn_pre_g")[l], 8),
                _fm(g("ffn_post_g")[l], 8), _fm(g("sgu_norm_g")[l], 3), _fm(g("sgu_norm_b")[l], 3), _fm(g("conv_b")[l], 3),
                _fm(g("conv_norm_g")[l], 3), _fm(g("conv_norm_b")[l], 3), _fm(g("pool_scale")[l], 2)]
        cw = g("conv_w")[l].reshape(31, 3, 128).transpose(2, 1, 0).reshape(128, 93)
        fw = g("ffn_conv_w")[l].reshape(3, 44, 128).transpose(2, 0, 1).reshape(128, 132)
        cols += [cw, fw, _fm(g("ffn_conv_b")[l], 44), _fm(g("mod_b")[l], 48)]
        p = np.concatenate(cols, axis=1)
        assert p.shape == (128, NPV)
        pv.append(p)
        modc.append(g("mod_w")[l].reshape(8, 128, 12, 4, 128).transpose(2, 1, 3, 0, 4))
        winc.append(g("w_in")[l].reshape(8, 128, 14, 128).transpose(2, 1, 0, 3)[IN_ORDER])
        woutc.append(g("w_out")[l].reshape(8, 128, 8, 128).transpose(2, 1, 0, 3))
        upc.append(g("ffn_up")[l].reshape(8, 128, 2, NJ, 128).transpose(3, 1, 0, 2, 4).reshape(NJ, 128, 8, 256))
        dnc.append(g("ffn_down")[l].reshape(NJ, 128, 8, 128).transpose(2, 1, 0, 3))
        wst.append(g("sgu_w")[l].transpose(2, 0, 1))
        sgub.append(g("sgu_b")[l])
        pw = g("pool_w")[l]
        bd = np.zeros((128, 2, 128), np.float32)
        for c in range(2):
            bd[0:64, c, 0:64] = pw[2 * c]
            bd[64:128, c, 64:128] = pw[2 * c + 1]
        pwbd.append(bd)
    ca = lambda lst: np.ascontiguousarray(np.stack(lst, 0), dtype=np.float32)
    s_idx = np.arange(128)
    tri = (s_idx[None, :] >= s_idx[:, None]).astype(np.float32)
    wins = np.array([2, 4, 8, 16], np.float32)
    ic0 = np.zeros((128, 2, 16), np.float32)
    invw = np.zeros((128, 2), np.float32)
    for c in range(2):
        for hh in range(2):
            w_ = wins[2 * c + hh]
            ic0[64 * hh:64 * hh + 64, c, :] = 1.0 / np.minimum(np.arange(1, 17, dtype=np.float32), w_)
            invw[64 * hh:64 * hh + 64, c] = 1.0 / w_
    return {"pv": ca(pv), "mod_c": ca(modc), "w_in_c": ca(winc), "w_out_c": ca(woutc), "up_c": ca(upc), "down_c": ca(dnc),
            "wst": ca(wst), "sgub": ca(sgub), "pwbd": ca(pwbd), "identh": (0.5 * np.eye(128)).astype(np.float32), "identf": np.eye(128, dtype=np.float32), "tri": tri,
            "ic0": ic0, "invw": invw}


def x_to_fm(xb, ntiles=NT):
    return np.ascontiguousarray(np.asarray(xb, np.float32).reshape(ntiles, T, 8, 128).transpose(0, 3, 2, 1))


def fm_to_x(o):
    nt = o.shape[0]
    return np.ascontiguousarray(o.transpose(0, 3, 2, 1).reshape(nt * T, D_MODEL))


_CACHE = {}


def _get_prog(layers):
    key = tuple(layers)
    if key not in _CACHE:
        _CACHE[key] = build(list(range(len(layers))))[0]
    return _CACHE[key]


def run_layers(x, inputs, layers):
    shared = prep_shared(inputs, layers)
    c = np.asarray(inputs["c"], np.float32)
    nc = _get_prog(layers)
    in_maps = []
    for b in range(BATCH):
        m = dict(shared)
        m["x_fm"] = x_to_fm(x[b])
        m["c_fm"] = _fm(c[b], 8)
        in_maps.append(m)
    res = run_bass_kernel_spmd(nc, in_maps, core_ids=list(range(BATCH)))
    return np.stack([fm_to_x(res.results[b]["out_fm"]) for b in range(BATCH)], 0)


FUSED = True


def kernel(**inputs):
    x = np.asarray(inputs["x"], np.float32)
    if FUSED:
        out = run_layers(x, inputs, list(range(DEPTH)))
    else:
        out = x
        for l in range(DEPTH):
            out = run_layers(out, inputs, [l])
    return out.astype(np.float32)
```

```python
import contextlib
import numpy as np
import concourse.bass as bass
import concourse.mybir as mybir
from concourse.bass_utils import run_bass_kernel_spmd

F32 = mybir.dt.float32
BF16 = mybir.dt.bfloat16
AF = mybir.ActivationFunctionType
ALU = mybir.AluOpType
AX = mybir.AxisListType

D_MODEL = 1024
BATCH = 8
SEQ = 4096
DEPTH = 2
T = 512
NT = SEQ // T
D_FF = 2816
NJ = D_FF // 128
EPS = 1e-6
NPV = 374
IN_ORDER = [9, 6, 10, 7, 11, 8, 3, 4, 5, 12, 13, 0, 1, 2]

PV_PRE, PV_POST, PV_BR, PV_FPRE, PV_FPOST = 0, 8, 16, 24, 32
PV_SG, PV_SB, PV_CB, PV_CNG, PV_CNB, PV_PS = 40, 43, 46, 49, 52, 55
PV_CW, PV_FW, PV_FB, PV_MB = 57, 150, 282, 326


class Buf:
    __slots__ = ("name", "w", "r", "al", "excl")

    def __init__(self, name):
        self.name = name
        self.w = None
        self.r = []
        self.al = ()
        self.excl = False


class Eng:
    def __init__(self, name, attr):
        self.name = name
        self.attr = attr
        self.ops = []
        self.count = 0
        self.waited = {}


class Prog:
    def __init__(self, nc):
        self.nc = nc
        self.es = contextlib.ExitStack()
        self.pe = Eng("pe", "tensor")
        self.act = Eng("act", "scalar")
        self.dve = Eng("dve", "vector")
        self.pool = Eng("pool", "gpsimd")
        self.sp = Eng("sp", "sync")
        self.engs = [self.pe, self.act, self.dve, self.pool, self.sp]
        self.sems = {}
        for e in self.engs:
            self.sems[e.name] = self.es.enter_context(nc.semaphore("s_" + e.name))
        self.dma_counts = {}
        self.nbuf = 0

    def sbuf(self, name, shape, dt):
        return self.es.enter_context(self.nc.sbuf_tensor(name, list(shape), dt))

    def psum(self, name, shape, dt=F32):
        return self.es.enter_context(self.nc.psum_tensor(name, list(shape), dt))

    def buf(self, name=None):
        self.nbuf += 1
        return Buf(name or "b%d" % self.nbuf)

    def bufs(self, n, name=None):
        return [self.buf(None if name is None else "%s%d" % (name, i)) for i in range(n)]

    def dma_sem(self, name):
        key = "d_" + name
        self.sems[key] = self.es.enter_context(self.nc.semaphore(key))
        self.dma_counts[key] = 0
        return key

    def _collect(self, eng, reads, writes):
        waits = {}

        def need(tok, skip_same):
            if tok is None:
                return
            k, v = tok
            if skip_same and k == eng.name:
                return
            if eng.waited.get(k, 0) >= v:
                return
            if waits.get(k, 0) < v:
                waits[k] = v

        pe_same = eng.name == "pe"
        for b0 in reads:
            for b in (b0,) + tuple(b0.al):
                need(b.w, False)
                if b.excl:
                    for t in b.r:
                        need(t, True)
        for b0 in writes:
            for b in (b0,) + tuple(b0.al):
                need(b.w, pe_same)
                for t in b.r:
                    need(t, pe_same)
        for k, v in waits.items():
            eng.waited[k] = v
        return sorted(waits.items())

    def op(self, eng, fn, reads=(), writes=(), signal=True):
        waits = self._collect(eng, reads, writes)
        tok = (eng.name, eng.count + 1)
        for b in reads:
            b.r.append(tok)
        for b in writes:
            b.w = tok
            b.r = []
        inc = None
        if signal:
            eng.count += 1
            inc = (eng.name, 1)
        eng.ops.append((waits, fn, inc))

    def dma(self, eng, semkey, out, in_, reads=(), writes=(), **kw):
        waits = self._collect(eng, reads, writes)
        self.dma_counts[semkey] += 16
        tok = (semkey, self.dma_counts[semkey])
        for b in reads:
            b.r.append(tok)
        for b in writes:
            b.w = tok
            b.r = []

        def fn(e, out=out, in_=in_, kw=kw):
            return e.dma_start(out=out, in_=in_, **kw)
        eng.ops.append((waits, fn, (semkey, 16)))
        return tok

    def wait_tok(self, eng, tok):
        k, v = tok
        if eng.waited.get(k, 0) >= v:
            return
        eng.waited[k] = v
        eng.ops.append(([(k, v)], None, None))

    def barrier(self):
        toks = [(e.name, e.count) for e in self.engs if e.count > 0]
        toks += [(k, v) for k, v in self.dma_counts.items() if v > 0]
        for e in self.engs:
            for t in toks:
                if t[0] != e.name:
                    self.wait_tok(e, t)

    def emit(self):
        nc = self.nc
        with nc.Block() as block:
            for e in self.engs:
                if not e.ops:
                    continue

                def body(h, e=e):
                    for waits, fn, inc in e.ops:
                        for k, v in waits:
                            h.wait_ge(self.sems[k], v)
                        if fn is not None:
                            ins = fn(h)
                            if inc is not None:
                                ins.then_inc(self.sems[inc[0]], inc[1])
                getattr(block, e.attr)(body)
        self.es.close()


class Rot:
    def __init__(self, items):
        self.items = items
        self.i = 0

    def get(self):
        it = self.items[self.i % len(self.items)]
        self.i += 1
        return it


class _Stop(Exception):
    pass


def build(layers, ntiles=NT, dbg=(), stop_after=None):
    nc = bass.Bass("TRN2", target_bir_lowering=False)
    P = Prog(nc)
    L = len(layers)
    dbg = set(dbg)
    dbg_out = {}

    def din(name, shape, dt=F32):
        return nc.dram_tensor(name, list(shape), dt, kind="ExternalInput").ap()

    x_fm = din("x_fm", [ntiles, 128, 8, T])
    out_fm = nc.dram_tensor("out_fm", [ntiles, 128, 8, T], F32, kind="ExternalOutput").ap()
    c_fm = din("c_fm", [128, 8])
    pv_d = din("pv", [L, 128, NPV])
    mod_d = din("mod_c", [L, 12, 128, 4, 8, 128])
    win_d = din("w_in_c", [L, 14, 128, 8, 128])
    wout_d = din("w_out_c", [L, 8, 128, 8, 128])
    up_d = din("up_c", [L, NJ, 128, 8, 256])
    dn_d = din("down_c", [L, 8, 128, NJ, 128])
    wst_d = din("wst", [L, 128, 6, 128])
    sgub_d = din("sgub", [L, 6, 128])
    pwbd_d = din("pwbd", [L, 128, 2, 128])
    ident_d = din("identh", [128, 128])
    identf_d = din("identf", [128, 128])
    tri_d = din("tri", [128, 128])
    ic0_d = din("ic0", [128, 2, 16])
    invw_d = din("invw", [128, 2])

    def dscr(name, shape):
        return nc.dram_tensor(name, list(shape), BF16, kind="Internal").ap()
    win_b = dscr("w_in_b", [L, 14, 128, 8, 128])
    wout_b = dscr("w_out_b", [L, 8, 128, 8, 128])
    up_b = dscr("up_b", [L, NJ, 128, 8, 256])
    dn_b = dscr("down_b", [L, 8, 128, NJ, 128])
    b_win = [P.bufs(14) for _ in range(L)]; b_wout = [P.bufs(8) for _ in range(L)]
    b_up = [P.bufs(NJ) for _ in range(L)]; b_dn = [P.bufs(8) for _ in range(L)]
    dg_b = dscr("dg_b", [L, 3, 128, 31, 128]); b_dgb = [P.bufs(3, "dgb%d_" % l) for l in range(L)]

    PV = P.sbuf("PV", [128, L, NPV], F32); b_PV = P.buf("PV")
    MOD = P.sbuf("MOD", [128, L, 48], F32); b_MOD = P.buf("MOD")
    DV = P.sbuf("DV", [128, L, 40], F32); b_DV = P.buf("DV")
    CF = P.sbuf("CF", [128, 8], F32); b_CF = P.buf("CF")
    CF2 = P.sbuf("CF2", [128, 8], F32); b_CF2 = P.buf("CF2")
    SCb = P.sbuf("SCb", [128, 8], BF16); b_SCb = P.buf("SCb")
    WST = P.sbuf("WST", [128, L, 6, 128], BF16); b_WSTl = P.bufs(L, "WST")
    CC = P.sbuf("CC", [128, L, 3, 128], F32); b_CCl = P.bufs(L, "CC")
    PWb = P.sbuf("PWb", [128, L, 2, 128], BF16); b_PWb = P.buf("PWb")
    onesb = P.sbuf("onesb", [128, 128], BF16); b_onesb = P.buf("onesb")
    onesf = P.sbuf("onesf", [128, 128], F32); b_onesf = P.buf("onesf")
    identh = P.sbuf("identh_s", [128, 128], BF16); b_identh = P.buf("identh")
    identf = P.sbuf("identf_s", [128, 128], F32); b_identf = P.buf("identf")
    NH4 = P.sbuf("NH4", [128, 4], F32); b_NH4 = P.buf("NH4")
    S4 = P.sbuf("S4", [128, 8, 4], F32); r_S4 = Rot([(S4[:, i, :], P.buf("S4_%d" % i)) for i in range(8)])
    D4 = P.sbuf("D4", [128, 1, 4, 128], F32); r_D4 = Rot([(D4[:, i], P.buf("D4_%d" % i)) for i in range(1)])
    IC0 = P.sbuf("IC0", [128, 2, 16], F32); b_IC0 = P.buf("IC0")
    INVW = P.sbuf("INVW", [128, 2], F32); b_INVW = P.buf("INVW")
    TRI = P.sbuf("TRI", [128, 128], F32); b_TRI = P.buf("TRI")
    HC = P.sbuf("HC", [128, L, 3, 30 + T], BF16); b_HC = [P.bufs(3) for _ in range(L)]
    ZC = P.sbuf("ZC", [128, L, 2, 15 + T], F32); b_ZC = [P.bufs(2) for _ in range(L)]
    ZT = P.sbuf("ZT", [128, L, NJ, 2, 2], F32); b_ZT = P.bufs(L)
    CORR = P.sbuf("CORR", [128, NJ, 2, 2], F32); b_CORR = P.buf("CORR")
    CT = P.sbuf("CT", [128, NJ, 2, 2], F32); b_CT = P.buf("CT")
    XT = P.sbuf("XT", [128, 8, T], F32); b_XT = P.bufs(8, "XT")
    HT = P.sbuf("HT", [128, 8, T], BF16); b_HT = P.bufs(8, "HT")
    Y32 = P.sbuf("Y32", [128, 8, T], F32); b_Y32 = P.bufs(8, "Y32")
    YT = P.sbuf("YT", [128, 8, T], BF16); b_YT = P.bufs(8, "YT")
    SQ = P.sbuf("SQ", [128, 3, T], BF16); r_SQ = Rot([(SQ[:, i, :], P.buf("SQ%d" % i)) for i in range(3)])
    TMP = P.sbuf("TMP", [128, 3, T], F32); r_TMP = Rot([(TMP[:, i, :], P.buf("TMP%d" % i)) for i in range(3)])
    SS = P.sbuf("SS", [128, 3, T], F32); r_SS = Rot([(SS[:, i, :], P.buf("SS%d" % i)) for i in range(3)])
    AR = P.sbuf("AR", [128, NJ * T], BF16)
    AT = AR[:].rearrange("p (j t) -> p j t", j=NJ); b_AT = P.bufs(NJ, "AT")
    ARf = AR[:].bitcast(F32)
    UU = ARf[:, 0:1536].rearrange("p (c t) -> p c t", c=3); b_UU = P.bufs(3, "UU")
    TH = ARf[:, 1536:2560].rearrange("p (c t) -> p c t", c=2); r_TH = Rot([(TH[:, i, :], P.buf("TH%d" % i)) for i in range(2)])
    PA = ARf[:, 2560:2560 + 2 * 527].rearrange("p (c t) -> p c t", c=2); b_PA = P.buf("PA")
    PB = ARf[:, 3616:3616 + 2 * 527].rearrange("p (c t) -> p c t", c=2); b_PB = P.buf("PB")
    DD = AR[:, 2 * 4672:2 * 4672 + 2 * T].rearrange("p (c t) -> p c t", c=2); b_DD = P.bufs(2, "DD")
    mix_scr = b_UU + [it[1] for it in r_TH.items] + [b_PA, b_PB] + b_DD
    for b in mix_scr:
        b.al = tuple(b_AT)
    for b in b_AT:
        b.al = tuple(mix_scr)
    VV4 = P.sbuf("VV4", [128, 4, 384], F32); b_VV4 = P.bufs(4, "VV4")
    VSQ = P.sbuf("VSQ", [128, 384], F32); b_VSQ = P.buf("VSQ")
    VST = P.sbuf("VST", [128, 5, 24], F32); b_VST = P.buf("VST"); b_VST2 = P.buf("VST2"); b_VST3 = P.buf("VST3")
    NH24 = P.sbuf("NH24", [128, 24], F32); b_NH24 = P.buf("NH24")
    VN = P.sbuf("VN", [128, 4, 384], BF16); b_VN = P.bufs(4, "VN")
    ACC = P.sbuf("ACC", [128, 5, 2, T], F32); r_ACC = Rot([(ACC[:, i], P.buf("ACC%d" % i)) for i in range(5)])
    GL = P.sbuf("GL", [128, 3, T], F32); r_GL = Rot([(GL[:, i, :], P.buf("GL%d" % i)) for i in range(3)])
    DG = P.sbuf("DG", [128, 3, 31, 128], BF16); b_DG = P.bufs(3, "DG")
    XA = DG[:].rearrange("p a k n -> p (a k n)").bitcast(F32)[:, 0:8 * T].rearrange("p (c t) -> p c t", c=8); b_XA = P.bufs(8, "XA")
    for b in b_XA:
        b.al = tuple(b_DG)
    for b in b_DG:
        b.al = tuple(b_XA)
    WA = P.sbuf("WA", [128, 4, 8, 128], BF16); b_WA = P.bufs(4, "WA")
    WU = P.sbuf("WU", [128, 3, 8, 256], BF16); b_WU = P.bufs(3, "WU")
    WD = P.sbuf("WD", [128, 2, NJ, 128], BF16); b_WD = P.bufs(2, "WD")
    PSG = P.psum("PSG", [128, 6, T]); b_PSG = P.bufs(6, "PSG")
    PST = P.psum("PST", [128, 2, T]); b_PST = P.bufs(2, "PST")
    for b in b_PSG + b_PST:
        b.excl = True
    gen_i = [0]

    def gbank():
        i = gen_i[0] % 6
        gen_i[0] += 1
        return i

    pair_i = [0]

    def gpair():
        i = pair_i[0] % 3
        pair_i[0] += 1
        return 2 * i
    st_i = [0]

    def sbank():
        i = st_i[0] % 2
        st_i[0] += 1
        return i

    def ACT(out, in_, func, reads, writes, scale=1.0, bias=0.0):
        P.op(P.act, lambda e: e.activation(out=out, in_=in_, func=func, scale=scale, bias=bias), reads, writes)

    def TT(eng, out, in0, in1, op, reads, writes):
        P.op(eng, lambda e: e.tensor_tensor(out=out, in0=in0, in1=in1, op=op), reads, writes)

    def STT(out, in0, scalar, in1, op0, op1, reads, writes):
        P.op(P.dve, lambda e: e.scalar_tensor_tensor(out=out, in0=in0, scalar=scalar, in1=in1, op0=op0, op1=op1), reads, writes)

    def TS(out, in0, s1, s2, op0, op1, reads, writes):
        if s2 is None:
            P.op(P.dve, lambda e: e.tensor_scalar(out=out, in0=in0, scalar1=s1, scalar2=None, op0=op0), reads, writes)
        else:
            P.op(P.dve, lambda e: e.tensor_scalar(out=out, in0=in0, scalar1=s1, scalar2=s2, op0=op0, op1=op1), reads, writes)

    def CP(eng, out, in_, reads, writes):
        P.op(eng, lambda e: e.tensor_copy(out=out, in_=in_), reads, writes)

    def MM(out, lhsT, rhs, start, stop, reads, writes, signal=True):
        P.op(P.pe, lambda e: e.matmul(out, lhsT=lhsT, rhs=rhs, start=start, stop=stop), reads, writes, signal=signal)

    dsem = {}

    def DMA(eng, sem, out, in_, reads=(), writes=()):
        if sem not in dsem:
            dsem[sem] = P.dma_sem(sem)
        return P.dma(eng, dsem[sem], out, in_, reads, writes)

    def stage(name):
        if stop_after == name:
            raise _Stop()

    def dump(name, ap, bufs, shape):
        if name in dbg and name not in dbg_out:
            t = nc.dram_tensor("dbg_" + name, list(shape), ap.dtype, kind="ExternalOutput").ap()
            dbg_out[name] = t
            DMA(P.sp, "dbg_" + name, t, ap, reads=bufs)

    try:
        DMA(P.sp, "c0", PV[:], pv_d.rearrange("l p n -> p l n"), writes=[b_PV])
        DMA(P.sp, "c1", CF[:], c_fm, writes=[b_CF])
        DMA(P.sp, "c2", TRI[:], tri_d, writes=[b_TRI])
        DMA(P.sp, "c3", IC0[:], ic0_d, writes=[b_IC0])
        DMA(P.sp, "c4", INVW[:], invw_d, writes=[b_INVW])
        P.op(P.dve, lambda e: e.memset(onesb[:], 1.0), writes=[b_onesb])
        P.op(P.dve, lambda e: e.memset(onesf[:], 1.0), writes=[b_onesf])
        P.op(P.dve, lambda e: e.memset(NH4[:], -0.5), writes=[b_NH4])
        DMA(P.sp, "c5", identf[:], identf_d, writes=[b_identf])
        P.op(P.dve, lambda e: e.memset(NH24[:], -0.5), writes=[b_NH24])
        P.op(P.dve, lambda e: e.memset(HC[:].rearrange("p l c t -> p (l c t)"), 0.0), writes=[b for bl in b_HC for b in bl])
        P.op(P.dve, lambda e: e.memset(ZC[:].rearrange("p l c t -> p (l c t)"), 0.0), writes=[b for bl in b_ZC for b in bl])
        P.op(P.dve, lambda e: e.memset(ZT[:].rearrange("p l j a b -> p (l j a b)"), 0.0), writes=b_ZT)
        ACT(CF2[:], CF[:], AF.Tanh, [b_CF], [b_CF2], scale=0.5)
        STT(CF2[:], CF2[:], 1.0, CF[:], ALU.add, ALU.mult, [b_CF2, b_CF], [b_CF2])
        TS(SCb[:], CF2[:], 0.5, None, ALU.mult, None, [b_CF2], [b_SCb])
        stage('s_consts')
        MS = Y32[:].rearrange("p c t -> p (c t)").bitcast(BF16).rearrange("p (s j k n) -> p s j k n", s=2, j=4, k=8)
        b_MS = P.bufs(2, "MS")
        for b in b_MS:
            b.al = tuple(b_Y32)
        for b in b_Y32:
            b.al = tuple(b_MS)
        for li in range(L):
            for g in range(12):
                s = (li * 12 + g) % 2
                DMA(P.pool, "ms%d" % s, MS[:, s], mod_d[li, g], writes=[b_MS[s]])
                for jj in range(4):
                    j = g * 4 + jj
                    for kc in range(8):
                        MM(PST[:, 0, li * 48 + j:li * 48 + j + 1], MS[:, s, jj, kc, :], SCb[:, kc:kc + 1], kc == 0, kc == 7,
                           [b_MS[s], b_SCb], [b_PST[0]], signal=(kc == 7))
        for li in range(L):
            TT(P.dve, MOD[:, li, :], PST[:, 0, li * 48:(li + 1) * 48], PV[:, li, PV_MB:PV_MB + 48], ALU.add, [b_PST[0], b_PV], [b_MOD])
        stage('s_mod')
        DMA(P.pool, "ci", identh[:], ident_d, writes=[b_identh])
        DMA(P.pool, "cp", PWb[:], pwbd_d.rearrange("l p c n -> p l c n"), writes=[b_PWb])
        stage('s_casts')
        for li in range(L):
            STT(DV[:, li, 0:8], MOD[:, li, 8:16], 1.0, PV[:, li, PV_PRE:PV_PRE + 8], ALU.add, ALU.mult, [b_MOD, b_PV], [b_DV])
            TT(P.dve, DV[:, li, 8:16], MOD[:, li, 16:24], PV[:, li, PV_POST:PV_POST + 8], ALU.mult, [b_MOD, b_PV], [b_DV])
            STT(DV[:, li, 16:24], MOD[:, li, 32:40], 1.0, PV[:, li, PV_FPRE:PV_FPRE + 8], ALU.add, ALU.mult, [b_MOD, b_PV], [b_DV])
            TT(P.dve, DV[:, li, 24:32], MOD[:, li, 40:48], PV[:, li, PV_FPOST:PV_FPOST + 8], ALU.mult, [b_MOD, b_PV], [b_DV])
            TS(DV[:, li, 32:38], PV[:, li, PV_CNG:PV_CNG + 6], 0.5, None, ALU.mult, None, [b_PV], [b_DV])
        stage('s_derived')
        WSF = TMP[:].rearrange("p a t -> p (a t)")[:, 0:768].rearrange("p (h t) -> p h t", h=6)
        b_WSF = P.buf("WSF")
        b_WSF.al = tuple(it[1] for it in r_TMP.items)
        for it in r_TMP.items:
            it[1].al = (b_WSF,)
        SBB = SS[:].rearrange("p a t -> p (a t)")[:, 0:384].rearrange("p (c t) -> p c t", c=3)
        b_SBB = P.buf("SBB")
        b_SBB.al = tuple(it[1] for it in r_SS.items)
        for it in r_SS.items:
            it[1].al = (b_SBB,)
        for li in range(L):
            DMA(P.sp, "wsf", WSF, wst_d[li], writes=[b_WSF])
            for c in range(3):
                DMA(P.sp, "sbb", SBB[0:64, c, :], sgub_d[li, 2 * c:2 * c + 1, :].broadcast_to([64, 128]), writes=[b_SBB])
                DMA(P.sp, "sbb", SBB[64:128, c, :], sgub_d[li, 2 * c + 1:2 * c + 2, :].broadcast_to([64, 128]), writes=[b_SBB])
            TT(P.dve, WSF, WSF, TRI[:].unsqueeze(1).broadcast_to([128, 6, 128]), ALU.mult, [b_WSF, b_TRI], [b_WSF])
            CP(P.dve, WST[:, li], WSF, [b_WSF], [b_WSTl[li]])
            for h in range(6):
                MM(PSG[:, h // 4, (h % 4) * 128:(h % 4 + 1) * 128], onesf[:], WSF[:, h, :], True, True, [b_onesf, b_WSF], [b_PSG[h // 4]])
            for c in range(3):
                for hh in range(2):
                    h = 2 * c + hh
                    sl = slice(64 * hh, 64 * hh + 64)
                    STT(CC[sl, li, c, :], PSG[sl, h // 4, (h % 4) * 128:(h % 4 + 1) * 128], PV[sl, li, PV_SB + c:PV_SB + c + 1], SBB[sl, c, :],
                        ALU.mult, ALU.add, [b_PSG[h // 4], b_PV, b_SBB], [b_CCl[li]])
        stage('s_sgu')

        stream = []
        for ti in range(ntiles):
            for li in range(L):
                for j in range(14):
                    stream.append(("A", win_d[li, j], win_b[li, j], b_win[li][j], ti == 0))
                for m in range(8):
                    stream.append(("A", wout_d[li, m], wout_b[li, m], b_wout[li][m], ti == 0))
                for j in range(NJ):
                    stream.append(("U", up_d[li, j], up_b[li, j], b_up[li][j], ti == 0))
                for m in range(8):
                    stream.append(("D", dn_d[li, m], dn_b[li, m], b_dn[li][m], ti == 0))
        rings = {"A": (WA, b_WA, 4), "U": (WU, b_WU, 3), "D": (WD, b_WD, 2)}
        issued = {"A": 0, "U": 0, "D": 0}
        consumed = {"A": 0, "U": 0, "D": 0}
        nxt = [0]

        def pump():
            while nxt[0] < len(stream):
                r, src32, scr, sb, first = stream[nxt[0]]
                ten, bl, ns = rings[r]
                if issued[r] - consumed[r] >= ns:
                    break
                s = issued[r] % ns
                if first:
                    DMA(P.pool, "wc%s%d" % (r, s), ten[:, s], src32, writes=[bl[s]])
                    if ntiles > 1:
                        DMA(P.sp, "ws%s%d" % (r, s), scr, ten[:, s], reads=[bl[s]], writes=[sb])
                else:
                    DMA(P.sp, "w%s%d" % (r, s), ten[:, s], scr, reads=[sb], writes=[bl[s]])
                issued[r] += 1
                nxt[0] += 1

        def wslot(r):
            ten, bl, ns = rings[r]
            assert consumed[r] < issued[r], "weight chunk not issued"
            s = consumed[r] % ns
            return ten[:, s], bl[s]

        def wdone(r):
            consumed[r] += 1
            pump()

        sc_i = [0]

        def tm_sum(chunks, fp32=False):
            col = 4 * (sc_i[0] % 16)
            sc_i[0] += 1
            oc = onesf[:, 0:1] if fp32 else onesb[:, 0:1]
            bo = b_onesf if fp32 else b_onesb
            n = len(chunks)
            for tb in range(4):
                for i, (ap, b) in enumerate(chunks):
                    MM(PST[:, 0, col + tb:col + tb + 1], ap[:, tb * 128:(tb + 1) * 128], oc, i == 0, i == n - 1, [b, bo], [b_PST[0]],
                       signal=(tb == 3 and i == n - 1))
            return col

        def small(col, scale, bias):
            a, ba = r_S4.get()
            ACT(a, PST[:, 0, col:col + 4], AF.Identity, [b_PST[0]], [ba], scale=scale, bias=bias)
            return a, ba

        def rpow(a, ba):
            r, br = r_S4.get()
            TT(P.pool, r, a, NH4[:], ALU.pow, [ba, b_NH4], [br])
            return r, br

        def bcast(a, ba):
            d4, bd4 = r_D4.get()
            TT(P.dve, d4, identf[:].unsqueeze(1).broadcast_to([128, 4, 128]), a.unsqueeze(2).broadcast_to([128, 4, 128]), ALU.mult,
               [b_identf, ba], [bd4])
            pb = gbank()
            MM(PSG[:, pb, :], onesf[:], d4.rearrange("p a b -> p (a b)"), True, True, [b_onesf, bd4], [b_PSG[pb]])
            return PSG[:, pb, :], b_PSG[pb]

        def rstd_of(chunks, scale, eps):
            n = len(chunks)
            for i, (ap, b) in enumerate(chunks):
                MM(PST[:, 1, :], onesb[:], ap, i == 0, i == n - 1, [b_onesb, b], [b_PST[1]], signal=(i == n - 1))
            r, br = r_SS.get()
            ACT(r, PST[:, 1, :], AF.Ln, [b_PST[1]], [br], scale=scale, bias=eps)
            ACT(r, r, AF.Exp, [br], [br], scale=-0.5)
            return r, br

        def prenorm(li, gcol, shcol, have_sq, xa=False):
            SQ8 = YT
            X, bX = (XA, b_XA) if xa else (XT, b_XT)
            if not have_sq:
                for c in range(8):
                    ACT(SQ8[:, c, :], X[:, c, :], AF.Square, [bX[c]], [b_YT[c]])
            r, br = rstd_of([(SQ8[:, c, :], b_YT[c]) for c in range(8)], 1.0 / D_MODEL, EPS)
            for c in range(8):
                t, bt = r_TMP.get()
                TT(P.dve, t, X[:, c, :], r, ALU.mult, [bX[c], br], [bt])
                ACT(HT[:, c, :], t, AF.Identity, [bt, b_DV, b_MOD], [b_HT[c]],
                    scale=DV[:, li, gcol + c:gcol + c + 1], bias=MOD[:, li, shcol + c:shcol + c + 1])
            if xa:
                for c in range(8):
                    ACT(XT[:, c, :], XA[:, c, :], AF.Copy, [b_XA[c]], [b_XT[c]])

        def postnorm_residual(li, ggcol, eps, produce, sq_next):
            SQ8 = HT
            for m in range(8):
                pb = produce(m)
                CP(P.dve, Y32[:, m, :], PSG[:, pb, :], [b_PSG[pb]], [b_Y32[m]])
                ACT(SQ8[:, m, :], Y32[:, m, :], AF.Square, [b_Y32[m]], [b_HT[m]])
            r, br = rstd_of([(SQ8[:, m, :], b_HT[m]) for m in range(8)], 1.0 / D_MODEL, eps)
            for m in range(8):
                t, bt = r_TMP.get()
                STT(t, Y32[:, m, :], DV[:, li, ggcol + m:ggcol + m + 1], r, ALU.mult, ALU.mult, [b_Y32[m], b_DV, br], [bt])
                TT(P.dve, XT[:, m, :], XT[:, m, :], t, ALU.add, [b_XT[m], bt], [b_XT[m]])
                if sq_next:
                    ACT(YT[:, m, :], XT[:, m, :], AF.Square, [b_XT[m]], [b_YT[m]])

        def mixer(ti, li, have_sq, xa=False):
            prenorm(li, 0, 0, have_sq, xa)
            dump("h1", HT[:], b_HT, [128, 8, T])
            hcl = HC[:, li]
            zcl = ZC[:, li]
            stage('m_prenorm')
            if ti > 0:
                ACT(hcl[:, :, 0:30], hcl[:, :, T:T + 30], AF.Copy, b_HC[li], b_HC[li])
                ACT(zcl[:, :, 0:15], zcl[:, :, T:T + 15], AF.Copy, b_ZC[li], b_ZC[li])
            def inproj_fm():
                w, bw = wslot("A")
                pb = gbank()
                for kc in range(8):
                    MM(PSG[:, pb, :], w[:, kc, :], HT[:, kc, :], kc == 0, kc == 7, [bw, b_HT[kc]], [b_PSG[pb]], signal=(kc == 7))
                wdone("A")
                return pb

            csq = []

            def conv_chunk(c):
                dg, bdg = dgs[c]
                pb = gbank()
                for k in range(31):
                    MM(PSG[:, pb, :], dg[:, k, :], hcl[:, c, k:k + T], k == 0, k == 30, [bdg, b_HC[li][c]], [b_PSG[pb]], signal=(k == 30))
                ACT(Y32[:, 3 + c, :], PSG[:, pb, :], AF.Identity, [b_PSG[pb], b_PV], [b_Y32[3 + c]], bias=PV[:, li, PV_CB + c:PV_CB + c + 1])
                q, bq = r_SQ.get()
                ACT(q, Y32[:, 3 + c, :], AF.Square, [b_Y32[3 + c]], [bq])
                csq.append((q, bq))

            def v_path_and_pool():
                def pool_sums():
                    TT(P.dve, PA[:, :, 1:527], zcl[:, :, 1:527], zcl[:, :, 0:526], ALU.add, b_ZC[li], [b_PA])
                    TT(P.dve, PB[64:128, 0, 3:527], PA[64:128, 0, 3:527], PA[64:128, 0, 1:525], ALU.add, [b_PA], [b_PB])
                    TT(P.dve, PB[:, 1, 3:527], PA[:, 1, 3:527], PA[:, 1, 1:525], ALU.add, [b_PA], [b_PB])
                    TT(P.dve, PA[:, 1, 7:527], PB[:, 1, 7:527], PB[:, 1, 3:523], ALU.add, [b_PB, b_PA], [b_PA])
                    TT(P.dve, PB[64:128, 1, 15:527], PA[64:128, 1, 15:527], PA[64:128, 1, 7:519], ALU.add, [b_PA, b_PB], [b_PB])
                st = VST[:]
                for tb in range(4):
                    v, bv = vvs[tb]
                    v3 = v.rearrange("p (h d) -> p h d", h=6)
                    P.op(P.dve, lambda e, v3=v3, tb=tb: e.tensor_reduce(out=st[:, 0, tb * 6:tb * 6 + 6], in_=v3, axis=AX.X, op=ALU.add), [bv], [b_VST])
                    ACT(VSQ[:], v, AF.Square, [bv], [b_VSQ])
                    P.op(P.dve, lambda e, tb=tb: e.tensor_reduce(out=st[:, 1, tb * 6:tb * 6 + 6], in_=VSQ[:].rearrange("p (h d) -> p h d", h=6), axis=AX.X,
                                                                  op=ALU.add), [b_VSQ], [b_VST])
                TS(st[:, 2, :], st[:, 0, :], 1.0 / 64, None, ALU.mult, None, [b_VST], [b_VST])
                TT(P.dve, st[:, 3, :], st[:, 2, :], st[:, 2, :], ALU.mult, [b_VST], [b_VST])
                STT(st[:, 3, :], st[:, 1, :], 1.0 / 64, st[:, 3, :], ALU.mult, ALU.subtract, [b_VST], [b_VST])
                TS(st[:, 3, :], st[:, 3, :], EPS, None, ALU.add, None, [b_VST], [b_VST])
                TT(P.pool, st[:, 4, :], st[:, 3, :], NH24[:], ALU.pow, [b_VST, b_NH24], [b_VST2])
                pool_sums()
                TT(P.dve, st[:, 0, :], st[:, 2, :], st[:, 4, :], ALU.mult, [b_VST, b_VST2], [b_VST3])
                TS(st[:, 0, :], st[:, 0, :], -1.0, None, ALU.mult, None, [b_VST3], [b_VST3])
                for tb in range(4):
                    v, bv = vvs[tb]
                    for h in range(6):
                        ACT(VN[:, tb, h * 64:(h + 1) * 64], v[:, h * 64:(h + 1) * 64], AF.Identity, [bv, b_VST2, b_VST3], [b_VN[tb]],
                            scale=st[:, 4, tb * 6 + h:tb * 6 + h + 1], bias=st[:, 0, tb * 6 + h:tb * 6 + h + 1])
                stage('m_v')
                srcs = [(PA, 0, 0), (PB, 0, 1), (PA, 1, 0), (PB, 1, 1)]
                for g in range(4):
                    ten, c, hh = srcs[g]
                    sl = slice(64 * hh, 64 * hh + 64)
                    STT(DD[sl, c, :], ten[sl, c, 15:527], INVW[sl, c:c + 1], zcl[sl, c, 15:527], ALU.mult, ALU.subtract,
                        [b_PA, b_PB, b_INVW, b_ZC[li][c]], [b_DD[c]])
                    if ti == 0:
                        t, bt = r_TMP.get()
                        TT(P.dve, t[sl, 0:16], ten[sl, c, 15:31], IC0[sl, c, :], ALU.mult, [b_PA, b_PB, b_IC0], [bt])
                        TT(P.dve, DD[sl, c, 0:16], t[sl, 0:16], zcl[sl, c, 15:31], ALU.subtract, [bt, b_ZC[li][c]], [b_DD[c]])

            def pool_mm():
                for c in range(2):
                    pb = gbank()
                    MM(PSG[:, pb, :], PWb[:, li, c, :], DD[:, c, :], True, True, [b_PWb, b_DD[c]], [b_PSG[pb]])
                    ACT(Y32[:, 6 + c, :], PSG[:, pb, :], AF.Identity, [b_PSG[pb], b_PV], [b_Y32[6 + c]], scale=PV[:, li, PV_PS + c:PV_PS + c + 1])
                dump("yc", Y32[:, 6:8, :], b_Y32[6:8], [128, 2, T])
                stage('m_pool')

            def sgu_mix():
                for c in range(3):
                    pb = gbank()
                    for tb in range(4):
                        for hh in range(2):
                            h = 2 * c + hh
                            MM(PSG[64 * hh:64 * hh + 64, pb, tb * 128:(tb + 1) * 128], VN[:, tb, h * 64:(h + 1) * 64], WST[:, li, h, :], True, True,
                               [b_VN[tb], b_WSTl[li]], [b_PSG[pb]], signal=(tb == 3 and hh == 1))
                    t, bt = r_TMP.get()
                    STT(t.rearrange("p (b t) -> p b t", b=4), PSG[:, pb, :].rearrange("p (b t) -> p b t", b=4), PV[:, li, PV_SG + c:PV_SG + c + 1],
                        CC[:, li, c, :].unsqueeze(1).broadcast_to([128, 4, 128]), ALU.mult, ALU.add, [b_PSG[pb], b_PV, b_CCl[li]], [bt])
                    TT(P.dve, Y32[:, c, :], UU[:, c, :], t, ALU.mult, [b_UU[c], bt], [b_Y32[c]])
                dump("ya", Y32[:, 0:3, :], b_Y32[0:3], [128, 3, T])
                stage('m_sgu')

            for c in range(3):
                pg = inproj_fm()
                th, bth = r_TH.get()
                ACT(th, PSG[:, pg, :], AF.Tanh, [b_PSG[pg]], [bth], scale=0.5)
                pa = inproj_fm()
                STT(hcl[:, c, 30:30 + T], th, 1.0, PSG[:, pa, :], ALU.add, ALU.mult, [bth, b_PSG[pa]], [b_HC[li][c]])
            stage('m_glu')
            dgs = []
            for c in range(3):
                if ti == 0:
                    TT(P.dve, DG[:, c], identh[:].unsqueeze(1).broadcast_to([128, 31, 128]),
                       PV[:, li, PV_CW + 31 * c:PV_CW + 31 * c + 31].unsqueeze(2).broadcast_to([128, 31, 128]), ALU.mult,
                       [b_identh, b_PV], [b_DG[c]])
                    if ntiles > 1:
                        DMA(P.sp, "dgw%d" % c, dg_b[li, c], DG[:, c], reads=[b_DG[c]], writes=[b_dgb[li][c]])
                else:
                    DMA(P.sp, "dg%d" % c, DG[:, c], dg_b[li, c], reads=[b_dgb[li][c]], writes=[b_DG[c]])
                dgs.append((DG[:, c], b_DG[c]))
            vb = [gbank() for _ in range(4)]
            for jv in range(3):
                w, bw = wslot("A")
                for tb in range(4):
                    for kc in range(8):
                        MM(PSG[:, vb[tb], jv * 128:(jv + 1) * 128], HT[:, kc, tb * 128:(tb + 1) * 128], w[:, kc, :], kc == 0, kc == 7,
                           [bw, b_HT[kc]], [b_PSG[vb[tb]]], signal=(kc == 7))
                wdone("A")
            vvs = []
            for tb in range(4):
                v = VV4[:, tb, :]
                bv = b_VV4[tb]
                ACT(v, PSG[:, vb[tb], 0:384], AF.Gelu, [b_PSG[vb[tb]]], [bv])
                vvs.append((v, bv))
            for c in range(2):
                pb = inproj_fm()
                ACT(zcl[:, c, 15:15 + T], PSG[:, pb, :], AF.Copy, [b_PSG[pb]], [b_ZC[li][c]])
            stage('m_inproj')
            conv_chunk(0)
            v_path_and_pool()
            conv_chunk(1)
            conv_chunk(2)
            for c in range(3):
                pb = inproj_fm()
                ACT(UU[:, c, :], PSG[:, pb, :], AF.Gelu, [b_PSG[pb]], [b_UU[c]])
            stage('m_u')
            colm = tm_sum([(Y32[:, 3 + c, :], b_Y32[3 + c]) for c in range(3)], fp32=True)
            colq = tm_sum(csq)
            pool_mm()
            sgu_mix()
            mean4, bmean4 = small(colm, 1.0 / 384, 0.0)
            var4, bvar4 = small(colq, 1.0 / 384, EPS)
            m24, bm24 = r_S4.get()
            TT(P.dve, m24, mean4, mean4, ALU.mult, [bmean4], [bm24])
            TT(P.dve, var4, var4, m24, ALU.subtract, [bvar4, bm24], [bvar4])
            rs4, brs4 = rpow(var4, bvar4)
            mean, bmean = bcast(mean4, bmean4)
            rs, brs = bcast(rs4, brs4)

            def branch_norm(c0, c1, n, eps):
                sqs = []
                for c in range(c0, c1):
                    q, bq = r_SQ.get()
                    ACT(q, Y32[:, c, :], AF.Square, [b_Y32[c]], [bq])
                    sqs.append((q, bq))
                r, br = rstd_of(sqs, 1.0 / n, eps)
                for c in range(c0, c1):
                    STT(YT[:, c, :], Y32[:, c, :], PV[:, li, PV_BR + c:PV_BR + c + 1], r, ALU.mult, ALU.mult, [b_Y32[c], b_PV, br], [b_YT[c]])
            for c in range(3):
                t, bt = r_TMP.get()
                TT(P.dve, t, Y32[:, 3 + c, :], mean, ALU.subtract, [b_Y32[3 + c], bmean], [bt])
                TT(P.dve, t, t, rs, ALU.mult, [bt, brs], [bt])
                th, bth = r_TH.get()
                ACT(th, t, AF.Tanh, [bt, b_DV], [bth], scale=DV[:, li, 32 + c:33 + c], bias=DV[:, li, 35 + c:36 + c])
                l_, bl_ = r_SS.get()
                ACT(l_, t, AF.Identity, [bt, b_PV], [bl_], scale=PV[:, li, PV_CNG + c:PV_CNG + c + 1], bias=PV[:, li, PV_CNB + c:PV_CNB + c + 1])
                STT(Y32[:, 3 + c, :], th, 1.0, l_, ALU.add, ALU.mult, [bth, bl_], [b_Y32[3 + c]])
            dump("yb", Y32[:, 3:6, :], b_Y32[3:6], [128, 3, T])
            stage('m_conv')
            branch_norm(3, 6, 384, 4 * EPS)
            branch_norm(6, 8, 256, EPS)
            branch_norm(0, 3, 384, EPS)
            dump("yT", YT[:], b_YT, [128, 8, T])
            stage('m_brnorm')

            def produce(m):
                w, bw = wslot("A")
                pb = gbank()
                for kc in range(8):
                    MM(PSG[:, pb, :], w[:, kc, :], YT[:, kc, :], kc == 0, kc == 7, [bw, b_YT[kc]], [b_PSG[pb]], signal=(kc == 7))
                wdone("A")
                return pb
            postnorm_residual(li, 8, EPS, produce, True)
            stage('m_out')
            dump("x1", XT[:], b_XT, [128, 8, T])

        def ffn(ti, li, last):
            prenorm(li, 16, 24, True)
            dump("h2", HT[:], b_HT, [128, 8, T])
            ztl = ZT[:, li]
            fw = lambda k, j: PV[:, li, PV_FW + 44 * k + j:PV_FW + 44 * k + j + 1]
            fwv = lambda k: PV[:, li, PV_FW + 44 * k:PV_FW + 44 * k + 44].rearrange("p (g j) -> p j g", g=2)
            TT(P.dve, CORR[:, :, :, 0], ztl[:, :, :, 1], fwv(1), ALU.mult, [b_ZT[li], b_PV], [b_CORR])
            TT(P.dve, CT[:, :, :, 0], ztl[:, :, :, 0], fwv(0), ALU.mult, [b_ZT[li], b_PV], [b_CT])
            TT(P.dve, CORR[:, :, :, 0], CORR[:, :, :, 0], CT[:, :, :, 0], ALU.add, [b_CORR, b_CT], [b_CORR])
            TT(P.dve, CORR[:, :, :, 1], ztl[:, :, :, 1], fwv(0), ALU.mult, [b_ZT[li], b_PV], [b_CORR])
            def fin_act(p):
                gl, bgl = r_GL.get()
                ACT(gl, p[1][:, 0, :], AF.Gelu, [p[2]], [bgl])
                return gl, bgl

            def fin_dve(p, glp):
                TT(P.dve, AT[:, p[0], :], glp[0], p[1][:, 1, :], ALU.mult, [glp[1], p[2]], [b_AT[p[0]]])
            pend = None
            for j in range(NJ):
                w, bw = wslot("U")
                p0 = gpair()
                for g in range(2):
                    for kc in range(8):
                        MM(PSG[:, p0 + g, :], w[:, kc, g * 128:(g + 1) * 128], HT[:, kc, :], kc == 0, kc == 7, [bw, b_HT[kc]], [b_PSG[p0 + g]],
                           signal=(kc == 7))
                wdone("U")
                acc, bacc = r_ACC.get()
                pbs = [b_PSG[p0], b_PSG[p0 + 1]]
                for g in range(2):
                    jj = j + 22 * g
                    ACT(acc[:, g, :], PSG[:, p0 + g, :], AF.Identity, [b_PSG[p0 + g], b_PV], [bacc], scale=fw(2, jj),
                        bias=PV[:, li, PV_FB + jj:PV_FB + jj + 1])
                if pend is not None:
                    glp = fin_act(pend)
                for g in range(2):
                    jj = j + 22 * g
                    STT(acc[:, g, 1:T], PSG[:, p0 + g, 0:T - 1], fw(1, jj), acc[:, g, 1:T], ALU.mult, ALU.add, [b_PSG[p0 + g], b_PV, bacc], [bacc])
                    STT(acc[:, g, 2:T], PSG[:, p0 + g, 0:T - 2], fw(0, jj), acc[:, g, 2:T], ALU.mult, ALU.add, [b_PSG[p0 + g], b_PV, bacc], [bacc])
                CP(P.dve, ztl[:, j, :, :], PSG[:, p0:p0 + 2, T - 2:T], pbs, [b_ZT[li]])
                TT(P.dve, acc[:, :, 0:2], acc[:, :, 0:2], CORR[:, j, :, :], ALU.add, [bacc, b_CORR], [bacc])
                if pend is not None:
                    fin_dve(pend, glp)
                pend = (j, acc, bacc)
            glp = fin_act(pend)
            fin_dve(pend, glp)
            stage('f_up')
            dump("aT", AT[:, 0:4, :], b_AT[0:4], [128, 4, T])

            def produce(m):
                w, bw = wslot("D")
                pb = gbank()
                for kc in range(NJ):
                    MM(PSG[:, pb, :], w[:, kc, :], AT[:, kc, :], kc == 0, kc == NJ - 1, [bw, b_AT[kc]], [b_PSG[pb]], signal=(kc == NJ - 1))
                wdone("D")
                return pb
            postnorm_residual(li, 24, EPS, produce, not last)
            dump("x2", XT[:], b_XT, [128, 8, T])

        pump()
        for ti in range(ntiles):
            if ti == 0:
                DMA(P.sp, "xin", XT[:], x_fm[ti], writes=b_XT)
            for li in range(L):
                mixer(ti, li, li > 0, xa=(ti > 0 and li == 0))
                if li == L - 1 and ti + 1 < ntiles:
                    DMA(P.sp, "xin", XA, x_fm[ti + 1], writes=b_XA)
                ffn(ti, li, li == L - 1)
            DMA(P.sp, "xout", out_fm[ti], XT[:], reads=b_XT)

    except _Stop:
        pass
    P.barrier()
    print('sbuf bytes remaining', nc.sbuf_bytes_remaining)
    P.emit()
    return nc, dbg_out


def _fm(v, n):
    return np.ascontiguousarray(np.asarray(v, np.float32).reshape(n, 128).T)


def prep_shared(inp, layers):
    g = lambda k: np.asarray(inp[k], np.float32)
    pv, modc, winc, woutc, upc, dnc, wst, sgub, pwbd = [], [], [], [], [], [], [], [], []
    for l in layers:
        cols = [_fm(g("mix_pre_g")[l], 8), _fm(g("mix_post_g")[l], 8), _fm(g("branch_g")[l], 8), _fm(g("ffn_pre_g")[l], 8),
                _fm(g("ffn_post_g")[l], 8), _fm(g("sgu_norm_g")[l], 3), _fm(g("sgu_norm_b")[l], 3), _fm(g("conv_b")[l], 3),
                _fm(g("conv_norm_g")[l], 3), _fm(g("conv_norm_b")[l], 3), _fm(g("pool_scale")[l], 2)]
        cw = g("conv_w")[l].reshape(31, 3, 128).transpose(2, 1, 0).reshape(128, 93)
        fw = g("ffn_conv_w")[l].reshape(3, 44, 128).transpose(2, 0, 1).reshape(128, 132)
        cols += [cw, fw, _fm(g("ffn_conv_b")[l], 44), _fm(g("mod_b")[l], 48)]
        p = np.concatenate(cols, axis=1)
        assert p.shape == (128, NPV)
        pv.append(p)
        modc.append(g("mod_w")[l].reshape(8, 128, 12, 4, 128).transpose(2, 1, 3, 0, 4))
        winc.append(g("w_in")[l].reshape(8, 128, 14, 128).transpose(2, 1, 0, 3)[IN_ORDER])
        woutc.append(g("w_out")[l].reshape(8, 128, 8, 128).transpose(2, 1, 0, 3))
        upc.append(g("ffn_up")[l].reshape(8, 128, 2, NJ, 128).transpose(3, 1, 0, 2, 4).reshape(NJ, 128, 8, 256))
        dnc.append(g("ffn_down")[l].reshape(NJ, 128, 8, 128).transpose(2, 1, 0, 3))
        wst.append(g("sgu_w")[l].transpose(2, 0, 1))
        sgub.append(g("sgu_b")[l])
        pw = g("pool_w")[l]
        bd = np.zeros((128, 2, 128), np.float32)
        for c in range(2):
            bd[0:64, c, 0:64] = pw[2 * c]
            bd[64:128, c, 64:128] = pw[2 * c + 1]
        pwbd.append(bd)
    ca = lambda lst: np.ascontiguousarray(np.stack(lst, 0), dtype=np.float32)
    s_idx = np.arange(128)
    tri = (s_idx[None, :] >= s_idx[:, None]).astype(np.float32)
    wins = np.array([2, 4, 8, 16], np.float32)
    ic0 = np.zeros((128, 2, 16), np.float32)
    invw = np.zeros((128, 2), np.float32)
    for c in range(2):
        for hh in range(2):
            w_ = wins[2 * c + hh]
            ic0[64 * hh:64 * hh + 64, c, :] = 1.0 / np.minimum(np.arange(1, 17, dtype=np.float32), w_)
            invw[64 * hh:64 * hh + 64, c] = 1.0 / w_
    return {"pv": ca(pv), "mod_c": ca(modc), "w_in_c": ca(winc), "w_out_c": ca(woutc), "up_c": ca(upc), "down_c": ca(dnc),
            "wst": ca(wst), "sgub": ca(sgub), "pwbd": ca(pwbd), "identh": (0.5 * np.eye(128)).astype(np.float32), "identf": np.eye(128, dtype=np.float32), "tri": tri,
            "ic0": ic0, "invw": invw}


def x_to_fm(xb, ntiles=NT):
    return np.ascontiguousarray(np.asarray(xb, np.float32).reshape(ntiles, T, 8, 128).transpose(0, 3, 2, 1))


def fm_to_x(o):
    nt = o.shape[0]
    return np.ascontiguousarray(o.transpose(0, 3, 2, 1).reshape(nt * T, D_MODEL))


_CACHE = {}


def _get_prog(layers):
    key = tuple(layers)
    if key not in _CACHE:
        _CACHE[key] = build(list(range(len(layers))))[0]
    return _CACHE[key]


def run_layers(x, inputs, layers):
    shared = prep_shared(inputs, layers)
    c = np.asarray(inputs["c"], np.float32)
    nc = _get_prog(layers)
    in_maps = []
    for b in range(BATCH):
        m = dict(shared)
        m["x_fm"] = x_to_fm(x[b])
        m["c_fm"] = _fm(c[b], 8)
        in_maps.append(m)
    res = run_bass_kernel_spmd(nc, in_maps, core_ids=list(range(BATCH)))
    return np.stack([fm_to_x(res.results[b]["out_fm"]) for b in range(BATCH)], 0)


FUSED = True


def kernel(**inputs):
    x = np.asarray(inputs["x"], np.float32)
    if FUSED:
        out = run_layers(x, inputs, list(range(DEPTH)))
    else:
        out = x
        for l in range(DEPTH):
            out = run_layers(out, inputs, [l])
    return out.astype(np.float32)
```

```python
import contextlib
import numpy as np
import concourse.bass as bass
import concourse.mybir as mybir
from concourse.bass_utils import run_bass_kernel_spmd

F32 = mybir.dt.float32
BF16 = mybir.dt.bfloat16
AF = mybir.ActivationFunctionType
ALU = mybir.AluOpType
AX = mybir.AxisListType

D_MODEL = 1024
BATCH = 8
SEQ = 4096
DEPTH = 2
T = 512
NT = SEQ // T
D_FF = 2816
NJ = D_FF // 128
EPS = 1e-6
NPV = 374
IN_ORDER = [9, 6, 10, 7, 11, 8, 3, 4, 5, 12, 13, 0, 1, 2]

PV_PRE, PV_POST, PV_BR, PV_FPRE, PV_FPOST = 0, 8, 16, 24, 32
PV_SG, PV_SB, PV_CB, PV_CNG, PV_CNB, PV_PS = 40, 43, 46, 49, 52, 55
PV_CW, PV_FW, PV_FB, PV_MB = 57, 150, 282, 326


class Buf:
    __slots__ = ("name", "w", "r", "al", "excl")

    def __init__(self, name):
        self.name = name
        self.w = None
        self.r = []
        self.al = ()
        self.excl = False


class Eng:
    def __init__(self, name, attr):
        self.name = name
        self.attr = attr
        self.ops = []
        self.count = 0
        self.waited = {}


class Prog:
    def __init__(self, nc):
        self.nc = nc
        self.es = contextlib.ExitStack()
        self.pe = Eng("pe", "tensor")
        self.act = Eng("act", "scalar")
        self.dve = Eng("dve", "vector")
        self.pool = Eng("pool", "gpsimd")
        self.sp = Eng("sp", "sync")
        self.engs = [self.pe, self.act, self.dve, self.pool, self.sp]
        self.sems = {}
        for e in self.engs:
            self.sems[e.name] = self.es.enter_context(nc.semaphore("s_" + e.name))
        self.dma_counts = {}
        self.nbuf = 0

    def sbuf(self, name, shape, dt):
        return self.es.enter_context(self.nc.sbuf_tensor(name, list(shape), dt))

    def psum(self, name, shape, dt=F32):
        return self.es.enter_context(self.nc.psum_tensor(name, list(shape), dt))

    def buf(self, name=None):
        self.nbuf += 1
        return Buf(name or "b%d" % self.nbuf)

    def bufs(self, n, name=None):
        return [self.buf(None if name is None else "%s%d" % (name, i)) for i in range(n)]

    def dma_sem(self, name):
        key = "d_" + name
        self.sems[key] = self.es.enter_context(self.nc.semaphore(key))
        self.dma_counts[key] = 0
        return key

    def _collect(self, eng, reads, writes):
        waits = {}

        def need(tok, skip_same):
            if tok is None:
                return
            k, v = tok
            if skip_same and k == eng.name:
                return
            if eng.waited.get(k, 0) >= v:
                return
            if waits.get(k, 0) < v:
                waits[k] = v

        pe_same = eng.name == "pe"
        for b0 in reads:
            for b in (b0,) + tuple(b0.al):
                need(b.w, False)
                if b.excl:
                    for t in b.r:
                        need(t, True)
        for b0 in writes:
            for b in (b0,) + tuple(b0.al):
                need(b.w, pe_same)
                for t in b.r:
                    need(t, pe_same)
        for k, v in waits.items():
            eng.waited[k] = v
        return sorted(waits.items())

    def op(self, eng, fn, reads=(), writes=(), signal=True):
        waits = self._collect(eng, reads, writes)
        tok = (eng.name, eng.count + 1)
        for b in reads:
            b.r.append(tok)
        for b in writes:
            b.w = tok
            b.r = []
        inc = None
        if signal:
            eng.count += 1
            inc = (eng.name, 1)
        eng.ops.append((waits, fn, inc))

    def dma(self, eng, semkey, out, in_, reads=(), writes=(), **kw):
        waits = self._collect(eng, reads, writes)
        self.dma_counts[semkey] += 16
        tok = (semkey, self.dma_counts[semkey])
        for b in reads:
            b.r.append(tok)
        for b in writes:
            b.w = tok
            b.r = []

        def fn(e, out=out, in_=in_, kw=kw):
            return e.dma_start(out=out, in_=in_, **kw)
        eng.ops.append((waits, fn, (semkey, 16)))
        return tok

    def wait_tok(self, eng, tok):
        k, v = tok
        if eng.waited.get(k, 0) >= v:
            return
        eng.waited[k] = v
        eng.ops.append(([(k, v)], None, None))

    def barrier(self):
        toks = [(e.name, e.count) for e in self.engs if e.count > 0]
        toks += [(k, v) for k, v in self.dma_counts.items() if v > 0]
        for e in self.engs:
            for t in toks:
                if t[0] != e.name:
                    self.wait_tok(e, t)

    def emit(self):
        nc = self.nc
        with nc.Block() as block:
            for e in self.engs:
                if not e.ops:
                    continue

                def body(h, e=e):
                    for waits, fn, inc in e.ops:
                        for k, v in waits:
                            h.wait_ge(self.sems[k], v)
                        if fn is not None:
                            ins = fn(h)
                            if inc is not None:
                                ins.then_inc(self.sems[inc[0]], inc[1])
                getattr(block, e.attr)(body)
        self.es.close()


class Rot:
    def __init__(self, items):
        self.items = items
        self.i = 0

    def get(self):
        it = self.items[self.i % len(self.items)]
        self.i += 1
        return it


class _Stop(Exception):
    pass


def build(layers, ntiles=NT, dbg=(), stop_after=None):
    nc = bass.Bass("TRN2", target_bir_lowering=False)
    P = Prog(nc)
    L = len(layers)
    dbg = set(dbg)
    dbg_out = {}

    def din(name, shape, dt=F32):
        return nc.dram_tensor(name, list(shape), dt, kind="ExternalInput").ap()

    x_fm = din("x_fm", [ntiles, 128, 8, T])
    out_fm = nc.dram_tensor("out_fm", [ntiles, 128, 8, T], F32, kind="ExternalOutput").ap()
    c_fm = din("c_fm", [128, 8])
    pv_d = din("pv", [L, 128, NPV])
    mod_d = din("mod_c", [L, 12, 128, 4, 8, 128])
    win_d = din("w_in_c", [L, 14, 128, 8, 128])
    wout_d = din("w_out_c", [L, 8, 128, 8, 128])
    up_d = din("up_c", [L, NJ, 128, 8, 256])
    dn_d = din("down_c", [L, 8, 128, NJ, 128])
    wst_d = din("wst", [L, 128, 6, 128])
    sgub_d = din("sgub", [L, 6, 128])
    pwbd_d = din("pwbd", [L, 128, 2, 128])
    ident_d = din("identh", [128, 128])
    identf_d = din("identf", [128, 128])
    tri_d = din("tri", [128, 128])
    ic0_d = din("ic0", [128, 2, 16])
    invw_d = din("invw", [128, 2])

    def dscr(name, shape):
        return nc.dram_tensor(name, list(shape), BF16, kind="Internal").ap()
    win_b = dscr("w_in_b", [L, 14, 128, 8, 128])
    wout_b = dscr("w_out_b", [L, 8, 128, 8, 128])
    up_b = dscr("up_b", [L, NJ, 128, 8, 256])
    dn_b = dscr("down_b", [L, 8, 128, NJ, 128])
    b_win = [P.bufs(14) for _ in range(L)]; b_wout = [P.bufs(8) for _ in range(L)]
    b_up = [P.bufs(NJ) for _ in range(L)]; b_dn = [P.bufs(8) for _ in range(L)]; b_dne = [P.bufs(8) for _ in range(L)]
    dg_b = dscr("dg_b", [L, 3, 128, 31, 128]); b_dgb = [P.bufs(3, "dgb%d_" % l) for l in range(L)]

    PV = P.sbuf("PV", [128, L, NPV], F32); b_PV = P.buf("PV")
    MOD = P.sbuf("MOD", [128, L, 48], F32); b_MOD = P.buf("MOD")
    DV = P.sbuf("DV", [128, L, 40], F32); b_DV = P.buf("DV")
    CF = P.sbuf("CF", [128, 8], F32); b_CF = P.buf("CF")
    CF2 = P.sbuf("CF2", [128, 8], F32); b_CF2 = P.buf("CF2")
    SCb = P.sbuf("SCb", [128, 8], BF16); b_SCb = P.buf("SCb")
    WST = P.sbuf("WST", [128, L, 6, 128], BF16); b_WSTl = P.bufs(L, "WST")
    CC = P.sbuf("CC", [128, L, 3, 128], F32); b_CCl = P.bufs(L, "CC")
    PWb = P.sbuf("PWb", [128, L, 2, 128], BF16); b_PWb = P.buf("PWb")
    onesb = P.sbuf("onesb", [128, 128], BF16); b_onesb = P.buf("onesb")
    onesf = P.sbuf("onesf", [128, 128], F32); b_onesf = P.buf("onesf")
    identh = P.sbuf("identh_s", [128, 128], BF16); b_identh = P.buf("identh")
    identf = P.sbuf("identf_s", [128, 128], F32); b_identf = P.buf("identf")
    NH4 = P.sbuf("NH4", [128, 4], F32); b_NH4 = P.buf("NH4")
    DUM = P.sbuf("DUM", [128, 2], F32); b_DUM = P.buf("DUM")
    S4 = P.sbuf("S4", [128, 8, 4], F32); r_S4 = Rot([(S4[:, i, :], P.buf("S4_%d" % i)) for i in range(8)])
    D4 = P.sbuf("D4", [128, 1, 4, 128], F32); r_D4 = Rot([(D4[:, i], P.buf("D4_%d" % i)) for i in range(1)])
    IC0 = P.sbuf("IC0", [128, 2, 16], F32); b_IC0 = P.buf("IC0")
    INVW = P.sbuf("INVW", [128, 2], F32); b_INVW = P.buf("INVW")
    TRI = P.sbuf("TRI", [128, 128], F32); b_TRI = P.buf("TRI")
    HC = P.sbuf("HC", [128, L, 3, 30 + T], BF16); b_HC = [P.bufs(3) for _ in range(L)]
    ZC = P.sbuf("ZC", [128, L, 2, 15 + T], F32); b_ZC = [P.bufs(2) for _ in range(L)]
    ZT = P.sbuf("ZT", [128, L, NJ, 2, 2], F32); b_ZT = P.bufs(L)
    CORR = P.sbuf("CORR", [128, NJ, 2, 2], F32); b_CORR = P.buf("CORR")
    CT = P.sbuf("CT", [128, NJ, 2, 2], F32); b_CT = P.buf("CT")
    XT = P.sbuf("XT", [128, 8, T], F32); b_XT = P.bufs(8, "XT")
    HT = P.sbuf("HT", [128, 8, T], BF16); b_HT = P.bufs(8, "HT")
    Y32 = P.sbuf("Y32", [128, 8, T], F32); b_Y32 = P.bufs(8, "Y32")
    YT = P.sbuf("YT", [128, 8, T], BF16); b_YT = P.bufs(8, "YT")
    SQ = P.sbuf("SQ", [128, 3, T], BF16); r_SQ = Rot([(SQ[:, i, :], P.buf("SQ%d" % i)) for i in range(3)])
    TMP = P.sbuf("TMP", [128, 3, T], F32); r_TMP = Rot([(TMP[:, i, :], P.buf("TMP%d" % i)) for i in range(3)])
    SS = P.sbuf("SS", [128, 2, T], F32); r_SS = Rot([(SS[:, i, :], P.buf("SS%d" % i)) for i in range(2)])
    AR = P.sbuf("AR", [128, NJ * T], BF16)
    AT = AR[:].rearrange("p (j t) -> p j t", j=NJ); b_AT = P.bufs(NJ, "AT")
    ARf = AR[:].bitcast(F32)
    UU = ARf[:, 0:1536].rearrange("p (c t) -> p c t", c=3); b_UU = P.bufs(3, "UU")
    TH = ARf[:, 1536:2560].rearrange("p (c t) -> p c t", c=2); r_TH = Rot([(TH[:, i, :], P.buf("TH%d" % i)) for i in range(2)])
    PA = ARf[:, 2560:2560 + 2 * 527].rearrange("p (c t) -> p c t", c=2); b_PA = P.buf("PA")
    PB = ARf[:, 3616:3616 + 2 * 527].rearrange("p (c t) -> p c t", c=2); b_PB = P.buf("PB")
    DD = AR[:, 2 * 4672:2 * 4672 + 2 * T].rearrange("p (c t) -> p c t", c=2); b_DD = P.bufs(2, "DD")
    mix_scr = b_UU + [it[1] for it in r_TH.items] + [b_PA, b_PB] + b_DD
    for b in mix_scr:
        b.al = tuple(b_AT)
    for b in b_AT:
        b.al = tuple(mix_scr)
    VV4 = P.sbuf("VV4", [128, 4, 384], F32); b_VV4 = P.bufs(4, "VV4")
    VSQ = P.sbuf("VSQ", [128, 384], F32); b_VSQ = P.buf("VSQ")
    VST = P.sbuf("VST", [128, 5, 24], F32); b_VST = P.buf("VST"); b_VST2 = P.buf("VST2"); b_VST3 = P.buf("VST3")
    NH24 = P.sbuf("NH24", [128, 24], F32); b_NH24 = P.buf("NH24")
    VN = P.sbuf("VN", [128, 4, 384], BF16); b_VN = P.bufs(4, "VN")
    ACC = P.sbuf("ACC", [128, 5, 2, T], F32); r_ACC = Rot([(ACC[:, i], P.buf("ACC%d" % i)) for i in range(5)])
    GL = P.sbuf("GL", [128, 3, T], F32); r_GL = Rot([(GL[:, i, :], P.buf("GL%d" % i)) for i in range(3)])
    DG = P.sbuf("DG", [128, 3, 31, 128], BF16); b_DG = P.bufs(3, "DG")
    XA = DG[:].rearrange("p a k n -> p (a k n)").bitcast(F32)[:, 0:8 * T].rearrange("p (c t) -> p c t", c=8); b_XA = P.bufs(8, "XA")
    for b in b_XA:
        b.al = tuple(b_DG)
    for b in b_DG:
        b.al = tuple(b_XA)
    WA = P.sbuf("WA", [128, 4, 8, 128], BF16); b_WA = P.bufs(4, "WA")
    WU = P.sbuf("WU", [128, 3, 8, 256], BF16); b_WU = P.bufs(3, "WU")
    WD = P.sbuf("WD", [128, 2, 16, 128], BF16); b_WD = P.bufs(2, "WD")
    WE = P.sbuf("WE", [128, 2, NJ - 16, 128], BF16); b_WE = P.bufs(2, "WE")
    PSG = P.psum("PSG", [128, 6, T]); b_PSG = P.bufs(6, "PSG")
    PST = P.psum("PST", [128, 2, T]); b_PST = P.bufs(2, "PST")
    for b in b_PSG + b_PST:
        b.excl = True
    gen_i = [0]

    def gbank():
        i = gen_i[0] % 6
        gen_i[0] += 1
        return i

    pair_i = [0]

    def gpair():
        i = pair_i[0] % 3
        pair_i[0] += 1
        return 2 * i
    st_i = [0]

    def sbank():
        i = st_i[0] % 2
        st_i[0] += 1
        return i

    def ACT(out, in_, func, reads, writes, scale=1.0, bias=0.0):
        P.op(P.act, lambda e: e.activation(out=out, in_=in_, func=func, scale=scale, bias=bias), reads, writes)

    def TT(eng, out, in0, in1, op, reads, writes):
        P.op(eng, lambda e: e.tensor_tensor(out=out, in0=in0, in1=in1, op=op), reads, writes)

    def STT(out, in0, scalar, in1, op0, op1, reads, writes):
        P.op(P.dve, lambda e: e.scalar_tensor_tensor(out=out, in0=in0, scalar=scalar, in1=in1, op0=op0, op1=op1), reads, writes)

    def TS(out, in0, s1, s2, op0, op1, reads, writes):
        if s2 is None:
            P.op(P.dve, lambda e: e.tensor_scalar(out=out, in0=in0, scalar1=s1, scalar2=None, op0=op0), reads, writes)
        else:
            P.op(P.dve, lambda e: e.tensor_scalar(out=out, in0=in0, scalar1=s1, scalar2=s2, op0=op0, op1=op1), reads, writes)

    def CP(eng, out, in_, reads, writes):
        P.op(eng, lambda e: e.tensor_copy(out=out, in_=in_), reads, writes)

    def MM(out, lhsT, rhs, start, stop, reads, writes, signal=True):
        P.op(P.pe, lambda e: e.matmul(out, lhsT=lhsT, rhs=rhs, start=start, stop=stop), reads, writes, signal=signal)

    dsem = {}

    def DMA(eng, sem, out, in_, reads=(), writes=()):
        if sem not in dsem:
            dsem[sem] = P.dma_sem(sem)
        return P.dma(eng, dsem[sem], out, in_, reads, writes)

    def stage(name):
        if stop_after == name:
            raise _Stop()

    def dump(name, ap, bufs, shape):
        if name in dbg and name not in dbg_out:
            t = nc.dram_tensor("dbg_" + name, list(shape), ap.dtype, kind="ExternalOutput").ap()
            dbg_out[name] = t
            DMA(P.sp, "dbg_" + name, t, ap, reads=bufs)

    try:
        DMA(P.sp, "c0", PV[:], pv_d.rearrange("l p n -> p l n"), writes=[b_PV])
        DMA(P.sp, "c1", CF[:], c_fm, writes=[b_CF])
        DMA(P.sp, "c2", TRI[:], tri_d, writes=[b_TRI])
        DMA(P.sp, "c3", IC0[:], ic0_d, writes=[b_IC0])
        DMA(P.sp, "c4", INVW[:], invw_d, writes=[b_INVW])
        P.op(P.dve, lambda e: e.memset(onesb[:], 1.0), writes=[b_onesb])
        P.op(P.dve, lambda e: e.memset(onesf[:], 1.0), writes=[b_onesf])
        P.op(P.dve, lambda e: e.memset(NH4[:], -0.5), writes=[b_NH4])
        P.op(P.dve, lambda e: e.memset(DUM[:], 1.0), writes=[b_DUM])
        DMA(P.sp, "c5", identf[:], identf_d, writes=[b_identf])
        P.op(P.dve, lambda e: e.memset(NH24[:], -0.5), writes=[b_NH24])
        P.op(P.dve, lambda e: e.memset(HC[:].rearrange("p l c t -> p (l c t)"), 0.0), writes=[b for bl in b_HC for b in bl])
        P.op(P.dve, lambda e: e.memset(ZC[:].rearrange("p l c t -> p (l c t)"), 0.0), writes=[b for bl in b_ZC for b in bl])
        P.op(P.dve, lambda e: e.memset(ZT[:].rearrange("p l j a b -> p (l j a b)"), 0.0), writes=b_ZT)
        ACT(CF2[:], CF[:], AF.Tanh, [b_CF], [b_CF2], scale=0.5)
        STT(CF2[:], CF2[:], 1.0, CF[:], ALU.add, ALU.mult, [b_CF2, b_CF], [b_CF2])
        TS(SCb[:], CF2[:], 0.5, None, ALU.mult, None, [b_CF2], [b_SCb])
        stage('s_consts')
        MS = Y32[:].rearrange("p c t -> p (c t)").bitcast(BF16).rearrange("p (s j k n) -> p s j k n", s=2, j=4, k=8)
        b_MS = P.bufs(2, "MS")
        for b in b_MS:
            b.al = tuple(b_Y32)
        for b in b_Y32:
            b.al = tuple(b_MS)
        for li in range(L):
            for g in range(12):
                s = (li * 12 + g) % 2
                DMA(P.pool, "ms%d" % s, MS[:, s], mod_d[li, g], writes=[b_MS[s]])
                for jj in range(4):
                    j = g * 4 + jj
                    for kc in range(8):
                        MM(PST[:, 0, li * 48 + j:li * 48 + j + 1], MS[:, s, jj, kc, :], SCb[:, kc:kc + 1], kc == 0, kc == 7,
                           [b_MS[s], b_SCb], [b_PST[0]], signal=(kc == 7))
        for li in range(L):
            TT(P.dve, MOD[:, li, :], PST[:, 0, li * 48:(li + 1) * 48], PV[:, li, PV_MB:PV_MB + 48], ALU.add, [b_PST[0], b_PV], [b_MOD])
        stage('s_mod')
        DMA(P.pool, "ci", identh[:], ident_d, writes=[b_identh])
        DMA(P.pool, "cp", PWb[:], pwbd_d.rearrange("l p c n -> p l c n"), writes=[b_PWb])
        stage('s_casts')
        for li in range(L):
            STT(DV[:, li, 0:8], MOD[:, li, 8:16], 1.0, PV[:, li, PV_PRE:PV_PRE + 8], ALU.add, ALU.mult, [b_MOD, b_PV], [b_DV])
            TT(P.dve, DV[:, li, 8:16], MOD[:, li, 16:24], PV[:, li, PV_POST:PV_POST + 8], ALU.mult, [b_MOD, b_PV], [b_DV])
            STT(DV[:, li, 16:24], MOD[:, li, 32:40], 1.0, PV[:, li, PV_FPRE:PV_FPRE + 8], ALU.add, ALU.mult, [b_MOD, b_PV], [b_DV])
            TT(P.dve, DV[:, li, 24:32], MOD[:, li, 40:48], PV[:, li, PV_FPOST:PV_FPOST + 8], ALU.mult, [b_MOD, b_PV], [b_DV])
            TS(DV[:, li, 32:38], PV[:, li, PV_CNG:PV_CNG + 6], 0.5, None, ALU.mult, None, [b_PV], [b_DV])
        stage('s_derived')
        WSF = TMP[:].rearrange("p a t -> p (a t)")[:, 0:768].rearrange("p (h t) -> p h t", h=6)
        b_WSF = P.buf("WSF")
        b_WSF.al = tuple(it[1] for it in r_TMP.items)
        for it in r_TMP.items:
            it[1].al = (b_WSF,)
        SBB = SS[:].rearrange("p a t -> p (a t)")[:, 0:384].rearrange("p (c t) -> p c t", c=3)
        b_SBB = P.buf("SBB")
        b_SBB.al = tuple(it[1] for it in r_SS.items)
        for it in r_SS.items:
            it[1].al = (b_SBB,)
        for li in range(L):
            DMA(P.sp, "wsf", WSF, wst_d[li], writes=[b_WSF])
            for c in range(3):
                DMA(P.sp, "sbb", SBB[0:64, c, :], sgub_d[li, 2 * c:2 * c + 1, :].broadcast_to([64, 128]), writes=[b_SBB])
                DMA(P.sp, "sbb", SBB[64:128, c, :], sgub_d[li, 2 * c + 1:2 * c + 2, :].broadcast_to([64, 128]), writes=[b_SBB])
            TT(P.dve, WSF, WSF, TRI[:].unsqueeze(1).broadcast_to([128, 6, 128]), ALU.mult, [b_WSF, b_TRI], [b_WSF])
            CP(P.dve, WST[:, li], WSF, [b_WSF], [b_WSTl[li]])
            for h in range(6):
                MM(PSG[:, h // 4, (h % 4) * 128:(h % 4 + 1) * 128], onesf[:], WSF[:, h, :], True, True, [b_onesf, b_WSF], [b_PSG[h // 4]])
            for c in range(3):
                for hh in range(2):
                    h = 2 * c + hh
                    sl = slice(64 * hh, 64 * hh + 64)
                    STT(CC[sl, li, c, :], PSG[sl, h // 4, (h % 4) * 128:(h % 4 + 1) * 128], PV[sl, li, PV_SB + c:PV_SB + c + 1], SBB[sl, c, :],
                        ALU.mult, ALU.add, [b_PSG[h // 4], b_PV, b_SBB], [b_CCl[li]])
        stage('s_sgu')

        stream = []
        for ti in range(ntiles):
            for li in range(L):
                for j in range(14):
                    stream.append(("A", win_d[li, j], win_b[li, j], b_win[li][j], ti == 0))
                for m in range(8):
                    stream.append(("A", wout_d[li, m], wout_b[li, m], b_wout[li][m], ti == 0))
                for j in range(NJ):
                    stream.append(("U", up_d[li, j], up_b[li, j], b_up[li][j], ti == 0))
                order = [("D", m) for m in range(6)] + [("E", m) for m in range(6)] + [("D", 6), ("E", 6), ("D", 7), ("E", 7)]
                for r_, m in order:
                    if r_ == "D":
                        stream.append(("D", dn_d[li, m, :, 0:16, :], dn_b[li, m, :, 0:16, :], b_dn[li][m], ti == 0))
                    else:
                        stream.append(("E", dn_d[li, m, :, 16:NJ, :], dn_b[li, m, :, 16:NJ, :], b_dne[li][m], ti == 0))
        rings = {"A": (WA, b_WA, 4), "U": (WU, b_WU, 3), "D": (WD, b_WD, 2), "E": (WE, b_WE, 2)}
        issued = {"A": 0, "U": 0, "D": 0, "E": 0}
        consumed = {"A": 0, "U": 0, "D": 0, "E": 0}
        nxt = [0]

        def pump():
            while nxt[0] < len(stream):
                r, src32, scr, sb, first = stream[nxt[0]]
                ten, bl, ns = rings[r]
                if issued[r] - consumed[r] >= ns:
                    break
                s = issued[r] % ns
                if first:
                    DMA(P.pool, "wc%s%d" % (r, s), ten[:, s], src32, writes=[bl[s]])
                    if ntiles > 1:
                        DMA(P.sp, "ws%s%d" % (r, s), scr, ten[:, s], reads=[bl[s]], writes=[sb])
                else:
                    DMA(P.sp, "w%s%d" % (r, s), ten[:, s], scr, reads=[sb], writes=[bl[s]])
                issued[r] += 1
                nxt[0] += 1

        def wslot(r):
            ten, bl, ns = rings[r]
            assert consumed[r] < issued[r], "weight chunk not issued"
            s = consumed[r] % ns
            return ten[:, s], bl[s]

        def wdone(r):
            consumed[r] += 1
            pump()

        sc_i = [0]

        def preload(func):
            ACT(DUM[:, 1:2], DUM[:, 0:1], func, [b_DUM], [b_DUM])

        def tm_sum(chunks, fp32=False):
            col = 4 * (sc_i[0] % 16)
            sc_i[0] += 1
            oc = onesf[:, 0:1] if fp32 else onesb[:, 0:1]
            bo = b_onesf if fp32 else b_onesb
            n = len(chunks)
            for tb in range(4):
                for i, (ap, b) in enumerate(chunks):
                    MM(PST[:, 0, col + tb:col + tb + 1], ap[:, tb * 128:(tb + 1) * 128], oc, i == 0, i == n - 1, [b, bo], [b_PST[0]],
                       signal=(tb == 3 and i == n - 1))
            return col

        def small(col, scale, bias):
            a, ba = r_S4.get()
            ACT(a, PST[:, 0, col:col + 4], AF.Identity, [b_PST[0]], [ba], scale=scale, bias=bias)
            return a, ba

        def rpow(a, ba):
            r, br = r_S4.get()
            TT(P.pool, r, a, NH4[:], ALU.pow, [ba, b_NH4], [br])
            return r, br

        def bcast(a, ba):
            d4, bd4 = r_D4.get()
            TT(P.dve, d4, identf[:].unsqueeze(1).broadcast_to([128, 4, 128]), a.unsqueeze(2).broadcast_to([128, 4, 128]), ALU.mult,
               [b_identf, ba], [bd4])
            pb = gbank()
            MM(PSG[:, pb, :], onesf[:], d4.rearrange("p a b -> p (a b)"), True, True, [b_onesf, bd4], [b_PSG[pb]])
            return PSG[:, pb, :], b_PSG[pb]

        def rstd_of(chunks, scale, eps):
            n = len(chunks)
            for i, (ap, b) in enumerate(chunks):
                MM(PST[:, 1, :], onesb[:], ap, i == 0, i == n - 1, [b_onesb, b], [b_PST[1]], signal=(i == n - 1))
            r, br = r_SS.get()
            ACT(r, PST[:, 1, :], AF.Ln, [b_PST[1]], [br], scale=scale, bias=eps)
            ACT(r, r, AF.Exp, [br], [br], scale=-0.5)
            return r, br

        def prenorm(li, gcol, shcol, have_sq, xa=False, nxt=None):
            SQ8 = YT
            X, bX = (XA, b_XA) if xa else (XT, b_XT)
            if not have_sq:
                for c in range(8):
                    ACT(SQ8[:, c, :], X[:, c, :], AF.Square, [bX[c]], [b_YT[c]])
            r, br = rstd_of([(SQ8[:, c, :], b_YT[c]) for c in range(8)], 1.0 / D_MODEL, EPS)
            for c in range(8):
                t, bt = r_TMP.get()
                TT(P.dve, t, X[:, c, :], r, ALU.mult, [bX[c], br], [bt])
                ACT(HT[:, c, :], t, AF.Identity, [bt, b_DV, b_MOD], [b_HT[c]],
                    scale=DV[:, li, gcol + c:gcol + c + 1], bias=MOD[:, li, shcol + c:shcol + c + 1])
            if xa:
                for c in range(8):
                    ACT(XT[:, c, :], XA[:, c, :], AF.Copy, [b_XA[c]], [b_XT[c]])

        def postnorm_residual(li, ggcol, eps, produce, sq_next):
            SQ8 = HT
            for m in range(8):
                pb = produce(m)
                CP(P.dve, Y32[:, m, :], PSG[:, pb, :], [b_PSG[pb]], [b_Y32[m]])
                ACT(SQ8[:, m, :], Y32[:, m, :], AF.Square, [b_Y32[m]], [b_HT[m]])
            r, br = rstd_of([(SQ8[:, m, :], b_HT[m]) for m in range(8)], 1.0 / D_MODEL, eps)
            for m in range(8):
                t, bt = r_TMP.get()
                STT(t, Y32[:, m, :], DV[:, li, ggcol + m:ggcol + m + 1], r, ALU.mult, ALU.mult, [b_Y32[m], b_DV, br], [bt])
                TT(P.dve, XT[:, m, :], XT[:, m, :], t, ALU.add, [b_XT[m], bt], [b_XT[m]])
                if sq_next:
                    ACT(YT[:, m, :], XT[:, m, :], AF.Square, [b_XT[m]], [b_YT[m]])

        def mixer(ti, li, have_sq, xa=False):
            prenorm(li, 0, 0, have_sq, xa, AF.Tanh)
            dump("h1", HT[:], b_HT, [128, 8, T])
            hcl = HC[:, li]
            zcl = ZC[:, li]
            stage('m_prenorm')
            if ti > 0:
                ACT(hcl[:, :, 0:30], hcl[:, :, T:T + 30], AF.Copy, b_HC[li], b_HC[li])
                ACT(zcl[:, :, 0:15], zcl[:, :, T:T + 15], AF.Copy, b_ZC[li], b_ZC[li])
            def inproj_fm():
                w, bw = wslot("A")
                pb = gbank()
                for kc in range(8):
                    MM(PSG[:, pb, :], w[:, kc, :], HT[:, kc, :], kc == 0, kc == 7, [bw, b_HT[kc]], [b_PSG[pb]], signal=(kc == 7))
                wdone("A")
                return pb

            csq = []

            def conv_chunk(c):
                dg, bdg = dgs[c]
                pb = gbank()
                for k in range(31):
                    MM(PSG[:, pb, :], dg[:, k, :], hcl[:, c, k:k + T], k == 0, k == 30, [bdg, b_HC[li][c]], [b_PSG[pb]], signal=(k == 30))
                ACT(Y32[:, 3 + c, :], PSG[:, pb, :], AF.Identity, [b_PSG[pb], b_PV], [b_Y32[3 + c]], bias=PV[:, li, PV_CB + c:PV_CB + c + 1])
                q, bq = r_SQ.get()
                ACT(q, Y32[:, 3 + c, :], AF.Square, [b_Y32[3 + c]], [bq])
                csq.append((q, bq))

            def v_path_and_pool():
                def pool_sums():
                    TT(P.dve, PA[:, :, 1:527], zcl[:, :, 1:527], zcl[:, :, 0:526], ALU.add, b_ZC[li], [b_PA])
                    TT(P.dve, PB[64:128, 0, 3:527], PA[64:128, 0, 3:527], PA[64:128, 0, 1:525], ALU.add, [b_PA], [b_PB])
                    TT(P.dve, PB[:, 1, 3:527], PA[:, 1, 3:527], PA[:, 1, 1:525], ALU.add, [b_PA], [b_PB])
                    TT(P.dve, PA[:, 1, 7:527], PB[:, 1, 7:527], PB[:, 1, 3:523], ALU.add, [b_PB, b_PA], [b_PA])
                    TT(P.dve, PB[64:128, 1, 15:527], PA[64:128, 1, 15:527], PA[64:128, 1, 7:519], ALU.add, [b_PA, b_PB], [b_PB])
                st = VST[:]
                for tb in range(4):
                    v, bv = vvs[tb]
                    v3 = v.rearrange("p (h d) -> p h d", h=6)
                    P.op(P.dve, lambda e, v3=v3, tb=tb: e.tensor_reduce(out=st[:, 0, tb * 6:tb * 6 + 6], in_=v3, axis=AX.X, op=ALU.add), [bv], [b_VST])
                    ACT(VSQ[:], v, AF.Square, [bv], [b_VSQ])
                    P.op(P.dve, lambda e, tb=tb: e.tensor_reduce(out=st[:, 1, tb * 6:tb * 6 + 6], in_=VSQ[:].rearrange("p (h d) -> p h d", h=6), axis=AX.X,
                                                                  op=ALU.add), [b_VSQ], [b_VST])
                TS(st[:, 2, :], st[:, 0, :], 1.0 / 64, None, ALU.mult, None, [b_VST], [b_VST])
                TT(P.dve, st[:, 3, :], st[:, 2, :], st[:, 2, :], ALU.mult, [b_VST], [b_VST])
                STT(st[:, 3, :], st[:, 1, :], 1.0 / 64, st[:, 3, :], ALU.mult, ALU.subtract, [b_VST], [b_VST])
                TS(st[:, 3, :], st[:, 3, :], EPS, None, ALU.add, None, [b_VST], [b_VST])
                TT(P.pool, st[:, 4, :], st[:, 3, :], NH24[:], ALU.pow, [b_VST, b_NH24], [b_VST2])
                pool_sums()
                TT(P.dve, st[:, 0, :], st[:, 2, :], st[:, 4, :], ALU.mult, [b_VST, b_VST2], [b_VST3])
                TS(st[:, 0, :], st[:, 0, :], -1.0, None, ALU.mult, None, [b_VST3], [b_VST3])
                for tb in range(4):
                    v, bv = vvs[tb]
                    for h in range(6):
                        ACT(VN[:, tb, h * 64:(h + 1) * 64], v[:, h * 64:(h + 1) * 64], AF.Identity, [bv, b_VST2, b_VST3], [b_VN[tb]],
                            scale=st[:, 4, tb * 6 + h:tb * 6 + h + 1], bias=st[:, 0, tb * 6 + h:tb * 6 + h + 1])
                stage('m_v')
                srcs = [(PA, 0, 0), (PB, 0, 1), (PA, 1, 0), (PB, 1, 1)]
                for g in range(4):
                    ten, c, hh = srcs[g]
                    sl = slice(64 * hh, 64 * hh + 64)
                    STT(DD[sl, c, :], ten[sl, c, 15:527], INVW[sl, c:c + 1], zcl[sl, c, 15:527], ALU.mult, ALU.subtract,
                        [b_PA, b_PB, b_INVW, b_ZC[li][c]], [b_DD[c]])
                    if ti == 0:
                        t, bt = r_TMP.get()
                        TT(P.dve, t[sl, 0:16], ten[sl, c, 15:31], IC0[sl, c, :], ALU.mult, [b_PA, b_PB, b_IC0], [bt])
                        TT(P.dve, DD[sl, c, 0:16], t[sl, 0:16], zcl[sl, c, 15:31], ALU.subtract, [bt, b_ZC[li][c]], [b_DD[c]])

            def pool_mm():
                for c in range(2):
                    pb = gbank()
                    MM(PSG[:, pb, :], PWb[:, li, c, :], DD[:, c, :], True, True, [b_PWb, b_DD[c]], [b_PSG[pb]])
                    ACT(Y32[:, 6 + c, :], PSG[:, pb, :], AF.Identity, [b_PSG[pb], b_PV], [b_Y32[6 + c]], scale=PV[:, li, PV_PS + c:PV_PS + c + 1])
                dump("yc", Y32[:, 6:8, :], b_Y32[6:8], [128, 2, T])
                stage('m_pool')

            def sgu_mix():
                for c in range(3):
                    pb = gbank()
                    for tb in range(4):
                        for hh in range(2):
                            h = 2 * c + hh
                            MM(PSG[64 * hh:64 * hh + 64, pb, tb * 128:(tb + 1) * 128], VN[:, tb, h * 64:(h + 1) * 64], WST[:, li, h, :], True, True,
                               [b_VN[tb], b_WSTl[li]], [b_PSG[pb]], signal=(tb == 3 and hh == 1))
                    t, bt = r_TMP.get()
                    STT(t.rearrange("p (b t) -> p b t", b=4), PSG[:, pb, :].rearrange("p (b t) -> p b t", b=4), PV[:, li, PV_SG + c:PV_SG + c + 1],
                        CC[:, li, c, :].unsqueeze(1).broadcast_to([128, 4, 128]), ALU.mult, ALU.add, [b_PSG[pb], b_PV, b_CCl[li]], [bt])
                    TT(P.dve, Y32[:, c, :], UU[:, c, :], t, ALU.mult, [b_UU[c], bt], [b_Y32[c]])
                dump("ya", Y32[:, 0:3, :], b_Y32[0:3], [128, 3, T])
                stage('m_sgu')

            for c in range(3):
                pg = inproj_fm()
                th, bth = r_TH.get()
                ACT(th, PSG[:, pg, :], AF.Tanh, [b_PSG[pg]], [bth], scale=0.5)
                pa = inproj_fm()
                STT(hcl[:, c, 30:30 + T], th, 1.0, PSG[:, pa, :], ALU.add, ALU.mult, [bth, b_PSG[pa]], [b_HC[li][c]])
            stage('m_glu')
            dgs = []
            for c in range(3):
                if ti == 0:
                    TT(P.dve, DG[:, c], identh[:].unsqueeze(1).broadcast_to([128, 31, 128]),
                       PV[:, li, PV_CW + 31 * c:PV_CW + 31 * c + 31].unsqueeze(2).broadcast_to([128, 31, 128]), ALU.mult,
                       [b_identh, b_PV], [b_DG[c]])
                    if ntiles > 1:
                        DMA(P.sp, "dgw%d" % c, dg_b[li, c], DG[:, c], reads=[b_DG[c]], writes=[b_dgb[li][c]])
                else:
                    DMA(P.sp, "dg%d" % c, DG[:, c], dg_b[li, c], reads=[b_dgb[li][c]], writes=[b_DG[c]])
                dgs.append((DG[:, c], b_DG[c]))
            vb = [gbank() for _ in range(4)]
            for jv in range(3):
                w, bw = wslot("A")
                for tb in range(4):
                    for kc in range(8):
                        MM(PSG[:, vb[tb], jv * 128:(jv + 1) * 128], HT[:, kc, tb * 128:(tb + 1) * 128], w[:, kc, :], kc == 0, kc == 7,
                           [bw, b_HT[kc]], [b_PSG[vb[tb]]], signal=(kc == 7))
                wdone("A")
            vvs = []
            for tb in range(4):
                v = VV4[:, tb, :]
                bv = b_VV4[tb]
                ACT(v, PSG[:, vb[tb], 0:384], AF.Gelu, [b_PSG[vb[tb]]], [bv])
                vvs.append((v, bv))
            for c in range(2):
                pb = inproj_fm()
                ACT(zcl[:, c, 15:15 + T], PSG[:, pb, :], AF.Copy, [b_PSG[pb]], [b_ZC[li][c]])
            stage('m_inproj')
            conv_chunk(0)
            v_path_and_pool()
            conv_chunk(1)
            conv_chunk(2)
            for c in range(3):
                pb = inproj_fm()
                ACT(UU[:, c, :], PSG[:, pb, :], AF.Gelu, [b_PSG[pb]], [b_UU[c]])
            stage('m_u')
            colm = tm_sum([(Y32[:, 3 + c, :], b_Y32[3 + c]) for c in range(3)], fp32=True)
            colq = tm_sum(csq)
            pool_mm()
            sgu_mix()
            mean4, bmean4 = small(colm, 1.0 / 384, 0.0)
            var4, bvar4 = small(colq, 1.0 / 384, EPS)
            m24, bm24 = r_S4.get()
            TT(P.dve, m24, mean4, mean4, ALU.mult, [bmean4], [bm24])
            TT(P.dve, var4, var4, m24, ALU.subtract, [bvar4, bm24], [bvar4])
            rs4, brs4 = rpow(var4, bvar4)
            mean, bmean = bcast(mean4, bmean4)
            rs, brs = bcast(rs4, brs4)

            def branch_norm(c0, c1, n, eps):
                sqs = []
                for c in range(c0, c1):
                    q, bq = r_SQ.get()
                    ACT(q, Y32[:, c, :], AF.Square, [b_Y32[c]], [bq])
                    sqs.append((q, bq))
                r, br = rstd_of(sqs, 1.0 / n, eps)
                for c in range(c0, c1):
                    STT(YT[:, c, :], Y32[:, c, :], PV[:, li, PV_BR + c:PV_BR + c + 1], r, ALU.mult, ALU.mult, [b_Y32[c], b_PV, br], [b_YT[c]])
            for c in range(3):
                t, bt = r_TMP.get()
                TT(P.dve, t, Y32[:, 3 + c, :], mean, ALU.subtract, [b_Y32[3 + c], bmean], [bt])
                TT(P.dve, t, t, rs, ALU.mult, [bt, brs], [bt])
                th, bth = r_TH.get()
                ACT(th, t, AF.Tanh, [bt, b_DV], [bth], scale=DV[:, li, 32 + c:33 + c], bias=DV[:, li, 35 + c:36 + c])
                l_, bl_ = r_SS.get()
                ACT(l_, t, AF.Identity, [bt, b_PV], [bl_], scale=PV[:, li, PV_CNG + c:PV_CNG + c + 1], bias=PV[:, li, PV_CNB + c:PV_CNB + c + 1])
                STT(Y32[:, 3 + c, :], th, 1.0, l_, ALU.add, ALU.mult, [bth, bl_], [b_Y32[3 + c]])
            dump("yb", Y32[:, 3:6, :], b_Y32[3:6], [128, 3, T])
            stage('m_conv')
            branch_norm(3, 6, 384, 4 * EPS)
            branch_norm(6, 8, 256, EPS)
            branch_norm(0, 3, 384, EPS)
            dump("yT", YT[:], b_YT, [128, 8, T])
            stage('m_brnorm')

            def produce(m):
                w, bw = wslot("A")
                pb = gbank()
                for kc in range(8):
                    MM(PSG[:, pb, :], w[:, kc, :], YT[:, kc, :], kc == 0, kc == 7, [bw, b_YT[kc]], [b_PSG[pb]], signal=(kc == 7))
                wdone("A")
                return pb
            postnorm_residual(li, 8, EPS, produce, True)
            stage('m_out')
            dump("x1", XT[:], b_XT, [128, 8, T])

        def ffn(ti, li, last):
            prenorm(li, 16, 24, True, False, AF.Gelu)
            dump("h2", HT[:], b_HT, [128, 8, T])
            ztl = ZT[:, li]
            fw = lambda k, j: PV[:, li, PV_FW + 44 * k + j:PV_FW + 44 * k + j + 1]
            fwv = lambda k: PV[:, li, PV_FW + 44 * k:PV_FW + 44 * k + 44].rearrange("p (g j) -> p j g", g=2)
            TT(P.dve, CORR[:, :, :, 0], ztl[:, :, :, 1], fwv(1), ALU.mult, [b_ZT[li], b_PV], [b_CORR])
            TT(P.dve, CT[:, :, :, 0], ztl[:, :, :, 0], fwv(0), ALU.mult, [b_ZT[li], b_PV], [b_CT])
            TT(P.dve, CORR[:, :, :, 0], CORR[:, :, :, 0], CT[:, :, :, 0], ALU.add, [b_CORR, b_CT], [b_CORR])
            TT(P.dve, CORR[:, :, :, 1], ztl[:, :, :, 1], fwv(0), ALU.mult, [b_ZT[li], b_PV], [b_CORR])
            def fin_act(p):
                gl, bgl = r_GL.get()
                ACT(gl, p[1][:, 0, :], AF.Gelu, [p[2]], [bgl])
                return gl, bgl

            def fin_dve(p, glp):
                TT(P.dve, AT[:, p[0], :], glp[0], p[1][:, 1, :], ALU.mult, [glp[1], p[2]], [b_AT[p[0]]])
            pend = None
            for j in range(NJ):
                w, bw = wslot("U")
                p0 = gpair()
                for g in range(2):
                    for kc in range(8):
                        MM(PSG[:, p0 + g, :], w[:, kc, g * 128:(g + 1) * 128], HT[:, kc, :], kc == 0, kc == 7, [bw, b_HT[kc]], [b_PSG[p0 + g]],
                           signal=(kc == 7))
                wdone("U")
                acc, bacc = r_ACC.get()
                pbs = [b_PSG[p0], b_PSG[p0 + 1]]
                for g in range(2):
                    jj = j + 22 * g
                    ACT(acc[:, g, :], PSG[:, p0 + g, :], AF.Identity, [b_PSG[p0 + g], b_PV], [bacc], scale=fw(2, jj),
                        bias=PV[:, li, PV_FB + jj:PV_FB + jj + 1])
                if pend is not None:
                    glp = fin_act(pend)
                for g in range(2):
                    jj = j + 22 * g
                    STT(acc[:, g, 1:T], PSG[:, p0 + g, 0:T - 1], fw(1, jj), acc[:, g, 1:T], ALU.mult, ALU.add, [b_PSG[p0 + g], b_PV, bacc], [bacc])
                    STT(acc[:, g, 2:T], PSG[:, p0 + g, 0:T - 2], fw(0, jj), acc[:, g, 2:T], ALU.mult, ALU.add, [b_PSG[p0 + g], b_PV, bacc], [bacc])
                CP(P.dve, ztl[:, j, :, :], PSG[:, p0:p0 + 2, T - 2:T], pbs, [b_ZT[li]])
                TT(P.dve, acc[:, :, 0:2], acc[:, :, 0:2], CORR[:, j, :, :], ALU.add, [bacc, b_CORR], [bacc])
                if pend is not None:
                    fin_dve(pend, glp)
                pend = (j, acc, bacc)
            glp = fin_act(pend)
            fin_dve(pend, glp)
            stage('f_up')
            dump("aT", AT[:, 0:4, :], b_AT[0:4], [128, 4, T])

            first_pair = pair_i[0] % 3
            dbanks = [(2 * ((first_pair + i // 2) % 3) + i % 2) for i in range(6)]
            for m in range(6):
                w, bw = wslot("D")
                for kc in range(16):
                    MM(PSG[:, dbanks[m], :], w[:, kc, :], AT[:, kc, :], kc == 0, False, [bw, b_AT[kc]], [b_PSG[dbanks[m]]], signal=(kc == 15))
                wdone("D")

            def produce(m):
                if m < 6:
                    pb = dbanks[m]
                else:
                    pb = dbanks[m - 6]
                    w, bw = wslot("D")
                    for kc in range(16):
                        MM(PSG[:, pb, :], w[:, kc, :], AT[:, kc, :], kc == 0, False, [bw, b_AT[kc]], [b_PSG[pb]], signal=(kc == 15))
                    wdone("D")
                w, bw = wslot("E")
                for kc in range(16, NJ):
                    MM(PSG[:, pb, :], w[:, kc - 16, :], AT[:, kc, :], False, kc == NJ - 1, [bw, b_AT[kc]], [b_PSG[pb]], signal=(kc == NJ - 1))
                wdone("E")
                return pb
            postnorm_residual(li, 24, EPS, produce, not last)
            dump("x2", XT[:], b_XT, [128, 8, T])

        pump()
        for ti in range(ntiles):
            if ti == 0:
                DMA(P.sp, "xin", XT[:], x_fm[ti], writes=b_XT)
            for li in range(L):
                mixer(ti, li, li > 0, xa=(ti > 0 and li == 0))
                if li == L - 1 and ti + 1 < ntiles:
                    DMA(P.sp, "xin", XA, x_fm[ti + 1], writes=b_XA)
                ffn(ti, li, li == L - 1)
            DMA(P.sp, "xout", out_fm[ti], XT[:], reads=b_XT)

    except _Stop:
        pass
    P.barrier()
    print('sbuf bytes remaining', nc.sbuf_bytes_remaining)
    P.emit()
    return nc, dbg_out


def _fm(v, n):
    return np.ascontiguousarray(np.asarray(v, np.float32).reshape(n, 128).T)


def prep_shared(inp, layers):
    g = lambda k: np.asarray(inp[k], np.float32)
    pv, modc, winc, woutc, upc, dnc, wst, sgub, pwbd = [], [], [], [], [], [], [], [], []
    for l in layers:
        cols = [_fm(g("mix_pre_g")[l], 8), _fm(g("mix_post_g")[l], 8), _fm(g("branch_g")[l], 8), _fm(g("ffn_pre_g")[l], 8),
                _fm(g("ffn_post_g")[l], 8), _fm(g("sgu_norm_g")[l], 3), _fm(g("sgu_norm_b")[l], 3), _fm(g("conv_b")[l], 3),
                _fm(g("conv_norm_g")[l], 3), _fm(g("conv_norm_b")[l], 3), _fm(g("pool_scale")[l], 2)]
        cw = g("conv_w")[l].reshape(31, 3, 128).transpose(2, 1, 0).reshape(128, 93)
        fw = g("ffn_conv_w")[l].reshape(3, 44, 128).transpose(2, 0, 1).reshape(128, 132)
        cols += [cw, fw, _fm(g("ffn_conv_b")[l], 44), _fm(g("mod_b")[l], 48)]
        p = np.concatenate(cols, axis=1)
        assert p.shape == (128, NPV)
        pv.append(p)
        modc.append(g("mod_w")[l].reshape(8, 128, 12, 4, 128).transpose(2, 1, 3, 0, 4))
        winc.append(g("w_in")[l].reshape(8, 128, 14, 128).transpose(2, 1, 0, 3)[IN_ORDER])
        woutc.append(g("w_out")[l].reshape(8, 128, 8, 128).transpose(2, 1, 0, 3))
        upc.append(g("ffn_up")[l].reshape(8, 128, 2, NJ, 128).transpose(3, 1, 0, 2, 4).reshape(NJ, 128, 8, 256))
        dnc.append(g("ffn_down")[l].reshape(NJ, 128, 8, 128).transpose(2, 1, 0, 3))
        wst.append(g("sgu_w")[l].transpose(2, 0, 1))
        sgub.append(g("sgu_b")[l])
        pw = g("pool_w")[l]
        bd = np.zeros((128, 2, 128), np.float32)
        for c in range(2):
            bd[0:64, c, 0:64] = pw[2 * c]
            bd[64:128, c, 64:128] = pw[2 * c + 1]
        pwbd.append(bd)
    ca = lambda lst: np.ascontiguousarray(np.stack(lst, 0), dtype=np.float32)
    s_idx = np.arange(128)
    tri = (s_idx[None, :] >= s_idx[:, None]).astype(np.float32)
    wins = np.array([2, 4, 8, 16], np.float32)
    ic0 = np.zeros((128, 2, 16), np.float32)
    invw = np.zeros((128, 2), np.float32)
    for c in range(2):
        for hh in range(2):
            w_ = wins[2 * c + hh]
            ic0[64 * hh:64 * hh + 64, c, :] = 1.0 / np.minimum(np.arange(1, 17, dtype=np.float32), w_)
            invw[64 * hh:64 * hh + 64, c] = 1.0 / w_
    return {"pv": ca(pv), "mod_c": ca(modc), "w_in_c": ca(winc), "w_out_c": ca(woutc), "up_c": ca(upc), "down_c": ca(dnc),
            "wst": ca(wst), "sgub": ca(sgub), "pwbd": ca(pwbd), "identh": (0.5 * np.eye(128)).astype(np.float32), "identf": np.eye(128, dtype=np.float32), "tri": tri,
            "ic0": ic0, "invw": invw}


def x_to_fm(xb, ntiles=NT):
    return np.ascontiguousarray(np.asarray(xb, np.float32).reshape(ntiles, T, 8, 128).transpose(0, 3, 2, 1))


def fm_to_x(o):
    nt = o.shape[0]
    return np.ascontiguousarray(o.transpose(0, 3, 2, 1).reshape(nt * T, D_MODEL))


_CACHE = {}


def _get_prog(layers):
    key = tuple(layers)
    if key not in _CACHE:
        _CACHE[key] = build(list(range(len(layers))))[0]
    return _CACHE[key]


def run_layers(x, inputs, layers):
    shared = prep_shared(inputs, layers)
    c = np.asarray(inputs["c"], np.float32)
    nc = _get_prog(layers)
    in_maps = []
    for b in range(BATCH):
        m = dict(shared)
        m["x_fm"] = x_to_fm(x[b])
        m["c_fm"] = _fm(c[b], 8)
        in_maps.append(m)
    res = run_bass_kernel_spmd(nc, in_maps, core_ids=list(range(BATCH)))
    return np.stack([fm_to_x(res.results[b]["out_fm"]) for b in range(BATCH)], 0)


FUSED = True


def kernel(**inputs):
    x = np.asarray(inputs["x"], np.float32)
    if FUSED:
        out = run_layers(x, inputs, list(range(DEPTH)))
    else:
        out = x
        for l in range(DEPTH):
            out = run_layers(out, inputs, [l])
    return out.astype(np.float32)
```

```python
import contextlib
import numpy as np
import concourse.bass as bass
import concourse.mybir as mybir
from concourse.bass_utils import run_bass_kernel_spmd

F32 = mybir.dt.float32
BF16 = mybir.dt.bfloat16
AF = mybir.ActivationFunctionType
ALU = mybir.AluOpType
AX = mybir.AxisListType

D_MODEL = 1024
BATCH = 8
SEQ = 4096
DEPTH = 2
T = 512
NT = SEQ // T
D_FF = 2816
NJ = D_FF // 128
EPS = 1e-6
NPV = 374
IN_ORDER = [9, 6, 10, 7, 11, 8, 3, 4, 5, 12, 13, 0, 1, 2]

PV_PRE, PV_POST, PV_BR, PV_FPRE, PV_FPOST = 0, 8, 16, 24, 32
PV_SG, PV_SB, PV_CB, PV_CNG, PV_CNB, PV_PS = 40, 43, 46, 49, 52, 55
PV_CW, PV_FW, PV_FB, PV_MB = 57, 150, 282, 326


class Buf:
    __slots__ = ("name", "w", "r", "al", "excl")

    def __init__(self, name):
        self.name = name
        self.w = None
        self.r = []
        self.al = ()
        self.excl = False


class Eng:
    def __init__(self, name, attr):
        self.name = name
        self.attr = attr
        self.ops = []
        self.count = 0
        self.waited = {}


class Prog:
    def __init__(self, nc):
        self.nc = nc
        self.es = contextlib.ExitStack()
        self.pe = Eng("pe", "tensor")
        self.act = Eng("act", "scalar")
        self.dve = Eng("dve", "vector")
        self.pool = Eng("pool", "gpsimd")
        self.sp = Eng("sp", "sync")
        self.engs = [self.pe, self.act, self.dve, self.pool, self.sp]
        self.sems = {}
        for e in self.engs:
            self.sems[e.name] = self.es.enter_context(nc.semaphore("s_" + e.name))
        self.dma_counts = {}
        self.nbuf = 0

    def sbuf(self, name, shape, dt):
        return self.es.enter_context(self.nc.sbuf_tensor(name, list(shape), dt))

    def psum(self, name, shape, dt=F32):
        return self.es.enter_context(self.nc.psum_tensor(name, list(shape), dt))

    def buf(self, name=None):
        self.nbuf += 1
        return Buf(name or "b%d" % self.nbuf)

    def bufs(self, n, name=None):
        return [self.buf(None if name is None else "%s%d" % (name, i)) for i in range(n)]

    def dma_sem(self, name):
        key = "d_" + name
        self.sems[key] = self.es.enter_context(self.nc.semaphore(key))
        self.dma_counts[key] = 0
        return key

    def _collect(self, eng, reads, writes):
        waits = {}

        def need(tok, skip_same):
            if tok is None:
                return
            k, v = tok
            if skip_same and k == eng.name:
                return
            if eng.waited.get(k, 0) >= v:
                return
            if waits.get(k, 0) < v:
                waits[k] = v

        pe_same = eng.name == "pe"
        for b0 in reads:
            for b in (b0,) + tuple(b0.al):
                need(b.w, False)
                if b.excl:
                    for t in b.r:
                        need(t, True)
        for b0 in writes:
            for b in (b0,) + tuple(b0.al):
                need(b.w, pe_same)
                for t in b.r:
                    need(t, pe_same)
        for k, v in waits.items():
            eng.waited[k] = v
        return sorted(waits.items())

    def op(self, eng, fn, reads=(), writes=(), signal=True):
        waits = self._collect(eng, reads, writes)
        tok = (eng.name, eng.count + 1)
        for b in reads:
            b.r.append(tok)
        for b in writes:
            b.w = tok
            b.r = []
        inc = None
        if signal:
            eng.count += 1
            inc = (eng.name, 1)
        eng.ops.append((waits, fn, inc))

    def dma(self, eng, semkey, out, in_, reads=(), writes=(), **kw):
        waits = self._collect(eng, reads, writes)
        self.dma_counts[semkey] += 16
        tok = (semkey, self.dma_counts[semkey])
        for b in reads:
            b.r.append(tok)
        for b in writes:
            b.w = tok
            b.r = []

        def fn(e, out=out, in_=in_, kw=kw):
            return e.dma_start(out=out, in_=in_, **kw)
        eng.ops.append((waits, fn, (semkey, 16)))
        return tok

    def wait_tok(self, eng, tok):
        k, v = tok
        if eng.waited.get(k, 0) >= v:
            return
        eng.waited[k] = v
        eng.ops.append(([(k, v)], None, None))

    def barrier(self):
        toks = [(e.name, e.count) for e in self.engs if e.count > 0]
        toks += [(k, v) for k, v in self.dma_counts.items() if v > 0]
        for e in self.engs:
            for t in toks:
                if t[0] != e.name:
                    self.wait_tok(e, t)

    def emit(self):
        nc = self.nc
        with nc.Block() as block:
            for e in self.engs:
                if not e.ops:
                    continue

                def body(h, e=e):
                    for waits, fn, inc in e.ops:
                        for k, v in waits:
                            h.wait_ge(self.sems[k], v)
                        if fn is not None:
                            ins = fn(h)
                            if inc is not None:
                                ins.then_inc(self.sems[inc[0]], inc[1])
                getattr(block, e.attr)(body)
        self.es.close()


class Rot:
    def __init__(self, items):
        self.items = items
        self.i = 0

    def get(self):
        it = self.items[self.i % len(self.items)]
        self.i += 1
        return it


class _Stop(Exception):
    pass


def build(layers, ntiles=NT, dbg=(), stop_after=None):
    nc = bass.Bass("TRN2", target_bir_lowering=False)
    P = Prog(nc)
    L = len(layers)
    dbg = set(dbg)
    dbg_out = {}

    def din(name, shape, dt=F32):
        return nc.dram_tensor(name, list(shape), dt, kind="ExternalInput").ap()

    x_fm = din("x_fm", [ntiles, 128, 8, T])
    out_fm = nc.dram_tensor("out_fm", [ntiles, 128, 8, T], F32, kind="ExternalOutput").ap()
    c_fm = din("c_fm", [128, 8])
    pv_d = din("pv", [L, 128, NPV])
    mod_d = din("mod_c", [L, 12, 128, 4, 8, 128])
    win_d = din("w_in_c", [L, 14, 128, 8, 128])
    wout_d = din("w_out_c", [L, 8, 128, 8, 128])
    up_d = din("up_c", [L, NJ, 128, 8, 256])
    dn_d = din("down_c", [L, 8, 128, NJ, 128])
    wst_d = din("wst", [L, 128, 6, 128])
    sgub_d = din("sgub", [L, 6, 128])
    pwbd_d = din("pwbd", [L, 128, 2, 128])
    ident_d = din("identh", [128, 128])
    identf_d = din("identf", [128, 128])
    tri_d = din("tri", [128, 128])
    ic0_d = din("ic0", [128, 2, 16])
    invw_d = din("invw", [128, 2])

    def dscr(name, shape):
        return nc.dram_tensor(name, list(shape), BF16, kind="Internal").ap()
    win_b = dscr("w_in_b", [L, 14, 128, 8, 128])
    wout_b = dscr("w_out_b", [L, 8, 128, 8, 128])
    up_b = dscr("up_b", [L, NJ, 128, 8, 256])
    dn_b = dscr("down_b", [L, 8, 128, NJ, 128])
    b_win = [P.bufs(14) for _ in range(L)]; b_wout = [P.bufs(8) for _ in range(L)]
    b_up = [P.bufs(NJ) for _ in range(L)]; b_dn = [P.bufs(8) for _ in range(L)]; b_dne = [P.bufs(8) for _ in range(L)]
    dg_b = dscr("dg_b", [L, 3, 128, 31, 128]); b_dgb = [P.bufs(3, "dgb%d_" % l) for l in range(L)]

    PV = P.sbuf("PV", [128, L, NPV], F32); b_PV = P.buf("PV")
    MOD = P.sbuf("MOD", [128, L, 48], F32); b_MOD = P.buf("MOD")
    DV = P.sbuf("DV", [128, L, 40], F32); b_DV = P.buf("DV")
    CF = P.sbuf("CF", [128, 8], F32); b_CF = P.buf("CF")
    CF2 = P.sbuf("CF2", [128, 8], F32); b_CF2 = P.buf("CF2")
    SCb = P.sbuf("SCb", [128, 8], BF16); b_SCb = P.buf("SCb")
    WST = P.sbuf("WST", [128, L, 6, 128], BF16); b_WSTl = P.bufs(L, "WST")
    CC = P.sbuf("CC", [128, L, 3, 128], F32); b_CCl = P.bufs(L, "CC")
    PWb = P.sbuf("PWb", [128, L, 2, 128], BF16); b_PWb = P.buf("PWb")
    onesb = P.sbuf("onesb", [128, 128], BF16); b_onesb = P.buf("onesb")
    onesf = P.sbuf("onesf", [128, 128], F32); b_onesf = P.buf("onesf")
    identh = P.sbuf("identh_s", [128, 128], BF16); b_identh = P.buf("identh")
    identf = P.sbuf("identf_s", [128, 128], F32); b_identf = P.buf("identf")
    NH4 = P.sbuf("NH4", [128, 4], F32); b_NH4 = P.buf("NH4")
    DUM = P.sbuf("DUM", [128, 2], F32); b_DUM = P.buf("DUM")
    S4 = P.sbuf("S4", [128, 8, 4], F32); r_S4 = Rot([(S4[:, i, :], P.buf("S4_%d" % i)) for i in range(8)])
    D4 = P.sbuf("D4", [128, 1, 4, 128], F32); r_D4 = Rot([(D4[:, i], P.buf("D4_%d" % i)) for i in range(1)])
    IC0 = P.sbuf("IC0", [128, 2, 16], F32); b_IC0 = P.buf("IC0")
    INVW = P.sbuf("INVW", [128, 2], F32); b_INVW = P.buf("INVW")
    TRI = P.sbuf("TRI", [128, 128], F32); b_TRI = P.buf("TRI")
    HC = P.sbuf("HC", [128, L, 3, 30 + T], BF16); b_HC = [P.bufs(3) for _ in range(L)]
    ZC = P.sbuf("ZC", [128, L, 2, 15 + T], F32); b_ZC = [P.bufs(2) for _ in range(L)]
    ZT = P.sbuf("ZT", [128, L, NJ, 2, 2], F32); b_ZT = P.bufs(L)
    CORR = P.sbuf("CORR", [128, NJ, 2, 2], F32); b_CORR = P.buf("CORR")
    CT = P.sbuf("CT", [128, NJ, 2, 2], F32); b_CT = P.buf("CT")
    XT = P.sbuf("XT", [128, 8, T], F32); b_XT = P.bufs(8, "XT")
    HT = P.sbuf("HT", [128, 8, T], BF16); b_HT = P.bufs(8, "HT")
    Y32 = P.sbuf("Y32", [128, 8, T], F32); b_Y32 = P.bufs(8, "Y32")
    YT = P.sbuf("YT", [128, 8, T], BF16); b_YT = P.bufs(8, "YT")
    SQ = P.sbuf("SQ", [128, 3, T], BF16); r_SQ = Rot([(SQ[:, i, :], P.buf("SQ%d" % i)) for i in range(3)])
    TMP = P.sbuf("TMP", [128, 3, T], F32); r_TMP = Rot([(TMP[:, i, :], P.buf("TMP%d" % i)) for i in range(3)])
    SS = P.sbuf("SS", [128, 2, T], F32); r_SS = Rot([(SS[:, i, :], P.buf("SS%d" % i)) for i in range(2)])
    AR = P.sbuf("AR", [128, NJ * T], BF16)
    AT = AR[:].rearrange("p (j t) -> p j t", j=NJ); b_AT = P.bufs(NJ, "AT")
    ARf = AR[:].bitcast(F32)
    UU = ARf[:, 0:1536].rearrange("p (c t) -> p c t", c=3); b_UU = P.bufs(3, "UU")
    TH = ARf[:, 1536:2560].rearrange("p (c t) -> p c t", c=2); r_TH = Rot([(TH[:, i, :], P.buf("TH%d" % i)) for i in range(2)])
    PA = ARf[:, 2560:2560 + 2 * 527].rearrange("p (c t) -> p c t", c=2); b_PA = P.buf("PA")
    PB = ARf[:, 3616:3616 + 2 * 527].rearrange("p (c t) -> p c t", c=2); b_PB = P.buf("PB")
    DD = AR[:, 2 * 4672:2 * 4672 + 2 * T].rearrange("p (c t) -> p c t", c=2); b_DD = P.bufs(2, "DD")
    mix_scr = b_UU + [it[1] for it in r_TH.items] + [b_PA, b_PB] + b_DD
    for b in mix_scr:
        b.al = tuple(b_AT)
    for b in b_AT:
        b.al = tuple(mix_scr)
    VV4 = P.sbuf("VV4", [128, 4, 384], F32); b_VV4 = P.bufs(4, "VV4")
    VSQ = P.sbuf("VSQ", [128, 384], F32); b_VSQ = P.buf("VSQ")
    VST = P.sbuf("VST", [128, 5, 24], F32); b_VST = P.buf("VST"); b_VST2 = P.buf("VST2"); b_VST3 = P.buf("VST3")
    NH24 = P.sbuf("NH24", [128, 24], F32); b_NH24 = P.buf("NH24")
    VN = P.sbuf("VN", [128, 4, 384], BF16); b_VN = P.bufs(4, "VN")
    ACC = P.sbuf("ACC", [128, 5, 2, T], F32); r_ACC = Rot([(ACC[:, i], P.buf("ACC%d" % i)) for i in range(5)])
    GL = P.sbuf("GL", [128, 3, T], F32); r_GL = Rot([(GL[:, i, :], P.buf("GL%d" % i)) for i in range(3)])
    DG = P.sbuf("DG", [128, 3, 31, 128], BF16); b_DG = P.bufs(3, "DG")
    XA = DG[:].rearrange("p a k n -> p (a k n)").bitcast(F32)[:, 0:8 * T].rearrange("p (c t) -> p c t", c=8); b_XA = P.bufs(8, "XA")
    for b in b_XA:
        b.al = tuple(b_DG)
    for b in b_DG:
        b.al = tuple(b_XA)
    WA = P.sbuf("WA", [128, 4, 8, 128], BF16); b_WA = P.bufs(4, "WA")
    WU = P.sbuf("WU", [128, 3, 8, 256], BF16); b_WU = P.bufs(3, "WU")
    WD = P.sbuf("WD", [128, 2, 16, 128], BF16); b_WD = P.bufs(2, "WD")
    WE = P.sbuf("WE", [128, 2, NJ - 16, 128], BF16); b_WE = P.bufs(2, "WE")
    PSG = P.psum("PSG", [128, 6, T]); b_PSG = P.bufs(6, "PSG")
    PST = P.psum("PST", [128, 2, T]); b_PST = P.bufs(2, "PST")
    for b in b_PSG + b_PST:
        b.excl = True
    gen_i = [0]

    def gbank():
        i = gen_i[0] % 6
        gen_i[0] += 1
        return i

    pair_i = [0]

    def gpair():
        i = pair_i[0] % 3
        pair_i[0] += 1
        return 2 * i
    st_i = [0]

    def sbank():
        i = st_i[0] % 2
        st_i[0] += 1
        return i

    def ACT(out, in_, func, reads, writes, scale=1.0, bias=0.0):
        P.op(P.act, lambda e: e.activation(out=out, in_=in_, func=func, scale=scale, bias=bias), reads, writes)

    def TT(eng, out, in0, in1, op, reads, writes):
        P.op(eng, lambda e: e.tensor_tensor(out=out, in0=in0, in1=in1, op=op), reads, writes)

    def STT(out, in0, scalar, in1, op0, op1, reads, writes):
        P.op(P.dve, lambda e: e.scalar_tensor_tensor(out=out, in0=in0, scalar=scalar, in1=in1, op0=op0, op1=op1), reads, writes)

    def TS(out, in0, s1, s2, op0, op1, reads, writes):
        if s2 is None:
            P.op(P.dve, lambda e: e.tensor_scalar(out=out, in0=in0, scalar1=s1, scalar2=None, op0=op0), reads, writes)
        else:
            P.op(P.dve, lambda e: e.tensor_scalar(out=out, in0=in0, scalar1=s1, scalar2=s2, op0=op0, op1=op1), reads, writes)

    def CP(eng, out, in_, reads, writes):
        P.op(eng, lambda e: e.tensor_copy(out=out, in_=in_), reads, writes)

    def MM(out, lhsT, rhs, start, stop, reads, writes, signal=True):
        P.op(P.pe, lambda e: e.matmul(out, lhsT=lhsT, rhs=rhs, start=start, stop=stop), reads, writes, signal=signal)

    dsem = {}

    def DMA(eng, sem, out, in_, reads=(), writes=()):
        if sem not in dsem:
            dsem[sem] = P.dma_sem(sem)
        return P.dma(eng, dsem[sem], out, in_, reads, writes)

    def stage(name):
        if stop_after == name:
            raise _Stop()

    def dump(name, ap, bufs, shape):
        if name in dbg and name not in dbg_out:
            t = nc.dram_tensor("dbg_" + name, list(shape), ap.dtype, kind="ExternalOutput").ap()
            dbg_out[name] = t
            DMA(P.sp, "dbg_" + name, t, ap, reads=bufs)

    try:
        DMA(P.sp, "c0", PV[:], pv_d.rearrange("l p n -> p l n"), writes=[b_PV])
        DMA(P.sp, "c1", CF[:], c_fm, writes=[b_CF])
        DMA(P.sp, "c2", TRI[:], tri_d, writes=[b_TRI])
        DMA(P.sp, "c3", IC0[:], ic0_d, writes=[b_IC0])
        DMA(P.sp, "c4", INVW[:], invw_d, writes=[b_INVW])
        P.op(P.dve, lambda e: e.memset(onesb[:], 1.0), writes=[b_onesb])
        P.op(P.dve, lambda e: e.memset(onesf[:], 1.0), writes=[b_onesf])
        P.op(P.dve, lambda e: e.memset(NH4[:], -0.5), writes=[b_NH4])
        P.op(P.dve, lambda e: e.memset(DUM[:], 1.0), writes=[b_DUM])
        DMA(P.sp, "c5", identf[:], identf_d, writes=[b_identf])
        P.op(P.dve, lambda e: e.memset(NH24[:], -0.5), writes=[b_NH24])
        P.op(P.dve, lambda e: e.memset(HC[:].rearrange("p l c t -> p (l c t)"), 0.0), writes=[b for bl in b_HC for b in bl])
        P.op(P.dve, lambda e: e.memset(ZC[:].rearrange("p l c t -> p (l c t)"), 0.0), writes=[b for bl in b_ZC for b in bl])
        P.op(P.dve, lambda e: e.memset(ZT[:].rearrange("p l j a b -> p (l j a b)"), 0.0), writes=b_ZT)
        ACT(CF2[:], CF[:], AF.Tanh, [b_CF], [b_CF2], scale=0.5)
        STT(CF2[:], CF2[:], 1.0, CF[:], ALU.add, ALU.mult, [b_CF2, b_CF], [b_CF2])
        TS(SCb[:], CF2[:], 0.5, None, ALU.mult, None, [b_CF2], [b_SCb])
        stage('s_consts')
        MS = Y32[:].rearrange("p c t -> p (c t)").bitcast(BF16).rearrange("p (s j k n) -> p s j k n", s=2, j=4, k=8)
        b_MS = P.bufs(2, "MS")
        for b in b_MS:
            b.al = tuple(b_Y32)
        for b in b_Y32:
            b.al = tuple(b_MS)
        for li in range(L):
            for g in range(12):
                s = (li * 12 + g) % 2
                DMA(P.pool, "ms%d" % s, MS[:, s], mod_d[li, g], writes=[b_MS[s]])
                for jj in range(4):
                    j = g * 4 + jj
                    for kc in range(8):
                        MM(PST[:, 0, li * 48 + j:li * 48 + j + 1], MS[:, s, jj, kc, :], SCb[:, kc:kc + 1], kc == 0, kc == 7,
                           [b_MS[s], b_SCb], [b_PST[0]], signal=(kc == 7))
        for li in range(L):
            TT(P.dve, MOD[:, li, :], PST[:, 0, li * 48:(li + 1) * 48], PV[:, li, PV_MB:PV_MB + 48], ALU.add, [b_PST[0], b_PV], [b_MOD])
        stage('s_mod')
        DMA(P.pool, "ci", identh[:], ident_d, writes=[b_identh])
        DMA(P.pool, "cp", PWb[:], pwbd_d.rearrange("l p c n -> p l c n"), writes=[b_PWb])
        stage('s_casts')
        for li in range(L):
            STT(DV[:, li, 0:8], MOD[:, li, 8:16], 1.0, PV[:, li, PV_PRE:PV_PRE + 8], ALU.add, ALU.mult, [b_MOD, b_PV], [b_DV])
            TT(P.dve, DV[:, li, 8:16], MOD[:, li, 16:24], PV[:, li, PV_POST:PV_POST + 8], ALU.mult, [b_MOD, b_PV], [b_DV])
            STT(DV[:, li, 16:24], MOD[:, li, 32:40], 1.0, PV[:, li, PV_FPRE:PV_FPRE + 8], ALU.add, ALU.mult, [b_MOD, b_PV], [b_DV])
            TT(P.dve, DV[:, li, 24:32], MOD[:, li, 40:48], PV[:, li, PV_FPOST:PV_FPOST + 8], ALU.mult, [b_MOD, b_PV], [b_DV])
            TS(DV[:, li, 32:38], PV[:, li, PV_CNG:PV_CNG + 6], 0.5, None, ALU.mult, None, [b_PV], [b_DV])
        stage('s_derived')
        WSF = TMP[:].rearrange("p a t -> p (a t)")[:, 0:768].rearrange("p (h t) -> p h t", h=6)
        b_WSF = P.buf("WSF")
        b_WSF.al = tuple(it[1] for it in r_TMP.items)
        for it in r_TMP.items:
            it[1].al = (b_WSF,)
        SBB = SS[:].rearrange("p a t -> p (a t)")[:, 0:384].rearrange("p (c t) -> p c t", c=3)
        b_SBB = P.buf("SBB")
        b_SBB.al = tuple(it[1] for it in r_SS.items)
        for it in r_SS.items:
            it[1].al = (b_SBB,)
        for li in range(L):
            DMA(P.sp, "wsf", WSF, wst_d[li], writes=[b_WSF])
            for c in range(3):
                DMA(P.sp, "sbb", SBB[0:64, c, :], sgub_d[li, 2 * c:2 * c + 1, :].broadcast_to([64, 128]), writes=[b_SBB])
                DMA(P.sp, "sbb", SBB[64:128, c, :], sgub_d[li, 2 * c + 1:2 * c + 2, :].broadcast_to([64, 128]), writes=[b_SBB])
            TT(P.dve, WSF, WSF, TRI[:].unsqueeze(1).broadcast_to([128, 6, 128]), ALU.mult, [b_WSF, b_TRI], [b_WSF])
            CP(P.dve, WST[:, li], WSF, [b_WSF], [b_WSTl[li]])
            for h in range(6):
                MM(PSG[:, h // 4, (h % 4) * 128:(h % 4 + 1) * 128], onesf[:], WSF[:, h, :], True, True, [b_onesf, b_WSF], [b_PSG[h // 4]])
            for c in range(3):
                for hh in range(2):
                    h = 2 * c + hh
                    sl = slice(64 * hh, 64 * hh + 64)
                    STT(CC[sl, li, c, :], PSG[sl, h // 4, (h % 4) * 128:(h % 4 + 1) * 128], PV[sl, li, PV_SB + c:PV_SB + c + 1], SBB[sl, c, :],
                        ALU.mult, ALU.add, [b_PSG[h // 4], b_PV, b_SBB], [b_CCl[li]])
        stage('s_sgu')

        stream = []
        for ti in range(ntiles):
            for li in range(L):
                for j in range(14):
                    stream.append(("A", win_d[li, j], win_b[li, j], b_win[li][j], ti == 0))
                for m in range(8):
                    stream.append(("A", wout_d[li, m], wout_b[li, m], b_wout[li][m], ti == 0))
                for j in range(NJ):
                    stream.append(("U", up_d[li, j], up_b[li, j], b_up[li][j], ti == 0))
                order = [("D", m) for m in range(6)] + [("E", m) for m in range(6)] + [("D", 6), ("E", 6), ("D", 7), ("E", 7)]
                for r_, m in order:
                    if r_ == "D":
                        stream.append(("D", dn_d[li, m, :, 0:16, :], dn_b[li, m, :, 0:16, :], b_dn[li][m], ti == 0))
                    else:
                        stream.append(("E", dn_d[li, m, :, 16:NJ, :], dn_b[li, m, :, 16:NJ, :], b_dne[li][m], ti == 0))
        rings = {"A": (WA, b_WA, 4), "U": (WU, b_WU, 3), "D": (WD, b_WD, 2), "E": (WE, b_WE, 2)}
        issued = {"A": 0, "U": 0, "D": 0, "E": 0}
        consumed = {"A": 0, "U": 0, "D": 0, "E": 0}
        nxt = [0]

        def pump():
            while nxt[0] < len(stream):
                r, src32, scr, sb, first = stream[nxt[0]]
                ten, bl, ns = rings[r]
                if issued[r] - consumed[r] >= ns:
                    break
                s = issued[r] % ns
                if first:
                    DMA(P.pool, "wc%s%d" % (r, s), ten[:, s], src32, writes=[bl[s]])
                    if ntiles > 1:
                        DMA(P.sp, "ws%s%d" % (r, s), scr, ten[:, s], reads=[bl[s]], writes=[sb])
                else:
                    DMA(P.sp, "w%s%d" % (r, s), ten[:, s], scr, reads=[sb], writes=[bl[s]])
                issued[r] += 1
                nxt[0] += 1

        def wslot(r):
            ten, bl, ns = rings[r]
            assert consumed[r] < issued[r], "weight chunk not issued"
            s = consumed[r] % ns
            return ten[:, s], bl[s]

        def wdone(r):
            consumed[r] += 1
            pump()

        sc_i = [0]

        def preload(func):
            ACT(DUM[:, 1:2], DUM[:, 0:1], func, [b_DUM], [b_DUM])

        def tm_sum(chunks, fp32=False):
            col = 4 * (sc_i[0] % 16)
            sc_i[0] += 1
            oc = onesf[:, 0:1] if fp32 else onesb[:, 0:1]
            bo = b_onesf if fp32 else b_onesb
            n = len(chunks)
            for tb in range(4):
                for i, (ap, b) in enumerate(chunks):
                    MM(PST[:, 0, col + tb:col + tb + 1], ap[:, tb * 128:(tb + 1) * 128], oc, i == 0, i == n - 1, [b, bo], [b_PST[0]],
                       signal=(tb == 3 and i == n - 1))
            return col

        def small(col, scale, bias):
            a, ba = r_S4.get()
            ACT(a, PST[:, 0, col:col + 4], AF.Identity, [b_PST[0]], [ba], scale=scale, bias=bias)
            return a, ba

        def rpow(a, ba):
            r, br = r_S4.get()
            TT(P.pool, r, a, NH4[:], ALU.pow, [ba, b_NH4], [br])
            return r, br

        def bcast(a, ba):
            d4, bd4 = r_D4.get()
            TT(P.dve, d4, identf[:].unsqueeze(1).broadcast_to([128, 4, 128]), a.unsqueeze(2).broadcast_to([128, 4, 128]), ALU.mult,
               [b_identf, ba], [bd4])
            pb = gbank()
            MM(PSG[:, pb, :], onesf[:], d4.rearrange("p a b -> p (a b)"), True, True, [b_onesf, bd4], [b_PSG[pb]])
            return PSG[:, pb, :], b_PSG[pb]

        def rstd_of(chunks, scale, eps):
            n = len(chunks)
            for i, (ap, b) in enumerate(chunks):
                MM(PST[:, 1, :], onesb[:], ap, i == 0, i == n - 1, [b_onesb, b], [b_PST[1]], signal=(i == n - 1))
            r, br = r_SS.get()
            ACT(r, PST[:, 1, :], AF.Ln, [b_PST[1]], [br], scale=scale, bias=eps)
            ACT(r, r, AF.Exp, [br], [br], scale=-0.5)
            return r, br

        def prenorm(li, gcol, shcol, have_sq, xa=False, nxt=None):
            SQ8 = YT
            X, bX = (XA, b_XA) if xa else (XT, b_XT)
            if not have_sq:
                for c in range(8):
                    ACT(SQ8[:, c, :], X[:, c, :], AF.Square, [bX[c]], [b_YT[c]])
            r, br = rstd_of([(SQ8[:, c, :], b_YT[c]) for c in range(8)], 1.0 / D_MODEL, EPS)
            for c in range(8):
                t, bt = r_TMP.get()
                TT(P.dve, t, X[:, c, :], r, ALU.mult, [bX[c], br], [bt])
                ACT(HT[:, c, :], t, AF.Identity, [bt, b_DV, b_MOD], [b_HT[c]],
                    scale=DV[:, li, gcol + c:gcol + c + 1], bias=MOD[:, li, shcol + c:shcol + c + 1])
            if xa:
                for c in range(8):
                    ACT(XT[:, c, :], XA[:, c, :], AF.Copy, [b_XA[c]], [b_XT[c]])

        def postnorm_residual(li, ggcol, eps, produce, sq_next):
            SQ8 = HT
            for m in range(8):
                pb = produce(m)
                CP(P.dve, Y32[:, m, :], PSG[:, pb, :], [b_PSG[pb]], [b_Y32[m]])
                ACT(SQ8[:, m, :], Y32[:, m, :], AF.Square, [b_Y32[m]], [b_HT[m]])
            r, br = rstd_of([(SQ8[:, m, :], b_HT[m]) for m in range(8)], 1.0 / D_MODEL, eps)
            for m in range(8):
                t, bt = r_TMP.get()
                STT(t, Y32[:, m, :], DV[:, li, ggcol + m:ggcol + m + 1], r, ALU.mult, ALU.mult, [b_Y32[m], b_DV, br], [bt])
                TT(P.dve, XT[:, m, :], XT[:, m, :], t, ALU.add, [b_XT[m], bt], [b_XT[m]])
                if sq_next:
                    ACT(YT[:, m, :], XT[:, m, :], AF.Square, [b_XT[m]], [b_YT[m]])

        def mixer(ti, li, have_sq, xa=False):
            prenorm(li, 0, 0, have_sq, xa, AF.Tanh)
            dump("h1", HT[:], b_HT, [128, 8, T])
            hcl = HC[:, li]
            zcl = ZC[:, li]
            stage('m_prenorm')
            if ti > 0:
                ACT(hcl[:, :, 0:30], hcl[:, :, T:T + 30], AF.Copy, b_HC[li], b_HC[li])
                ACT(zcl[:, :, 0:15], zcl[:, :, T:T + 15], AF.Copy, b_ZC[li], b_ZC[li])
            def inproj_fm():
                w, bw = wslot("A")
                pb = gbank()
                for kc in range(8):
                    MM(PSG[:, pb, :], w[:, kc, :], HT[:, kc, :], kc == 0, kc == 7, [bw, b_HT[kc]], [b_PSG[pb]], signal=(kc == 7))
                wdone("A")
                return pb

            csq = []

            def conv_chunk(c):
                dg, bdg = dgs[c]
                pb = gbank()
                for k in range(31):
                    MM(PSG[:, pb, :], dg[:, k, :], hcl[:, c, k:k + T], k == 0, k == 30, [bdg, b_HC[li][c]], [b_PSG[pb]], signal=(k == 30))
                ACT(Y32[:, 3 + c, :], PSG[:, pb, :], AF.Identity, [b_PSG[pb], b_PV], [b_Y32[3 + c]], bias=PV[:, li, PV_CB + c:PV_CB + c + 1])
                q, bq = r_SQ.get()
                ACT(q, Y32[:, 3 + c, :], AF.Square, [b_Y32[3 + c]], [bq])
                csq.append((q, bq))

            def v_path_and_pool():
                def pool_sums():
                    TT(P.dve, PA[:, :, 1:527], zcl[:, :, 1:527], zcl[:, :, 0:526], ALU.add, b_ZC[li], [b_PA])
                    TT(P.dve, PB[64:128, 0, 3:527], PA[64:128, 0, 3:527], PA[64:128, 0, 1:525], ALU.add, [b_PA], [b_PB])
                    TT(P.dve, PB[:, 1, 3:527], PA[:, 1, 3:527], PA[:, 1, 1:525], ALU.add, [b_PA], [b_PB])
                    TT(P.dve, PA[:, 1, 7:527], PB[:, 1, 7:527], PB[:, 1, 3:523], ALU.add, [b_PB, b_PA], [b_PA])
                    TT(P.dve, PB[64:128, 1, 15:527], PA[64:128, 1, 15:527], PA[64:128, 1, 7:519], ALU.add, [b_PA, b_PB], [b_PB])
                st = VST[:]
                for tb in range(4):
                    v, bv = vvs[tb]
                    v3 = v.rearrange("p (h d) -> p h d", h=6)
                    P.op(P.dve, lambda e, v3=v3, tb=tb: e.tensor_reduce(out=st[:, 0, tb * 6:tb * 6 + 6], in_=v3, axis=AX.X, op=ALU.add), [bv], [b_VST])
                    ACT(VSQ[:], v, AF.Square, [bv], [b_VSQ])
                    P.op(P.dve, lambda e, tb=tb: e.tensor_reduce(out=st[:, 1, tb * 6:tb * 6 + 6], in_=VSQ[:].rearrange("p (h d) -> p h d", h=6), axis=AX.X,
                                                                  op=ALU.add), [b_VSQ], [b_VST])
                TS(st[:, 2, :], st[:, 0, :], 1.0 / 64, None, ALU.mult, None, [b_VST], [b_VST])
                TT(P.dve, st[:, 3, :], st[:, 2, :], st[:, 2, :], ALU.mult, [b_VST], [b_VST])
                STT(st[:, 3, :], st[:, 1, :], 1.0 / 64, st[:, 3, :], ALU.mult, ALU.subtract, [b_VST], [b_VST])
                TS(st[:, 3, :], st[:, 3, :], EPS, None, ALU.add, None, [b_VST], [b_VST])
                TT(P.pool, st[:, 4, :], st[:, 3, :], NH24[:], ALU.pow, [b_VST, b_NH24], [b_VST2])
                pool_sums()
                stage('m_v')
                srcs = [(PA, 0, 0), (PB, 0, 1), (PA, 1, 0), (PB, 1, 1)]
                for g in range(4):
                    ten, c, hh = srcs[g]
                    sl = slice(64 * hh, 64 * hh + 64)
                    STT(DD[sl, c, :], ten[sl, c, 15:527], INVW[sl, c:c + 1], zcl[sl, c, 15:527], ALU.mult, ALU.subtract,
                        [b_PA, b_PB, b_INVW, b_ZC[li][c]], [b_DD[c]])
                    if ti == 0:
                        t, bt = r_TMP.get()
                        TT(P.dve, t[sl, 0:16], ten[sl, c, 15:31], IC0[sl, c, :], ALU.mult, [b_PA, b_PB, b_IC0], [bt])
                        TT(P.dve, DD[sl, c, 0:16], t[sl, 0:16], zcl[sl, c, 15:31], ALU.subtract, [bt, b_ZC[li][c]], [b_DD[c]])

                for tb in range(4):
                    v, bv = vvs[tb]
                    v3 = v.rearrange("p (h d) -> p h d", h=6)
                    TT(P.dve, v3, v3, st[:, 2, tb * 6:tb * 6 + 6].unsqueeze(2).broadcast_to([128, 6, 64]), ALU.subtract, [bv, b_VST], [bv])
                    TT(P.dve, VN[:, tb, :].rearrange("p (h d) -> p h d", h=6), v3,
                       st[:, 4, tb * 6:tb * 6 + 6].unsqueeze(2).broadcast_to([128, 6, 64]), ALU.mult, [bv, b_VST2], [b_VN[tb]])

            def pool_mm():
                for c in range(2):
                    pb = gbank()
                    MM(PSG[:, pb, :], PWb[:, li, c, :], DD[:, c, :], True, True, [b_PWb, b_DD[c]], [b_PSG[pb]])
                    ACT(Y32[:, 6 + c, :], PSG[:, pb, :], AF.Identity, [b_PSG[pb], b_PV], [b_Y32[6 + c]], scale=PV[:, li, PV_PS + c:PV_PS + c + 1])
                dump("yc", Y32[:, 6:8, :], b_Y32[6:8], [128, 2, T])
                stage('m_pool')

            def sgu_mix():
                for c in range(3):
                    pb = gbank()
                    for tb in range(4):
                        for hh in range(2):
                            h = 2 * c + hh
                            MM(PSG[64 * hh:64 * hh + 64, pb, tb * 128:(tb + 1) * 128], VN[:, tb, h * 64:(h + 1) * 64], WST[:, li, h, :], True, True,
                               [b_VN[tb], b_WSTl[li]], [b_PSG[pb]], signal=(tb == 3 and hh == 1))
                    t, bt = r_TMP.get()
                    STT(t.rearrange("p (b t) -> p b t", b=4), PSG[:, pb, :].rearrange("p (b t) -> p b t", b=4), PV[:, li, PV_SG + c:PV_SG + c + 1],
                        CC[:, li, c, :].unsqueeze(1).broadcast_to([128, 4, 128]), ALU.mult, ALU.add, [b_PSG[pb], b_PV, b_CCl[li]], [bt])
                    TT(P.dve, Y32[:, c, :], UU[:, c, :], t, ALU.mult, [b_UU[c], bt], [b_Y32[c]])
                dump("ya", Y32[:, 0:3, :], b_Y32[0:3], [128, 3, T])
                stage('m_sgu')

            for c in range(3):
                pg = inproj_fm()
                th, bth = r_TH.get()
                ACT(th, PSG[:, pg, :], AF.Tanh, [b_PSG[pg]], [bth], scale=0.5)
                pa = inproj_fm()
                STT(hcl[:, c, 30:30 + T], th, 1.0, PSG[:, pa, :], ALU.add, ALU.mult, [bth, b_PSG[pa]], [b_HC[li][c]])
            stage('m_glu')
            dgs = []
            for c in range(3):
                if ti == 0:
                    TT(P.dve, DG[:, c], identh[:].unsqueeze(1).broadcast_to([128, 31, 128]),
                       PV[:, li, PV_CW + 31 * c:PV_CW + 31 * c + 31].unsqueeze(2).broadcast_to([128, 31, 128]), ALU.mult,
                       [b_identh, b_PV], [b_DG[c]])
                    if ntiles > 1:
                        DMA(P.sp, "dgw%d" % c, dg_b[li, c], DG[:, c], reads=[b_DG[c]], writes=[b_dgb[li][c]])
                else:
                    DMA(P.sp, "dg%d" % c, DG[:, c], dg_b[li, c], reads=[b_dgb[li][c]], writes=[b_DG[c]])
                dgs.append((DG[:, c], b_DG[c]))
            vb = [gbank() for _ in range(4)]
            for jv in range(3):
                w, bw = wslot("A")
                for tb in range(4):
                    for kc in range(8):
                        MM(PSG[:, vb[tb], jv * 128:(jv + 1) * 128], HT[:, kc, tb * 128:(tb + 1) * 128], w[:, kc, :], kc == 0, kc == 7,
                           [bw, b_HT[kc]], [b_PSG[vb[tb]]], signal=(kc == 7))
                wdone("A")
            vvs = []
            for tb in range(4):
                v = VV4[:, tb, :]
                bv = b_VV4[tb]
                ACT(v, PSG[:, vb[tb], 0:384], AF.Gelu, [b_PSG[vb[tb]]], [bv])
                vvs.append((v, bv))
            for c in range(2):
                pb = inproj_fm()
                ACT(zcl[:, c, 15:15 + T], PSG[:, pb, :], AF.Copy, [b_PSG[pb]], [b_ZC[li][c]])
            stage('m_inproj')
            conv_chunk(0)
            v_path_and_pool()
            conv_chunk(1)
            conv_chunk(2)
            for c in range(3):
                pb = inproj_fm()
                ACT(UU[:, c, :], PSG[:, pb, :], AF.Gelu, [b_PSG[pb]], [b_UU[c]])
            stage('m_u')
            colm = tm_sum([(Y32[:, 3 + c, :], b_Y32[3 + c]) for c in range(3)], fp32=True)
            colq = tm_sum(csq)
            pool_mm()
            sgu_mix()
            mean4, bmean4 = small(colm, 1.0 / 384, 0.0)
            var4, bvar4 = small(colq, 1.0 / 384, EPS)
            m24, bm24 = r_S4.get()
            TT(P.dve, m24, mean4, mean4, ALU.mult, [bmean4], [bm24])
            TT(P.dve, var4, var4, m24, ALU.subtract, [bvar4, bm24], [bvar4])
            rs4, brs4 = rpow(var4, bvar4)
            mean, bmean = bcast(mean4, bmean4)
            rs, brs = bcast(rs4, brs4)

            def branch_norm(c0, c1, n, eps):
                sqs = []
                for c in range(c0, c1):
                    q, bq = r_SQ.get()
                    ACT(q, Y32[:, c, :], AF.Square, [b_Y32[c]], [bq])
                    sqs.append((q, bq))
                r, br = rstd_of(sqs, 1.0 / n, eps)
                for c in range(c0, c1):
                    STT(YT[:, c, :], Y32[:, c, :], PV[:, li, PV_BR + c:PV_BR + c + 1], r, ALU.mult, ALU.mult, [b_Y32[c], b_PV, br], [b_YT[c]])
            for c in range(3):
                t, bt = r_TMP.get()
                TT(P.dve, t, Y32[:, 3 + c, :], mean, ALU.subtract, [b_Y32[3 + c], bmean], [bt])
                TT(P.dve, t, t, rs, ALU.mult, [bt, brs], [bt])
                th, bth = r_TH.get()
                ACT(th, t, AF.Tanh, [bt, b_DV], [bth], scale=DV[:, li, 32 + c:33 + c], bias=DV[:, li, 35 + c:36 + c])
                l_, bl_ = r_SS.get()
                ACT(l_, t, AF.Identity, [bt, b_PV], [bl_], scale=PV[:, li, PV_CNG + c:PV_CNG + c + 1], bias=PV[:, li, PV_CNB + c:PV_CNB + c + 1])
                STT(Y32[:, 3 + c, :], th, 1.0, l_, ALU.add, ALU.mult, [bth, bl_], [b_Y32[3 + c]])
            dump("yb", Y32[:, 3:6, :], b_Y32[3:6], [128, 3, T])
            stage('m_conv')
            branch_norm(3, 6, 384, 4 * EPS)
            branch_norm(6, 8, 256, EPS)
            branch_norm(0, 3, 384, EPS)
            dump("yT", YT[:], b_YT, [128, 8, T])
            stage('m_brnorm')

            def produce(m):
                w, bw = wslot("A")
                pb = gbank()
                for kc in range(8):
                    MM(PSG[:, pb, :], w[:, kc, :], YT[:, kc, :], kc == 0, kc == 7, [bw, b_YT[kc]], [b_PSG[pb]], signal=(kc == 7))
                wdone("A")
                return pb
            postnorm_residual(li, 8, EPS, produce, True)
            stage('m_out')
            dump("x1", XT[:], b_XT, [128, 8, T])

        def ffn(ti, li, last):
            prenorm(li, 16, 24, True, False, AF.Gelu)
            dump("h2", HT[:], b_HT, [128, 8, T])
            ztl = ZT[:, li]
            fw = lambda k, j: PV[:, li, PV_FW + 44 * k + j:PV_FW + 44 * k + j + 1]
            fwv = lambda k: PV[:, li, PV_FW + 44 * k:PV_FW + 44 * k + 44].rearrange("p (g j) -> p j g", g=2)
            TT(P.dve, CORR[:, :, :, 0], ztl[:, :, :, 1], fwv(1), ALU.mult, [b_ZT[li], b_PV], [b_CORR])
            TT(P.dve, CT[:, :, :, 0], ztl[:, :, :, 0], fwv(0), ALU.mult, [b_ZT[li], b_PV], [b_CT])
            TT(P.dve, CORR[:, :, :, 0], CORR[:, :, :, 0], CT[:, :, :, 0], ALU.add, [b_CORR, b_CT], [b_CORR])
            TT(P.dve, CORR[:, :, :, 1], ztl[:, :, :, 1], fwv(0), ALU.mult, [b_ZT[li], b_PV], [b_CORR])
            def fin_act(p):
                gl, bgl = r_GL.get()
                ACT(gl, p[1][:, 0, :], AF.Gelu, [p[2]], [bgl])
                return gl, bgl

            def fin_dve(p, glp):
                TT(P.dve, AT[:, p[0], :], glp[0], p[1][:, 1, :], ALU.mult, [glp[1], p[2]], [b_AT[p[0]]])
            pend = None
            for j in range(NJ):
                w, bw = wslot("U")
                p0 = gpair()
                for g in range(2):
                    for kc in range(8):
                        MM(PSG[:, p0 + g, :], w[:, kc, g * 128:(g + 1) * 128], HT[:, kc, :], kc == 0, kc == 7, [bw, b_HT[kc]], [b_PSG[p0 + g]],
                           signal=(kc == 7))
                wdone("U")
                acc, bacc = r_ACC.get()
                pbs = [b_PSG[p0], b_PSG[p0 + 1]]
                for g in range(2):
                    jj = j + 22 * g
                    ACT(acc[:, g, :], PSG[:, p0 + g, :], AF.Identity, [b_PSG[p0 + g], b_PV], [bacc], scale=fw(2, jj),
                        bias=PV[:, li, PV_FB + jj:PV_FB + jj + 1])
                if pend is not None:
                    glp = fin_act(pend)
                for g in range(2):
                    jj = j + 22 * g
                    STT(acc[:, g, 1:T], PSG[:, p0 + g, 0:T - 1], fw(1, jj), acc[:, g, 1:T], ALU.mult, ALU.add, [b_PSG[p0 + g], b_PV, bacc], [bacc])
                    STT(acc[:, g, 2:T], PSG[:, p0 + g, 0:T - 2], fw(0, jj), acc[:, g, 2:T], ALU.mult, ALU.add, [b_PSG[p0 + g], b_PV, bacc], [bacc])
                CP(P.dve, ztl[:, j, :, :], PSG[:, p0:p0 + 2, T - 2:T], pbs, [b_ZT[li]])
                TT(P.dve, acc[:, :, 0:2], acc[:, :, 0:2], CORR[:, j, :, :], ALU.add, [bacc, b_CORR], [bacc])
                if pend is not None:
                    fin_dve(pend, glp)
                pend = (j, acc, bacc)
            glp = fin_act(pend)
            fin_dve(pend, glp)
            stage('f_up')
            dump("aT", AT[:, 0:4, :], b_AT[0:4], [128, 4, T])

            first_pair = pair_i[0] % 3
            dbanks = [(2 * ((first_pair + i // 2) % 3) + i % 2) for i in range(6)]
            for m in range(6):
                w, bw = wslot("D")
                for kc in range(16):
                    MM(PSG[:, dbanks[m], :], w[:, kc, :], AT[:, kc, :], kc == 0, False, [bw, b_AT[kc]], [b_PSG[dbanks[m]]], signal=(kc == 15))
                wdone("D")

            def produce(m):
                if m < 6:
                    pb = dbanks[m]
                else:
                    pb = dbanks[m - 6]
                    w, bw = wslot("D")
                    for kc in range(16):
                        MM(PSG[:, pb, :], w[:, kc, :], AT[:, kc, :], kc == 0, False, [bw, b_AT[kc]], [b_PSG[pb]], signal=(kc == 15))
                    wdone("D")
                w, bw = wslot("E")
                for kc in range(16, NJ):
                    MM(PSG[:, pb, :], w[:, kc - 16, :], AT[:, kc, :], False, kc == NJ - 1, [bw, b_AT[kc]], [b_PSG[pb]], signal=(kc == NJ - 1))
                wdone("E")
                return pb
            postnorm_residual(li, 24, EPS, produce, not last)
            dump("x2", XT[:], b_XT, [128, 8, T])

        pump()
        for ti in range(ntiles):
            if ti == 0:
                DMA(P.sp, "xin", XT[:], x_fm[ti], writes=b_XT)
            for li in range(L):
                mixer(ti, li, li > 0, xa=(ti > 0 and li == 0))
                if li == L - 1 and ti + 1 < ntiles:
                    DMA(P.sp, "xin", XA, x_fm[ti + 1], writes=b_XA)
                ffn(ti, li, li == L - 1)
            DMA(P.sp, "xout", out_fm[ti], XT[:], reads=b_XT)

    except _Stop:
        pass
    P.barrier()
    print('sbuf bytes remaining', nc.sbuf_bytes_remaining)
    P.emit()
    return nc, dbg_out


def _fm(v, n):
    return np.ascontiguousarray(np.asarray(v, np.float32).reshape(n, 128).T)


def prep_shared(inp, layers):
    g = lambda k: np.asarray(inp[k], np.float32)
    pv, modc, winc, woutc, upc, dnc, wst, sgub, pwbd = [], [], [], [], [], [], [], [], []
    for l in layers:
        cols = [_fm(g("mix_pre_g")[l], 8), _fm(g("mix_post_g")[l], 8), _fm(g("branch_g")[l], 8), _fm(g("ffn_pre_g")[l], 8),
                _fm(g("ffn_post_g")[l], 8), _fm(g("sgu_norm_g")[l], 3), _fm(g("sgu_norm_b")[l], 3), _fm(g("conv_b")[l], 3),
                _fm(g("conv_norm_g")[l], 3), _fm(g("conv_norm_b")[l], 3), _fm(g("pool_scale")[l], 2)]
        cw = g("conv_w")[l].reshape(31, 3, 128).transpose(2, 1, 0).reshape(128, 93)
        fw = g("ffn_conv_w")[l].reshape(3, 44, 128).transpose(2, 0, 1).reshape(128, 132)
        cols += [cw, fw, _fm(g("ffn_conv_b")[l], 44), _fm(g("mod_b")[l], 48)]
        p = np.concatenate(cols, axis=1)
        assert p.shape == (128, NPV)
        pv.append(p)
        modc.append(g("mod_w")[l].reshape(8, 128, 12, 4, 128).transpose(2, 1, 3, 0, 4))
        winc.append(g("w_in")[l].reshape(8, 128, 14, 128).transpose(2, 1, 0, 3)[IN_ORDER])
        woutc.append(g("w_out")[l].reshape(8, 128, 8, 128).transpose(2, 1, 0, 3))
        upc.append(g("ffn_up")[l].reshape(8, 128, 2, NJ, 128).transpose(3, 1, 0, 2, 4).reshape(NJ, 128, 8, 256))
        dnc.append(g("ffn_down")[l].reshape(NJ, 128, 8, 128).transpose(2, 1, 0, 3))
        wst.append(g("sgu_w")[l].transpose(2, 0, 1))
        sgub.append(g("sgu_b")[l])
        pw = g("pool_w")[l]
        bd = np.zeros((128, 2, 128), np.float32)
        for c in range(2):
            bd[0:64, c, 0:64] = pw[2 * c]
            bd[64:128, c, 64:128] = pw[2 * c + 1]
        pwbd.append(bd)
    ca = lambda lst: np.ascontiguousarray(np.stack(lst, 0), dtype=np.float32)
    s_idx = np.arange(128)
    tri = (s_idx[None, :] >= s_idx[:, None]).astype(np.float32)
    wins = np.array([2, 4, 8, 16], np.float32)
    ic0 = np.zeros((128, 2, 16), np.float32)
    invw = np.zeros((128, 2), np.float32)
    for c in range(2):
        for hh in range(2):
            w_ = wins[2 * c + hh]
            ic0[64 * hh:64 * hh + 64, c, :] = 1.0 / np.minimum(np.arange(1, 17, dtype=np.float32), w_)
            invw[64 * hh:64 * hh + 64, c] = 1.0 / w_
    return {"pv": ca(pv), "mod_c": ca(modc), "w_in_c": ca(winc), "w_out_c": ca(woutc), "up_c": ca(upc), "down_c": ca(dnc),
            "wst": ca(wst), "sgub": ca(sgub), "pwbd": ca(pwbd), "identh": (0.5 * np.eye(128)).astype(np.float32), "identf": np.eye(128, dtype=np.float32), "tri": tri,
            "ic0": ic0, "invw": invw}


def x_to_fm(xb, ntiles=NT):
    return np.ascontiguousarray(np.asarray(xb, np.float32).reshape(ntiles, T, 8, 128).transpose(0, 3, 2, 1))


def fm_to_x(o):
    nt = o.shape[0]
    return np.ascontiguousarray(o.transpose(0, 3, 2, 1).reshape(nt * T, D_MODEL))


_CACHE = {}


def _get_prog(layers):
    key = tuple(layers)
    if key not in _CACHE:
        _CACHE[key] = build(list(range(len(layers))))[0]
    return _CACHE[key]


def run_layers(x, inputs, layers):
    shared = prep_shared(inputs, layers)
    c = np.asarray(inputs["c"], np.float32)
    nc = _get_prog(layers)
    in_maps = []
    for b in range(BATCH):
        m = dict(shared)
        m["x_fm"] = x_to_fm(x[b])
        m["c_fm"] = _fm(c[b], 8)
        in_maps.append(m)
    res = run_bass_kernel_spmd(nc, in_maps, core_ids=list(range(BATCH)))
    return np.stack([fm_to_x(res.results[b]["out_fm"]) for b in range(BATCH)], 0)


FUSED = True


def kernel(**inputs):
    x = np.asarray(inputs["x"], np.float32)
    if FUSED:
        out = run_layers(x, inputs, list(range(DEPTH)))
    else:
        out = x
        for l in range(DEPTH):
            out = run_layers(out, inputs, [l])
    return out.astype(np.float32)
```

```python
import contextlib
import numpy as np
import concourse.bass as bass
import concourse.mybir as mybir
from concourse.bass_utils import run_bass_kernel_spmd

F32 = mybir.dt.float32
BF16 = mybir.dt.bfloat16
AF = mybir.ActivationFunctionType
ALU = mybir.AluOpType
AX = mybir.AxisListType

D_MODEL = 1024
BATCH = 8
SEQ = 4096
DEPTH = 2
T = 512
NT = SEQ // T
D_FF = 2816
NJ = D_FF // 128
EPS = 1e-6
NPV = 374
IN_ORDER = [9, 6, 10, 7, 11, 8, 3, 4, 5, 12, 13, 0, 1, 2]

PV_PRE, PV_POST, PV_BR, PV_FPRE, PV_FPOST = 0, 8, 16, 24, 32
PV_SG, PV_SB, PV_CB, PV_CNG, PV_CNB, PV_PS = 40, 43, 46, 49, 52, 55
PV_CW, PV_FW, PV_FB, PV_MB = 57, 150, 282, 326


class Buf:
    __slots__ = ("name", "w", "r", "al", "excl")

    def __init__(self, name):
        self.name = name
        self.w = None
        self.r = []
        self.al = ()
        self.excl = False


class Eng:
    def __init__(self, name, attr):
        self.name = name
        self.attr = attr
        self.ops = []
        self.count = 0
        self.waited = {}


class Prog:
    def __init__(self, nc):
        self.nc = nc
        self.es = contextlib.ExitStack()
        self.pe = Eng("pe", "tensor")
        self.act = Eng("act", "scalar")
        self.dve = Eng("dve", "vector")
        self.pool = Eng("pool", "gpsimd")
        self.sp = Eng("sp", "sync")
        self.engs = [self.pe, self.act, self.dve, self.pool, self.sp]
        self.sems = {}
        for e in self.engs:
            self.sems[e.name] = self.es.enter_context(nc.semaphore("s_" + e.name))
        self.dma_counts = {}
        self.nbuf = 0

    def sbuf(self, name, shape, dt):
        return self.es.enter_context(self.nc.sbuf_tensor(name, list(shape), dt))

    def psum(self, name, shape, dt=F32):
        return self.es.enter_context(self.nc.psum_tensor(name, list(shape), dt))

    def buf(self, name=None):
        self.nbuf += 1
        return Buf(name or "b%d" % self.nbuf)

    def bufs(self, n, name=None):
        return [self.buf(None if name is None else "%s%d" % (name, i)) for i in range(n)]

    def dma_sem(self, name):
        key = "d_" + name
        self.sems[key] = self.es.enter_context(self.nc.semaphore(key))
        self.dma_counts[key] = 0
        return key

    def _collect(self, eng, reads, writes):
        waits = {}

        def need(tok, skip_same):
            if tok is None:
                return
            k, v = tok
            if skip_same and k == eng.name:
                return
            if eng.waited.get(k, 0) >= v:
                return
            if waits.get(k, 0) < v:
                waits[k] = v

        pe_same = eng.name == "pe"
        for b0 in reads:
            for b in (b0,) + tuple(b0.al):
                need(b.w, False)
                if b.excl:
                    for t in b.r:
                        need(t, True)
        for b0 in writes:
            for b in (b0,) + tuple(b0.al):
                need(b.w, pe_same)
                for t in b.r:
                    need(t, pe_same)
        for k, v in waits.items():
            eng.waited[k] = v
        return sorted(waits.items())

    def op(self, eng, fn, reads=(), writes=(), signal=True):
        waits = self._collect(eng, reads, writes)
        tok = (eng.name, eng.count + 1)
        for b in reads:
            b.r.append(tok)
        for b in writes:
            b.w = tok
            b.r = []
        inc = None
        if signal:
            eng.count += 1
            inc = (eng.name, 1)
        eng.ops.append((waits, fn, inc))

    def dma(self, eng, semkey, out, in_, reads=(), writes=(), **kw):
        waits = self._collect(eng, reads, writes)
        self.dma_counts[semkey] += 16
        tok = (semkey, self.dma_counts[semkey])
        for b in reads:
            b.r.append(tok)
        for b in writes:
            b.w = tok
            b.r = []

        def fn(e, out=out, in_=in_, kw=kw):
            return e.dma_start(out=out, in_=in_, **kw)
        eng.ops.append((waits, fn, (semkey, 16)))
        return tok

    def wait_tok(self, eng, tok):
        k, v = tok
        if eng.waited.get(k, 0) >= v:
            return
        eng.waited[k] = v
        eng.ops.append(([(k, v)], None, None))

    def barrier(self):
        toks = [(e.name, e.count) for e in self.engs if e.count > 0]
        toks += [(k, v) for k, v in self.dma_counts.items() if v > 0]
        for e in self.engs:
            for t in toks:
                if t[0] != e.name:
                    self.wait_tok(e, t)

    def emit(self):
        nc = self.nc
        with nc.Block() as block:
            for e in self.engs:
                if not e.ops:
                    continue

                def body(h, e=e):
                    for waits, fn, inc in e.ops:
                        for k, v in waits:
                            h.wait_ge(self.sems[k], v)
                        if fn is not None:
                            ins = fn(h)
                            if inc is not None:
                                ins.then_inc(self.sems[inc[0]], inc[1])
                getattr(block, e.attr)(body)
        self.es.close()


class Rot:
    def __init__(self, items):
        self.items = items
        self.i = 0

    def get(self):
        it = self.items[self.i % len(self.items)]
        self.i += 1
        return it


class _Stop(Exception):
    pass


def build(layers, ntiles=NT, dbg=(), stop_after=None):
    nc = bass.Bass("TRN2", target_bir_lowering=False)
    P = Prog(nc)
    L = len(layers)
    dbg = set(dbg)
    dbg_out = {}

    def din(name, shape, dt=F32):
        return nc.dram_tensor(name, list(shape), dt, kind="ExternalInput").ap()

    x_fm = din("x_fm", [ntiles, 128, 8, T])
    out_fm = nc.dram_tensor("out_fm", [ntiles, 128, 8, T], F32, kind="ExternalOutput").ap()
    c_fm = din("c_fm", [128, 8])
    pv_d = din("pv", [L, 128, NPV])
    mod_d = din("mod_c", [L, 12, 128, 4, 8, 128])
    win_d = din("w_in_c", [L, 14, 128, 8, 128])
    wout_d = din("w_out_c", [L, 8, 128, 8, 128])
    up_d = din("up_c", [L, NJ, 128, 8, 256])
    dn_d = din("down_c", [L, 8, 128, NJ, 128])
    wst_d = din("wst", [L, 128, 6, 128])
    sgub_d = din("sgub", [L, 6, 128])
    pwbd_d = din("pwbd", [L, 128, 2, 128])
    ident_d = din("identh", [128, 128])
    identf_d = din("identf", [128, 128])
    tri_d = din("tri", [128, 128])
    ic0_d = din("ic0", [128, 2, 16])
    invw_d = din("invw", [128, 2])

    def dscr(name, shape):
        return nc.dram_tensor(name, list(shape), BF16, kind="Internal").ap()
    win_b = dscr("w_in_b", [L, 14, 128, 8, 128])
    wout_b = dscr("w_out_b", [L, 8, 128, 8, 128])
    up_b = dscr("up_b", [L, NJ, 128, 8, 256])
    dn_b = dscr("down_b", [L, 8, 128, NJ, 128])
    b_win = [P.bufs(14) for _ in range(L)]; b_wout = [P.bufs(8) for _ in range(L)]
    b_up = [P.bufs(NJ) for _ in range(L)]; b_dn = [P.bufs(8) for _ in range(L)]; b_dne = [P.bufs(8) for _ in range(L)]
    dg_b = dscr("dg_b", [L, 3, 128, 31, 128]); b_dgb = [P.bufs(3, "dgb%d_" % l) for l in range(L)]

    PV = P.sbuf("PV", [128, L, NPV], F32); b_PV = P.buf("PV")
    MOD = P.sbuf("MOD", [128, L, 48], F32); b_MOD = P.buf("MOD")
    DV = P.sbuf("DV", [128, L, 40], F32); b_DV = P.buf("DV")
    CF = P.sbuf("CF", [128, 8], F32); b_CF = P.buf("CF")
    CF2 = P.sbuf("CF2", [128, 8], F32); b_CF2 = P.buf("CF2")
    SCb = P.sbuf("SCb", [128, 8], BF16); b_SCb = P.buf("SCb")
    WST = P.sbuf("WST", [128, L, 6, 128], BF16); b_WSTl = P.bufs(L, "WST")
    CC = P.sbuf("CC", [128, L, 3, 128], F32); b_CCl = P.bufs(L, "CC")
    PWb = P.sbuf("PWb", [128, L, 2, 128], BF16); b_PWb = P.buf("PWb")
    onesb = P.sbuf("onesb", [128, 128], BF16); b_onesb = P.buf("onesb")
    onesf = P.sbuf("onesf", [128, 128], F32); b_onesf = P.buf("onesf")
    identh = P.sbuf("identh_s", [128, 128], BF16); b_identh = P.buf("identh")
    identf = P.sbuf("identf_s", [128, 128], F32); b_identf = P.buf("identf")
    NH4 = P.sbuf("NH4", [128, 4], F32); b_NH4 = P.buf("NH4")
    DUM = P.sbuf("DUM", [128, 2], F32); b_DUM = P.buf("DUM")
    CRS = P.sbuf("CRS", [128, T], F32); b_CRS = P.buf("CRS")
    S4 = P.sbuf("S4", [128, 8, 4], F32); r_S4 = Rot([(S4[:, i, :], P.buf("S4_%d" % i)) for i in range(8)])
    CMN = P.sbuf("CMN", [128, T], F32); b_CMN = P.buf("CMN")
    r_D4 = None
    IC0 = P.sbuf("IC0", [128, 2, 16], F32); b_IC0 = P.buf("IC0")
    INVW = P.sbuf("INVW", [128, 2], F32); b_INVW = P.buf("INVW")
    TRI = P.sbuf("TRI", [128, 128], F32); b_TRI = P.buf("TRI")
    HC = P.sbuf("HC", [128, L, 3, 30 + T], BF16); b_HC = [P.bufs(3) for _ in range(L)]
    ZC = P.sbuf("ZC", [128, L, 2, 15 + T], F32); b_ZC = [P.bufs(2) for _ in range(L)]
    ZT = P.sbuf("ZT", [128, L, NJ, 2, 2], F32); b_ZT = P.bufs(L)
    CORR = P.sbuf("CORR", [128, NJ, 2, 2], F32); b_CORR = P.buf("CORR")
    CT = P.sbuf("CT", [128, NJ, 2, 2], F32); b_CT = P.buf("CT")
    XT = P.sbuf("XT", [128, 8, T], F32); b_XT = P.bufs(8, "XT")
    HT = P.sbuf("HT", [128, 8, T], BF16); b_HT = P.bufs(8, "HT")
    Y32 = P.sbuf("Y32", [128, 8, T], F32); b_Y32 = P.bufs(8, "Y32")
    YT = P.sbuf("YT", [128, 8, T], BF16); b_YT = P.bufs(8, "YT")
    SQ = P.sbuf("SQ", [128, 3, T], BF16); r_SQ = Rot([(SQ[:, i, :], P.buf("SQ%d" % i)) for i in range(3)])
    TMP = P.sbuf("TMP", [128, 3, T], F32); r_TMP = Rot([(TMP[:, i, :], P.buf("TMP%d" % i)) for i in range(3)])
    SS = P.sbuf("SS", [128, 2, T], F32); r_SS = Rot([(SS[:, i, :], P.buf("SS%d" % i)) for i in range(2)])
    AR = P.sbuf("AR", [128, NJ * T], BF16)
    AT = AR[:].rearrange("p (j t) -> p j t", j=NJ); b_AT = P.bufs(NJ, "AT")
    ARf = AR[:].bitcast(F32)
    UU = ARf[:, 0:1536].rearrange("p (c t) -> p c t", c=3); b_UU = P.bufs(3, "UU")
    TH = ARf[:, 1536:2560].rearrange("p (c t) -> p c t", c=2); r_TH = Rot([(TH[:, i, :], P.buf("TH%d" % i)) for i in range(2)])
    PA = ARf[:, 2560:2560 + 2 * 527].rearrange("p (c t) -> p c t", c=2); b_PA = P.buf("PA")
    PB = ARf[:, 3616:3616 + 2 * 527].rearrange("p (c t) -> p c t", c=2); b_PB = P.buf("PB")
    DD = AR[:, 2 * 4672:2 * 4672 + 2 * T].rearrange("p (c t) -> p c t", c=2); b_DD = P.bufs(2, "DD")
    mix_scr = b_UU + [it[1] for it in r_TH.items] + [b_PA, b_PB] + b_DD
    for b in mix_scr:
        b.al = tuple(b_AT)
    for b in b_AT:
        b.al = tuple(mix_scr)
    VV4 = P.sbuf("VV4", [128, 4, 384], F32); b_VV4 = P.bufs(4, "VV4")
    VSQ = P.sbuf("VSQ", [128, 384], F32); b_VSQ = P.buf("VSQ")
    VST = P.sbuf("VST", [128, 5, 24], F32); b_VST = P.buf("VST"); b_VST2 = P.buf("VST2"); b_VST3 = P.buf("VST3")
    NH24 = P.sbuf("NH24", [128, 24], F32); b_NH24 = P.buf("NH24")
    VN = P.sbuf("VN", [128, 4, 384], BF16); b_VN = P.bufs(4, "VN")
    ACC = P.sbuf("ACC", [128, 5, 2, T], F32); r_ACC = Rot([(ACC[:, i], P.buf("ACC%d" % i)) for i in range(5)])
    GL = P.sbuf("GL", [128, 3, T], F32); r_GL = Rot([(GL[:, i, :], P.buf("GL%d" % i)) for i in range(3)])
    DG = P.sbuf("DG", [128, 3, 31, 128], BF16); b_DG = P.bufs(3, "DG")
    XA = DG[:].rearrange("p a k n -> p (a k n)").bitcast(F32)[:, 0:8 * T].rearrange("p (c t) -> p c t", c=8); b_XA = P.bufs(8, "XA")
    for b in b_XA:
        b.al = tuple(b_DG)
    for b in b_DG:
        b.al = tuple(b_XA)
    WA = P.sbuf("WA", [128, 4, 8, 128], BF16); b_WA = P.bufs(4, "WA")
    WU = P.sbuf("WU", [128, 3, 8, 256], BF16); b_WU = P.bufs(3, "WU")
    WD = P.sbuf("WD", [128, 2, 16, 128], BF16); b_WD = P.bufs(2, "WD")
    WE = P.sbuf("WE", [128, 2, NJ - 16, 128], BF16); b_WE = P.bufs(2, "WE")
    PSG = P.psum("PSG", [128, 6, T]); b_PSG = P.bufs(6, "PSG")
    PST = P.psum("PST", [128, 2, T]); b_PST = P.bufs(2, "PST")
    for b in b_PSG + b_PST:
        b.excl = True
    gen_i = [0]

    def gbank():
        i = gen_i[0] % 6
        gen_i[0] += 1
        return i

    pair_i = [0]

    def gpair():
        i = pair_i[0] % 3
        pair_i[0] += 1
        return 2 * i
    st_i = [0]

    def sbank():
        i = st_i[0] % 2
        st_i[0] += 1
        return i

    def ACT(out, in_, func, reads, writes, scale=1.0, bias=0.0):
        P.op(P.act, lambda e: e.activation(out=out, in_=in_, func=func, scale=scale, bias=bias), reads, writes)

    def TT(eng, out, in0, in1, op, reads, writes):
        P.op(eng, lambda e: e.tensor_tensor(out=out, in0=in0, in1=in1, op=op), reads, writes)

    def STT(out, in0, scalar, in1, op0, op1, reads, writes):
        P.op(P.dve, lambda e: e.scalar_tensor_tensor(out=out, in0=in0, scalar=scalar, in1=in1, op0=op0, op1=op1), reads, writes)

    def TS(out, in0, s1, s2, op0, op1, reads, writes):
        if s2 is None:
            P.op(P.dve, lambda e: e.tensor_scalar(out=out, in0=in0, scalar1=s1, scalar2=None, op0=op0), reads, writes)
        else:
            P.op(P.dve, lambda e: e.tensor_scalar(out=out, in0=in0, scalar1=s1, scalar2=s2, op0=op0, op1=op1), reads, writes)

    def CP(eng, out, in_, reads, writes):
        P.op(eng, lambda e: e.tensor_copy(out=out, in_=in_), reads, writes)

    def MM(out, lhsT, rhs, start, stop, reads, writes, signal=True):
        P.op(P.pe, lambda e: e.matmul(out, lhsT=lhsT, rhs=rhs, start=start, stop=stop), reads, writes, signal=signal)

    dsem = {}

    def DMA(eng, sem, out, in_, reads=(), writes=()):
        if sem not in dsem:
            dsem[sem] = P.dma_sem(sem)
        return P.dma(eng, dsem[sem], out, in_, reads, writes)

    def stage(name):
        if stop_after == name:
            raise _Stop()

    def dump(name, ap, bufs, shape):
        if name in dbg and name not in dbg_out:
            t = nc.dram_tensor("dbg_" + name, list(shape), ap.dtype, kind="ExternalOutput").ap()
            dbg_out[name] = t
            DMA(P.sp, "dbg_" + name, t, ap, reads=bufs)

    try:
        DMA(P.sp, "c0", PV[:], pv_d.rearrange("l p n -> p l n"), writes=[b_PV])
        DMA(P.sp, "c1", CF[:], c_fm, writes=[b_CF])
        DMA(P.sp, "c2", TRI[:], tri_d, writes=[b_TRI])
        DMA(P.sp, "c3", IC0[:], ic0_d, writes=[b_IC0])
        DMA(P.sp, "c4", INVW[:], invw_d, writes=[b_INVW])
        P.op(P.dve, lambda e: e.memset(onesb[:], 1.0), writes=[b_onesb])
        P.op(P.dve, lambda e: e.memset(onesf[:], 1.0), writes=[b_onesf])
        P.op(P.dve, lambda e: e.memset(NH4[:], -0.5), writes=[b_NH4])
        P.op(P.dve, lambda e: e.memset(DUM[:], 1.0), writes=[b_DUM])
        DMA(P.sp, "c5", identf[:], identf_d, writes=[b_identf])
        P.op(P.dve, lambda e: e.memset(NH24[:], -0.5), writes=[b_NH24])
        P.op(P.dve, lambda e: e.memset(HC[:].rearrange("p l c t -> p (l c t)"), 0.0), writes=[b for bl in b_HC for b in bl])
        P.op(P.dve, lambda e: e.memset(ZC[:].rearrange("p l c t -> p (l c t)"), 0.0), writes=[b for bl in b_ZC for b in bl])
        P.op(P.dve, lambda e: e.memset(ZT[:].rearrange("p l j a b -> p (l j a b)"), 0.0), writes=b_ZT)
        ACT(CF2[:], CF[:], AF.Tanh, [b_CF], [b_CF2], scale=0.5)
        STT(CF2[:], CF2[:], 1.0, CF[:], ALU.add, ALU.mult, [b_CF2, b_CF], [b_CF2])
        TS(SCb[:], CF2[:], 0.5, None, ALU.mult, None, [b_CF2], [b_SCb])
        stage('s_consts')
        MS = Y32[:].rearrange("p c t -> p (c t)").bitcast(BF16).rearrange("p (s j k n) -> p s j k n", s=2, j=4, k=8)
        b_MS = P.bufs(2, "MS")
        for b in b_MS:
            b.al = tuple(b_Y32)
        for b in b_Y32:
            b.al = tuple(b_MS)
        for li in range(L):
            for g in range(12):
                s = (li * 12 + g) % 2
                DMA(P.pool, "ms%d" % s, MS[:, s], mod_d[li, g], writes=[b_MS[s]])
                for jj in range(4):
                    j = g * 4 + jj
                    for kc in range(8):
                        MM(PST[:, 0, li * 48 + j:li * 48 + j + 1], MS[:, s, jj, kc, :], SCb[:, kc:kc + 1], kc == 0, kc == 7,
                           [b_MS[s], b_SCb], [b_PST[0]], signal=(kc == 7))
        for li in range(L):
            TT(P.dve, MOD[:, li, :], PST[:, 0, li * 48:(li + 1) * 48], PV[:, li, PV_MB:PV_MB + 48], ALU.add, [b_PST[0], b_PV], [b_MOD])
        stage('s_mod')
        DMA(P.pool, "ci", identh[:], ident_d, writes=[b_identh])
        DMA(P.pool, "cp", PWb[:], pwbd_d.rearrange("l p c n -> p l c n"), writes=[b_PWb])
        stage('s_casts')
        for li in range(L):
            STT(DV[:, li, 0:8], MOD[:, li, 8:16], 1.0, PV[:, li, PV_PRE:PV_PRE + 8], ALU.add, ALU.mult, [b_MOD, b_PV], [b_DV])
            TT(P.dve, DV[:, li, 8:16], MOD[:, li, 16:24], PV[:, li, PV_POST:PV_POST + 8], ALU.mult, [b_MOD, b_PV], [b_DV])
            STT(DV[:, li, 16:24], MOD[:, li, 32:40], 1.0, PV[:, li, PV_FPRE:PV_FPRE + 8], ALU.add, ALU.mult, [b_MOD, b_PV], [b_DV])
            TT(P.dve, DV[:, li, 24:32], MOD[:, li, 40:48], PV[:, li, PV_FPOST:PV_FPOST + 8], ALU.mult, [b_MOD, b_PV], [b_DV])
            TS(DV[:, li, 32:38], PV[:, li, PV_CNG:PV_CNG + 6], 0.5, None, ALU.mult, None, [b_PV], [b_DV])
        stage('s_derived')
        WSF = TMP[:].rearrange("p a t -> p (a t)")[:, 0:768].rearrange("p (h t) -> p h t", h=6)
        b_WSF = P.buf("WSF")
        b_WSF.al = tuple(it[1] for it in r_TMP.items)
        for it in r_TMP.items:
            it[1].al = (b_WSF,)
        SBB = SS[:].rearrange("p a t -> p (a t)")[:, 0:384].rearrange("p (c t) -> p c t", c=3)
        b_SBB = P.buf("SBB")
        b_SBB.al = tuple(it[1] for it in r_SS.items)
        for it in r_SS.items:
            it[1].al = (b_SBB,)
        for li in range(L):
            DMA(P.sp, "wsf", WSF, wst_d[li], writes=[b_WSF])
            for c in range(3):
                DMA(P.sp, "sbb", SBB[0:64, c, :], sgub_d[li, 2 * c:2 * c + 1, :].broadcast_to([64, 128]), writes=[b_SBB])
                DMA(P.sp, "sbb", SBB[64:128, c, :], sgub_d[li, 2 * c + 1:2 * c + 2, :].broadcast_to([64, 128]), writes=[b_SBB])
            TT(P.dve, WSF, WSF, TRI[:].unsqueeze(1).broadcast_to([128, 6, 128]), ALU.mult, [b_WSF, b_TRI], [b_WSF])
            CP(P.dve, WST[:, li], WSF, [b_WSF], [b_WSTl[li]])
            for h in range(6):
                MM(PSG[:, h // 4, (h % 4) * 128:(h % 4 + 1) * 128], onesf[:], WSF[:, h, :], True, True, [b_onesf, b_WSF], [b_PSG[h // 4]])
            for c in range(3):
                for hh in range(2):
                    h = 2 * c + hh
                    sl = slice(64 * hh, 64 * hh + 64)
                    STT(CC[sl, li, c, :], PSG[sl, h // 4, (h % 4) * 128:(h % 4 + 1) * 128], PV[sl, li, PV_SB + c:PV_SB + c + 1], SBB[sl, c, :],
                        ALU.mult, ALU.add, [b_PSG[h // 4], b_PV, b_SBB], [b_CCl[li]])
        stage('s_sgu')

        stream = []
        for ti in range(ntiles):
            for li in range(L):
                for j in range(14):
                    stream.append(("A", win_d[li, j], win_b[li, j], b_win[li][j], ti == 0))
                for m in range(8):
                    stream.append(("A", wout_d[li, m], wout_b[li, m], b_wout[li][m], ti == 0))
                for j in range(NJ):
                    stream.append(("U", up_d[li, j], up_b[li, j], b_up[li][j], ti == 0))
                order = [("D", m) for m in range(6)] + [("E", m) for m in range(6)] + [("D", 6), ("E", 6), ("D", 7), ("E", 7)]
                for r_, m in order:
                    if r_ == "D":
                        stream.append(("D", dn_d[li, m, :, 0:16, :], dn_b[li, m, :, 0:16, :], b_dn[li][m], ti == 0))
                    else:
                        stream.append(("E", dn_d[li, m, :, 16:NJ, :], dn_b[li, m, :, 16:NJ, :], b_dne[li][m], ti == 0))
        rings = {"A": (WA, b_WA, 4), "U": (WU, b_WU, 3), "D": (WD, b_WD, 2), "E": (WE, b_WE, 2)}
        issued = {"A": 0, "U": 0, "D": 0, "E": 0}
        consumed = {"A": 0, "U": 0, "D": 0, "E": 0}
        nxt = [0]

        def pump():
            while nxt[0] < len(stream):
                r, src32, scr, sb, first = stream[nxt[0]]
                ten, bl, ns = rings[r]
                if issued[r] - consumed[r] >= ns:
                    break
                s = issued[r] % ns
                if first:
                    DMA(P.pool, "wc%s%d" % (r, s), ten[:, s], src32, writes=[bl[s]])
                    if ntiles > 1:
                        DMA(P.sp, "ws%s%d" % (r, s), scr, ten[:, s], reads=[bl[s]], writes=[sb])
                else:
                    DMA(P.sp, "w%s%d" % (r, s), ten[:, s], scr, reads=[sb], writes=[bl[s]])
                issued[r] += 1
                nxt[0] += 1

        def wslot(r):
            ten, bl, ns = rings[r]
            assert consumed[r] < issued[r], "weight chunk not issued"
            s = consumed[r] % ns
            return ten[:, s], bl[s]

        def wdone(r):
            consumed[r] += 1
            pump()

        sc_i = [0]

        def preload(func):
            ACT(DUM[:, 1:2], DUM[:, 0:1], func, [b_DUM], [b_DUM])

        def tm_sum(chunks, fp32=False):
            col = 4 * (sc_i[0] % 16)
            sc_i[0] += 1
            oc = onesf[:, 0:1] if fp32 else onesb[:, 0:1]
            bo = b_onesf if fp32 else b_onesb
            n = len(chunks)
            for tb in range(4):
                for i, (ap, b) in enumerate(chunks):
                    MM(PST[:, 0, col + tb:col + tb + 1], ap[:, tb * 128:(tb + 1) * 128], oc, i == 0, i == n - 1, [b, bo], [b_PST[0]],
                       signal=(tb == 3 and i == n - 1))
            return col

        def small(col, scale, bias):
            a, ba = r_S4.get()
            ACT(a, PST[:, 0, col:col + 4], AF.Identity, [b_PST[0]], [ba], scale=scale, bias=bias)
            return a, ba

        def rpow(a, ba):
            r, br = r_S4.get()
            TT(P.pool, r, a, NH4[:], ALU.pow, [ba, b_NH4], [br])
            return r, br

        def bcast(a, ba):
            d4, bd4 = r_D4.get()
            TT(P.dve, d4, identf[:].unsqueeze(1).broadcast_to([128, 4, 128]), a.unsqueeze(2).broadcast_to([128, 4, 128]), ALU.mult,
               [b_identf, ba], [bd4])
            pb = gbank()
            MM(PSG[:, pb, :], onesf[:], d4.rearrange("p a b -> p (a b)"), True, True, [b_onesf, bd4], [b_PSG[pb]])
            return PSG[:, pb, :], b_PSG[pb]

        def rstd_of(chunks, scale, eps):
            n = len(chunks)
            for i, (ap, b) in enumerate(chunks):
                MM(PST[:, 1, :], onesb[:], ap, i == 0, i == n - 1, [b_onesb, b], [b_PST[1]], signal=(i == n - 1))
            r, br = r_SS.get()
            ACT(r, PST[:, 1, :], AF.Ln, [b_PST[1]], [br], scale=scale, bias=eps)
            ACT(r, r, AF.Exp, [br], [br], scale=-0.5)
            return r, br

        def prenorm(li, gcol, shcol, have_sq, xa=False, nxt=None):
            SQ8 = YT
            X, bX = (XA, b_XA) if xa else (XT, b_XT)
            if not have_sq:
                for c in range(8):
                    ACT(SQ8[:, c, :], X[:, c, :], AF.Square, [bX[c]], [b_YT[c]])
            r, br = rstd_of([(SQ8[:, c, :], b_YT[c]) for c in range(8)], 1.0 / D_MODEL, EPS)
            for c in range(8):
                t, bt = r_TMP.get()
                TT(P.dve, t, X[:, c, :], r, ALU.mult, [bX[c], br], [bt])
                ACT(HT[:, c, :], t, AF.Identity, [bt, b_DV, b_MOD], [b_HT[c]],
                    scale=DV[:, li, gcol + c:gcol + c + 1], bias=MOD[:, li, shcol + c:shcol + c + 1])
            if xa:
                for c in range(8):
                    ACT(XT[:, c, :], XA[:, c, :], AF.Copy, [b_XA[c]], [b_XT[c]])

        def postnorm_residual(li, ggcol, eps, produce, sq_next):
            SQ8 = HT
            for m in range(8):
                pb = produce(m)
                CP(P.dve, Y32[:, m, :], PSG[:, pb, :], [b_PSG[pb]], [b_Y32[m]])
                ACT(SQ8[:, m, :], Y32[:, m, :], AF.Square, [b_Y32[m]], [b_HT[m]])
            r, br = rstd_of([(SQ8[:, m, :], b_HT[m]) for m in range(8)], 1.0 / D_MODEL, eps)
            for m in range(8):
                t, bt = r_TMP.get()
                STT(t, Y32[:, m, :], DV[:, li, ggcol + m:ggcol + m + 1], r, ALU.mult, ALU.mult, [b_Y32[m], b_DV, br], [bt])
                TT(P.dve, XT[:, m, :], XT[:, m, :], t, ALU.add, [b_XT[m], bt], [b_XT[m]])
                if sq_next:
                    ACT(YT[:, m, :], XT[:, m, :], AF.Square, [b_XT[m]], [b_YT[m]])

        def mixer(ti, li, have_sq, xa=False):
            prenorm(li, 0, 0, have_sq, xa, AF.Tanh)
            dump("h1", HT[:], b_HT, [128, 8, T])
            hcl = HC[:, li]
            zcl = ZC[:, li]
            stage('m_prenorm')
            if ti > 0:
                ACT(hcl[:, :, 0:30], hcl[:, :, T:T + 30], AF.Copy, b_HC[li], b_HC[li])
                ACT(zcl[:, :, 0:15], zcl[:, :, T:T + 15], AF.Copy, b_ZC[li], b_ZC[li])
            def inproj_fm():
                w, bw = wslot("A")
                pb = gbank()
                for kc in range(8):
                    MM(PSG[:, pb, :], w[:, kc, :], HT[:, kc, :], kc == 0, kc == 7, [bw, b_HT[kc]], [b_PSG[pb]], signal=(kc == 7))
                wdone("A")
                return pb

            csq = []

            def conv_chunk(c):
                dg, bdg = dgs[c]
                pb = gbank()
                for k in range(31):
                    MM(PSG[:, pb, :], dg[:, k, :], hcl[:, c, k:k + T], k == 0, k == 30, [bdg, b_HC[li][c]], [b_PSG[pb]], signal=(k == 30))
                ACT(Y32[:, 3 + c, :], PSG[:, pb, :], AF.Identity, [b_PSG[pb], b_PV], [b_Y32[3 + c]], bias=PV[:, li, PV_CB + c:PV_CB + c + 1])
                q, bq = r_SQ.get()
                ACT(q, Y32[:, 3 + c, :], AF.Square, [b_Y32[3 + c]], [bq])
                csq.append((q, bq))

            def v_path_and_pool():
                def pool_sums():
                    TT(P.dve, PA[:, :, 1:527], zcl[:, :, 1:527], zcl[:, :, 0:526], ALU.add, b_ZC[li], [b_PA])
                    TT(P.dve, PB[64:128, 0, 3:527], PA[64:128, 0, 3:527], PA[64:128, 0, 1:525], ALU.add, [b_PA], [b_PB])
                    TT(P.dve, PB[:, 1, 3:527], PA[:, 1, 3:527], PA[:, 1, 1:525], ALU.add, [b_PA], [b_PB])
                    TT(P.dve, PA[:, 1, 7:527], PB[:, 1, 7:527], PB[:, 1, 3:523], ALU.add, [b_PB, b_PA], [b_PA])
                    TT(P.dve, PB[64:128, 1, 15:527], PA[64:128, 1, 15:527], PA[64:128, 1, 7:519], ALU.add, [b_PA, b_PB], [b_PB])
                st = VST[:]
                for tb in range(4):
                    v, bv = vvs[tb]
                    v3 = v.rearrange("p (h d) -> p h d", h=6)
                    P.op(P.dve, lambda e, v3=v3, tb=tb: e.tensor_reduce(out=st[:, 0, tb * 6:tb * 6 + 6], in_=v3, axis=AX.X, op=ALU.add), [bv], [b_VST])
                    ACT(VSQ[:], v, AF.Square, [bv], [b_VSQ])
                    P.op(P.dve, lambda e, tb=tb: e.tensor_reduce(out=st[:, 1, tb * 6:tb * 6 + 6], in_=VSQ[:].rearrange("p (h d) -> p h d", h=6), axis=AX.X,
                                                                  op=ALU.add), [b_VSQ], [b_VST])
                TS(st[:, 2, :], st[:, 0, :], 1.0 / 64, None, ALU.mult, None, [b_VST], [b_VST])
                TT(P.dve, st[:, 3, :], st[:, 2, :], st[:, 2, :], ALU.mult, [b_VST], [b_VST])
                STT(st[:, 3, :], st[:, 1, :], 1.0 / 64, st[:, 3, :], ALU.mult, ALU.subtract, [b_VST], [b_VST])
                TS(st[:, 3, :], st[:, 3, :], EPS, None, ALU.add, None, [b_VST], [b_VST])
                TT(P.pool, st[:, 4, :], st[:, 3, :], NH24[:], ALU.pow, [b_VST, b_NH24], [b_VST2])
                pool_sums()
                stage('m_v')
                srcs = [(PA, 0, 0), (PB, 0, 1), (PA, 1, 0), (PB, 1, 1)]
                for g in range(4):
                    ten, c, hh = srcs[g]
                    sl = slice(64 * hh, 64 * hh + 64)
                    STT(DD[sl, c, :], ten[sl, c, 15:527], INVW[sl, c:c + 1], zcl[sl, c, 15:527], ALU.mult, ALU.subtract,
                        [b_PA, b_PB, b_INVW, b_ZC[li][c]], [b_DD[c]])
                    if ti == 0:
                        t, bt = r_TMP.get()
                        TT(P.dve, t[sl, 0:16], ten[sl, c, 15:31], IC0[sl, c, :], ALU.mult, [b_PA, b_PB, b_IC0], [bt])
                        TT(P.dve, DD[sl, c, 0:16], t[sl, 0:16], zcl[sl, c, 15:31], ALU.subtract, [bt, b_ZC[li][c]], [b_DD[c]])

                for tb in range(4):
                    v, bv = vvs[tb]
                    v3 = v.rearrange("p (h d) -> p h d", h=6)
                    TT(P.dve, v3, v3, st[:, 2, tb * 6:tb * 6 + 6].unsqueeze(2).broadcast_to([128, 6, 64]), ALU.subtract, [bv, b_VST], [bv])
                    TT(P.dve, VN[:, tb, :].rearrange("p (h d) -> p h d", h=6), v3,
                       st[:, 4, tb * 6:tb * 6 + 6].unsqueeze(2).broadcast_to([128, 6, 64]), ALU.mult, [bv, b_VST2], [b_VN[tb]])

            def pool_mm():
                for c in range(2):
                    pb = gbank()
                    MM(PSG[:, pb, :], PWb[:, li, c, :], DD[:, c, :], True, True, [b_PWb, b_DD[c]], [b_PSG[pb]])
                    ACT(Y32[:, 6 + c, :], PSG[:, pb, :], AF.Identity, [b_PSG[pb], b_PV], [b_Y32[6 + c]], scale=PV[:, li, PV_PS + c:PV_PS + c + 1])
                dump("yc", Y32[:, 6:8, :], b_Y32[6:8], [128, 2, T])
                stage('m_pool')

            def sgu_mix():
                for c in range(3):
                    pb = gbank()
                    for tb in range(4):
                        for hh in range(2):
                            h = 2 * c + hh
                            MM(PSG[64 * hh:64 * hh + 64, pb, tb * 128:(tb + 1) * 128], VN[:, tb, h * 64:(h + 1) * 64], WST[:, li, h, :], True, True,
                               [b_VN[tb], b_WSTl[li]], [b_PSG[pb]], signal=(tb == 3 and hh == 1))
                    t, bt = r_TMP.get()
                    STT(t.rearrange("p (b t) -> p b t", b=4), PSG[:, pb, :].rearrange("p (b t) -> p b t", b=4), PV[:, li, PV_SG + c:PV_SG + c + 1],
                        CC[:, li, c, :].unsqueeze(1).broadcast_to([128, 4, 128]), ALU.mult, ALU.add, [b_PSG[pb], b_PV, b_CCl[li]], [bt])
                    TT(P.dve, Y32[:, c, :], UU[:, c, :], t, ALU.mult, [b_UU[c], bt], [b_Y32[c]])
                dump("ya", Y32[:, 0:3, :], b_Y32[0:3], [128, 3, T])
                stage('m_sgu')

            for c in range(3):
                pg = inproj_fm()
                th, bth = r_TH.get()
                ACT(th, PSG[:, pg, :], AF.Tanh, [b_PSG[pg]], [bth], scale=0.5)
                pa = inproj_fm()
                STT(hcl[:, c, 30:30 + T], th, 1.0, PSG[:, pa, :], ALU.add, ALU.mult, [bth, b_PSG[pa]], [b_HC[li][c]])
            stage('m_glu')
            dgs = []
            for c in range(3):
                if ti == 0:
                    TT(P.dve, DG[:, c], identh[:].unsqueeze(1).broadcast_to([128, 31, 128]),
                       PV[:, li, PV_CW + 31 * c:PV_CW + 31 * c + 31].unsqueeze(2).broadcast_to([128, 31, 128]), ALU.mult,
                       [b_identh, b_PV], [b_DG[c]])
                    if ntiles > 1:
                        DMA(P.sp, "dgw%d" % c, dg_b[li, c], DG[:, c], reads=[b_DG[c]], writes=[b_dgb[li][c]])
                else:
                    DMA(P.sp, "dg%d" % c, DG[:, c], dg_b[li, c], reads=[b_dgb[li][c]], writes=[b_DG[c]])
                dgs.append((DG[:, c], b_DG[c]))
            vb = [gbank() for _ in range(4)]
            for jv in range(3):
                w, bw = wslot("A")
                for tb in range(4):
                    for kc in range(8):
                        MM(PSG[:, vb[tb], jv * 128:(jv + 1) * 128], HT[:, kc, tb * 128:(tb + 1) * 128], w[:, kc, :], kc == 0, kc == 7,
                           [bw, b_HT[kc]], [b_PSG[vb[tb]]], signal=(kc == 7))
                wdone("A")
            vvs = []
            for tb in range(4):
                v = VV4[:, tb, :]
                bv = b_VV4[tb]
                ACT(v, PSG[:, vb[tb], 0:384], AF.Gelu, [b_PSG[vb[tb]]], [bv])
                vvs.append((v, bv))
            for c in range(2):
                pb = inproj_fm()
                ACT(zcl[:, c, 15:15 + T], PSG[:, pb, :], AF.Copy, [b_PSG[pb]], [b_ZC[li][c]])
            stage('m_inproj')
            conv_chunk(0)
            v_path_and_pool()
            conv_chunk(1)
            conv_chunk(2)
            pm = gbank()
            for c in range(3):
                MM(PSG[:, pm, :], onesf[:], Y32[:, 3 + c, :], c == 0, c == 2, [b_onesf, b_Y32[3 + c]], [b_PSG[pm]], signal=(c == 2))
            for i, (q, bq) in enumerate(csq):
                MM(PST[:, 1, :], onesb[:], q, i == 0, i == 2, [b_onesb, bq], [b_PST[1]], signal=(i == 2))
            m2, bm2 = r_TMP.get()
            ACT(m2, PSG[:, pm, :], AF.Square, [b_PSG[pm]], [bm2], scale=1.0 / 384)
            ACT(CMN[:], PSG[:, pm, :], AF.Identity, [b_PSG[pm]], [b_CMN], scale=-1.0 / 384)
            rs, brs = CRS[:], b_CRS
            STT(rs, PST[:, 1, :], 1.0 / 384, m2, ALU.mult, ALU.subtract, [b_PST[1], bm2], [brs])
            ACT(rs, rs, AF.Ln, [brs], [brs], bias=EPS)
            ACT(rs, rs, AF.Exp, [brs], [brs], scale=-0.5)
            for c in range(3):
                pb = inproj_fm()
                ACT(UU[:, c, :], PSG[:, pb, :], AF.Gelu, [b_PSG[pb]], [b_UU[c]])
            stage('m_u')
            pool_mm()
            sgu_mix()

            def branch_norm(c0, c1, n, eps):
                sqs = []
                for c in range(c0, c1):
                    q, bq = r_SQ.get()
                    ACT(q, Y32[:, c, :], AF.Square, [b_Y32[c]], [bq])
                    sqs.append((q, bq))
                r, br = rstd_of(sqs, 1.0 / n, eps)
                for c in range(c0, c1):
                    STT(YT[:, c, :], Y32[:, c, :], PV[:, li, PV_BR + c:PV_BR + c + 1], r, ALU.mult, ALU.mult, [b_Y32[c], b_PV, br], [b_YT[c]])
            for c in range(3):
                t, bt = r_TMP.get()
                TT(P.dve, t, Y32[:, 3 + c, :], CMN[:], ALU.add, [b_Y32[3 + c], b_CMN], [bt])
                TT(P.dve, t, t, rs, ALU.mult, [bt, brs], [bt])
                th, bth = r_TH.get()
                ACT(th, t, AF.Tanh, [bt, b_DV], [bth], scale=DV[:, li, 32 + c:33 + c], bias=DV[:, li, 35 + c:36 + c])
                l_, bl_ = r_SS.get()
                ACT(l_, t, AF.Identity, [bt, b_PV], [bl_], scale=PV[:, li, PV_CNG + c:PV_CNG + c + 1], bias=PV[:, li, PV_CNB + c:PV_CNB + c + 1])
                STT(Y32[:, 3 + c, :], th, 1.0, l_, ALU.add, ALU.mult, [bth, bl_], [b_Y32[3 + c]])
            dump("yb", Y32[:, 3:6, :], b_Y32[3:6], [128, 3, T])
            stage('m_conv')
            branch_norm(3, 6, 384, 4 * EPS)
            branch_norm(6, 8, 256, EPS)
            branch_norm(0, 3, 384, EPS)
            dump("yT", YT[:], b_YT, [128, 8, T])
            stage('m_brnorm')

            def produce(m):
                w, bw = wslot("A")
                pb = gbank()
                for kc in range(8):
                    MM(PSG[:, pb, :], w[:, kc, :], YT[:, kc, :], kc == 0, kc == 7, [bw, b_YT[kc]], [b_PSG[pb]], signal=(kc == 7))
                wdone("A")
                return pb
            postnorm_residual(li, 8, EPS, produce, True)
            stage('m_out')
            dump("x1", XT[:], b_XT, [128, 8, T])

        def ffn(ti, li, last):
            prenorm(li, 16, 24, True, False, AF.Gelu)
            dump("h2", HT[:], b_HT, [128, 8, T])
            ztl = ZT[:, li]
            fw = lambda k, j: PV[:, li, PV_FW + 44 * k + j:PV_FW + 44 * k + j + 1]
            fwv = lambda k: PV[:, li, PV_FW + 44 * k:PV_FW + 44 * k + 44].rearrange("p (g j) -> p j g", g=2)
            TT(P.dve, CORR[:, :, :, 0], ztl[:, :, :, 1], fwv(1), ALU.mult, [b_ZT[li], b_PV], [b_CORR])
            TT(P.dve, CT[:, :, :, 0], ztl[:, :, :, 0], fwv(0), ALU.mult, [b_ZT[li], b_PV], [b_CT])
            TT(P.dve, CORR[:, :, :, 0], CORR[:, :, :, 0], CT[:, :, :, 0], ALU.add, [b_CORR, b_CT], [b_CORR])
            TT(P.dve, CORR[:, :, :, 1], ztl[:, :, :, 1], fwv(0), ALU.mult, [b_ZT[li], b_PV], [b_CORR])
            def fin_act(p):
                gl, bgl = r_GL.get()
                ACT(gl, p[1][:, 0, :], AF.Gelu, [p[2]], [bgl])
                return gl, bgl

            def fin_dve(p, glp):
                TT(P.dve, AT[:, p[0], :], glp[0], p[1][:, 1, :], ALU.mult, [glp[1], p[2]], [b_AT[p[0]]])
            pend = None
            for j in range(NJ):
                w, bw = wslot("U")
                p0 = gpair()
                for g in range(2):
                    for kc in range(8):
                        MM(PSG[:, p0 + g, :], w[:, kc, g * 128:(g + 1) * 128], HT[:, kc, :], kc == 0, kc == 7, [bw, b_HT[kc]], [b_PSG[p0 + g]],
                           signal=(kc == 7))
                wdone("U")
                acc, bacc = r_ACC.get()
                pbs = [b_PSG[p0], b_PSG[p0 + 1]]
                for g in range(2):
                    jj = j + 22 * g
                    ACT(acc[:, g, :], PSG[:, p0 + g, :], AF.Identity, [b_PSG[p0 + g], b_PV], [bacc], scale=fw(2, jj),
                        bias=PV[:, li, PV_FB + jj:PV_FB + jj + 1])
                if pend is not None:
                    glp = fin_act(pend)
                for g in range(2):
                    jj = j + 22 * g
                    STT(acc[:, g, 1:T], PSG[:, p0 + g, 0:T - 1], fw(1, jj), acc[:, g, 1:T], ALU.mult, ALU.add, [b_PSG[p0 + g], b_PV, bacc], [bacc])
                    STT(acc[:, g, 2:T], PSG[:, p0 + g, 0:T - 2], fw(0, jj), acc[:, g, 2:T], ALU.mult, ALU.add, [b_PSG[p0 + g], b_PV, bacc], [bacc])
                CP(P.dve, ztl[:, j, :, :], PSG[:, p0:p0 + 2, T - 2:T], pbs, [b_ZT[li]])
                TT(P.dve, acc[:, :, 0:2], acc[:, :, 0:2], CORR[:, j, :, :], ALU.add, [bacc, b_CORR], [bacc])
                if pend is not None:
                    fin_dve(pend, glp)
                pend = (j, acc, bacc)
            glp = fin_act(pend)
            fin_dve(pend, glp)
            stage('f_up')
            dump("aT", AT[:, 0:4, :], b_AT[0:4], [128, 4, T])

            first_pair = pair_i[0] % 3
            dbanks = [(2 * ((first_pair + i // 2) % 3) + i % 2) for i in range(6)]
            for m in range(6):
                w, bw = wslot("D")
                for kc in range(16):
                    MM(PSG[:, dbanks[m], :], w[:, kc, :], AT[:, kc, :], kc == 0, False, [bw, b_AT[kc]], [b_PSG[dbanks[m]]], signal=(kc == 15))
                wdone("D")

            def produce(m):
                if m < 6:
                    pb = dbanks[m]
                else:
                    pb = dbanks[m - 6]
                    w, bw = wslot("D")
                    for kc in range(16):
                        MM(PSG[:, pb, :], w[:, kc, :], AT[:, kc, :], kc == 0, False, [bw, b_AT[kc]], [b_PSG[pb]], signal=(kc == 15))
                    wdone("D")
                w, bw = wslot("E")
                for kc in range(16, NJ):
                    MM(PSG[:, pb, :], w[:, kc - 16, :], AT[:, kc, :], False, kc == NJ - 1, [bw, b_AT[kc]], [b_PSG[pb]], signal=(kc == NJ - 1))
                wdone("E")
                return pb
            postnorm_residual(li, 24, EPS, produce, not last)
            dump("x2", XT[:], b_XT, [128, 8, T])

        pump()
        for ti in range(ntiles):
            if ti == 0:
                DMA(P.sp, "xin", XT[:], x_fm[ti], writes=b_XT)
            for li in range(L):
                mixer(ti, li, li > 0, xa=(ti > 0 and li == 0))
                if li == L - 1 and ti + 1 < ntiles:
                    DMA(P.sp, "xin", XA, x_fm[ti + 1], writes=b_XA)
                ffn(ti, li, li == L - 1)
            DMA(P.sp, "xout", out_fm[ti], XT[:], reads=b_XT)

    except _Stop:
        pass
    P.barrier()
    print('sbuf bytes remaining', nc.sbuf_bytes_remaining)
    P.emit()
    return nc, dbg_out


def _fm(v, n):
    return np.ascontiguousarray(np.asarray(v, np.float32).reshape(n, 128).T)


def prep_shared(inp, layers):
    g = lambda k: np.asarray(inp[k], np.float32)
    pv, modc, winc, woutc, upc, dnc, wst, sgub, pwbd = [], [], [], [], [], [], [], [], []
    for l in layers:
        cols = [_fm(g("mix_pre_g")[l], 8), _fm(g("mix_post_g")[l], 8), _fm(g("branch_g")[l], 8), _fm(g("ffn_pre_g")[l], 8),
                _fm(g("ffn_post_g")[l], 8), _fm(g("sgu_norm_g")[l], 3), _fm(g("sgu_norm_b")[l], 3), _fm(g("conv_b")[l], 3),
                _fm(g("conv_norm_g")[l], 3), _fm(g("conv_norm_b")[l], 3), _fm(g("pool_scale")[l], 2)]
        cw = g("conv_w")[l].reshape(31, 3, 128).transpose(2, 1, 0).reshape(128, 93)
        fw = g("ffn_conv_w")[l].reshape(3, 44, 128).transpose(2, 0, 1).reshape(128, 132)
        cols += [cw, fw, _fm(g("ffn_conv_b")[l], 44), _fm(g("mod_b")[l], 48)]
        p = np.concatenate(cols, axis=1)
        assert p.shape == (128, NPV)
        pv.append(p)
        modc.append(g("mod_w")[l].reshape(8, 128, 12, 4, 128).transpose(2, 1, 3, 0, 4))
        winc.append(g("w_in")[l].reshape(8, 128, 14, 128).transpose(2, 1, 0, 3)[IN_ORDER])
        woutc.append(g("w_out")[l].reshape(8, 128, 8, 128).transpose(2, 1, 0, 3))
        upc.append(g("ffn_up")[l].reshape(8, 128, 2, NJ, 128).transpose(3, 1, 0, 2, 4).reshape(NJ, 128, 8, 256))
        dnc.append(g("ffn_down")[l].reshape(NJ, 128, 8, 128).transpose(2, 1, 0, 3))
        wst.append(g("sgu_w")[l].transpose(2, 0, 1))
        sgub.append(g("sgu_b")[l])
        pw = g("pool_w")[l]
        bd = np.zeros((128, 2, 128), np.float32)
        for c in range(2):
            bd[0:64, c, 0:64] = pw[2 * c]
            bd[64:128, c, 64:128] = pw[2 * c + 1]
        pwbd.append(bd)
    ca = lambda lst: np.ascontiguousarray(np.stack(lst, 0), dtype=np.float32)
    s_idx = np.arange(128)
    tri = (s_idx[None, :] >= s_idx[:, None]).astype(np.float32)
    wins = np.array([2, 4, 8, 16], np.float32)
    ic0 = np.zeros((128, 2, 16), np.float32)
    invw = np.zeros((128, 2), np.float32)
    for c in range(2):
        for hh in range(2):
            w_ = wins[2 * c + hh]
            ic0[64 * hh:64 * hh + 64, c, :] = 1.0 / np.minimum(np.arange(1, 17, dtype=np.float32), w_)
            invw[64 * hh:64 * hh + 64, c] = 1.0 / w_
    return {"pv": ca(pv), "mod_c": ca(modc), "w_in_c": ca(winc), "w_out_c": ca(woutc), "up_c": ca(upc), "down_c": ca(dnc),
            "wst": ca(wst), "sgub": ca(sgub), "pwbd": ca(pwbd), "identh": (0.5 * np.eye(128)).astype(np.float32), "identf": np.eye(128, dtype=np.float32), "tri": tri,
            "ic0": ic0, "invw": invw}


def x_to_fm(xb, ntiles=NT):
    return np.ascontiguousarray(np.asarray(xb, np.float32).reshape(ntiles, T, 8, 128).transpose(0, 3, 2, 1))


def fm_to_x(o):
    nt = o.shape[0]
    return np.ascontiguousarray(o.transpose(0, 3, 2, 1).reshape(nt * T, D_MODEL))


_CACHE = {}


def _get_prog(layers):
    key = tuple(layers)
    if key not in _CACHE:
        _CACHE[key] = build(list(range(len(layers))))[0]
    return _CACHE[key]


def run_layers(x, inputs, layers):
    shared = prep_shared(inputs, layers)
    c = np.asarray(inputs["c"], np.float32)
    nc = _get_prog(layers)
    in_maps = []
    for b in range(BATCH):
        m = dict(shared)
        m["x_fm"] = x_to_fm(x[b])
        m["c_fm"] = _fm(c[b], 8)
        in_maps.append(m)
    res = run_bass_kernel_spmd(nc, in_maps, core_ids=list(range(BATCH)))
    return np.stack([fm_to_x(res.results[b]["out_fm"]) for b in range(BATCH)], 0)


FUSED = True


def kernel(**inputs):
    x = np.asarray(inputs["x"], np.float32)
    if FUSED:
        out = run_layers(x, inputs, list(range(DEPTH)))
    else:
        out = x
        for l in range(DEPTH):
            out = run_layers(out, inputs, [l])
    return out.astype(np.float32)
```

```python
import contextlib
import numpy as np
import concourse.bass as bass
import concourse.mybir as mybir
from concourse.bass_utils import run_bass_kernel_spmd

F32 = mybir.dt.float32
BF16 = mybir.dt.bfloat16
AF = mybir.ActivationFunctionType
ALU = mybir.AluOpType
AX = mybir.AxisListType

D_MODEL = 1024
BATCH = 8
SEQ = 4096
DEPTH = 2
T = 512
NT = SEQ // T
D_FF = 2816
NJ = D_FF // 128
EPS = 1e-6
NPV = 374
IN_ORDER = [9, 6, 10, 7, 11, 8, 3, 4, 5, 12, 13, 0, 1, 2]

PV_PRE, PV_POST, PV_BR, PV_FPRE, PV_FPOST = 0, 8, 16, 24, 32
PV_SG, PV_SB, PV_CB, PV_CNG, PV_CNB, PV_PS = 40, 43, 46, 49, 52, 55
PV_CW, PV_FW, PV_FB, PV_MB = 57, 150, 282, 326


class Buf:
    __slots__ = ("name", "w", "r", "al", "excl")

    def __init__(self, name):
        self.name = name
        self.w = None
        self.r = []
        self.al = ()
        self.excl = False


class Eng:
    def __init__(self, name, attr):
        self.name = name
        self.attr = attr
        self.ops = []
        self.count = 0
        self.waited = {}


class Prog:
    def __init__(self, nc):
        self.nc = nc
        self.es = contextlib.ExitStack()
        self.pe = Eng("pe", "tensor")
        self.act = Eng("act", "scalar")
        self.dve = Eng("dve", "vector")
        self.pool = Eng("pool", "gpsimd")
        self.sp = Eng("sp", "sync")
        self.engs = [self.pe, self.act, self.dve, self.pool, self.sp]
        self.sems = {}
        for e in self.engs:
            self.sems[e.name] = self.es.enter_context(nc.semaphore("s_" + e.name))
        self.dma_counts = {}
        self.nbuf = 0

    def sbuf(self, name, shape, dt):
        return self.es.enter_context(self.nc.sbuf_tensor(name, list(shape), dt))

    def psum(self, name, shape, dt=F32):
        return self.es.enter_context(self.nc.psum_tensor(name, list(shape), dt))

    def buf(self, name=None):
        self.nbuf += 1
        return Buf(name or "b%d" % self.nbuf)

    def bufs(self, n, name=None):
        return [self.buf(None if name is None else "%s%d" % (name, i)) for i in range(n)]

    def dma_sem(self, name):
        key = "d_" + name
        self.sems[key] = self.es.enter_context(self.nc.semaphore(key))
        self.dma_counts[key] = 0
        return key

    def _collect(self, eng, reads, writes):
        waits = {}

        def need(tok, skip_same):
            if tok is None:
                return
            k, v = tok
            if skip_same and k == eng.name:
                return
            if eng.waited.get(k, 0) >= v:
                return
            if waits.get(k, 0) < v:
                waits[k] = v

        pe_same = eng.name == "pe"
        for b0 in reads:
            for b in (b0,) + tuple(b0.al):
                need(b.w, False)
                if b.excl:
                    for t in b.r:
                        need(t, True)
        for b0 in writes:
            for b in (b0,) + tuple(b0.al):
                need(b.w, pe_same)
                for t in b.r:
                    need(t, pe_same)
        for k, v in waits.items():
            eng.waited[k] = v
        return sorted(waits.items())

    def op(self, eng, fn, reads=(), writes=(), signal=True):
        waits = self._collect(eng, reads, writes)
        tok = (eng.name, eng.count + 1)
        for b in reads:
            b.r.append(tok)
        for b in writes:
            b.w = tok
            b.r = []
        inc = None
        if signal:
            eng.count += 1
            inc = (eng.name, 1)
        eng.ops.append((waits, fn, inc))

    def dma(self, eng, semkey, out, in_, reads=(), writes=(), **kw):
        waits = self._collect(eng, reads, writes)
        self.dma_counts[semkey] += 16
        tok = (semkey, self.dma_counts[semkey])
        for b in reads:
            b.r.append(tok)
        for b in writes:
            b.w = tok
            b.r = []

        def fn(e, out=out, in_=in_, kw=kw):
            return e.dma_start(out=out, in_=in_, **kw)
        eng.ops.append((waits, fn, (semkey, 16)))
        return tok

    def wait_tok(self, eng, tok):
        k, v = tok
        if eng.waited.get(k, 0) >= v:
            return
        eng.waited[k] = v
        eng.ops.append(([(k, v)], None, None))

    def barrier(self):
        toks = [(e.name, e.count) for e in self.engs if e.count > 0]
        toks += [(k, v) for k, v in self.dma_counts.items() if v > 0]
        for e in self.engs:
            for t in toks:
                if t[0] != e.name:
                    self.wait_tok(e, t)

    def emit(self):
        nc = self.nc
        with nc.Block() as block:
            for e in self.engs:
                if not e.ops:
                    continue

                def body(h, e=e):
                    for waits, fn, inc in e.ops:
                        for k, v in waits:
                            h.wait_ge(self.sems[k], v)
                        if fn is not None:
                            ins = fn(h)
                            if inc is not None:
                                ins.then_inc(self.sems[inc[0]], inc[1])
                getattr(block, e.attr)(body)
        self.es.close()


class Rot:
    def __init__(self, items):
        self.items = items
        self.i = 0

    def get(self):
        it = self.items[self.i % len(self.items)]
        self.i += 1
        return it


class _Stop(Exception):
    pass


def build(layers, ntiles=NT, dbg=(), stop_after=None):
    nc = bass.Bass("TRN2", target_bir_lowering=False)
    P = Prog(nc)
    L = len(layers)
    dbg = set(dbg)
    dbg_out = {}

    def din(name, shape, dt=F32):
        return nc.dram_tensor(name, list(shape), dt, kind="ExternalInput").ap()

    x_fm = din("x_fm", [ntiles, 128, 8, T])
    out_fm = nc.dram_tensor("out_fm", [ntiles, 128, 8, T], F32, kind="ExternalOutput").ap()
    c_fm = din("c_fm", [128, 8])
    pv_d = din("pv", [L, 128, NPV])
    mod_d = din("mod_c", [L, 12, 128, 4, 8, 128])
    win_d = din("w_in_c", [L, 14, 128, 8, 128])
    wout_d = din("w_out_c", [L, 8, 128, 8, 128])
    up_d = din("up_c", [L, NJ, 128, 8, 256])
    dn_d = din("down_c", [L, 8, 128, NJ, 128])
    wst_d = din("wst", [L, 128, 6, 128])
    sgub_d = din("sgub", [L, 6, 128])
    pwbd_d = din("pwbd", [L, 128, 2, 128])
    ident_d = din("identh", [128, 128])
    identf_d = din("identf", [128, 128])
    tri_d = din("tri", [128, 128])
    ic0_d = din("ic0", [128, 2, 16])
    invw_d = din("invw", [128, 2])

    def dscr(name, shape):
        return nc.dram_tensor(name, list(shape), BF16, kind="Internal").ap()
    win_b = dscr("w_in_b", [L, 14, 128, 8, 128])
    wout_b = dscr("w_out_b", [L, 8, 128, 8, 128])
    up_b = dscr("up_b", [L, NJ, 128, 8, 256])
    dn_b = dscr("down_b", [L, 8, 128, NJ, 128])
    b_win = [P.bufs(14) for _ in range(L)]; b_wout = [P.bufs(8) for _ in range(L)]
    b_up = [P.bufs(NJ) for _ in range(L)]; b_dn = [P.bufs(8) for _ in range(L)]; b_dne = [P.bufs(8) for _ in range(L)]
    dg_b = dscr("dg_b", [L, 3, 128, 31, 128]); b_dgb = [P.bufs(3, "dgb%d_" % l) for l in range(L)]

    PV = P.sbuf("PV", [128, L, NPV], F32); b_PV = P.buf("PV")
    MOD = P.sbuf("MOD", [128, L, 48], F32); b_MOD = P.buf("MOD")
    DV = P.sbuf("DV", [128, L, 40], F32); b_DV = P.buf("DV")
    CF = P.sbuf("CF", [128, 8], F32); b_CF = P.buf("CF")
    CF2 = P.sbuf("CF2", [128, 8], F32); b_CF2 = P.buf("CF2")
    SCb = P.sbuf("SCb", [128, 8], BF16); b_SCb = P.buf("SCb")
    WST = P.sbuf("WST", [128, L, 6, 128], BF16); b_WSTl = P.bufs(L, "WST")
    CC = P.sbuf("CC", [128, L, 3, 128], F32); b_CCl = P.bufs(L, "CC")
    PWb = P.sbuf("PWb", [128, L, 2, 128], BF16); b_PWb = P.buf("PWb")
    onesb = P.sbuf("onesb", [128, 128], BF16); b_onesb = P.buf("onesb")
    onesf = P.sbuf("onesf", [128, 128], F32); b_onesf = P.buf("onesf")
    identh = P.sbuf("identh_s", [128, 128], BF16); b_identh = P.buf("identh")
    identf = P.sbuf("identf_s", [128, 128], F32); b_identf = P.buf("identf")
    NH4 = P.sbuf("NH4", [128, 4], F32); b_NH4 = P.buf("NH4")
    DUM = P.sbuf("DUM", [128, 2], F32); b_DUM = P.buf("DUM")
    CRS = P.sbuf("CRS", [128, T], F32); b_CRS = P.buf("CRS")
    S4 = P.sbuf("S4", [128, 8, 4], F32); r_S4 = Rot([(S4[:, i, :], P.buf("S4_%d" % i)) for i in range(8)])
    CMN = P.sbuf("CMN", [128, T], F32); b_CMN = P.buf("CMN")
    r_D4 = None
    IC0 = P.sbuf("IC0", [128, 2, 16], F32); b_IC0 = P.buf("IC0")
    INVW = P.sbuf("INVW", [128, 2], F32); b_INVW = P.buf("INVW")
    TRI = P.sbuf("TRI", [128, 128], F32); b_TRI = P.buf("TRI")
    HC = P.sbuf("HC", [128, L, 3, 30 + T], BF16); b_HC = [P.bufs(3) for _ in range(L)]
    ZC = P.sbuf("ZC", [128, L, 2, 15 + T], F32); b_ZC = [P.bufs(2) for _ in range(L)]
    ZT = P.sbuf("ZT", [128, L, NJ, 2, 2], F32); b_ZT = P.bufs(L)
    CORR = P.sbuf("CORR", [128, NJ, 2, 2], F32); b_CORR = P.buf("CORR")
    CT = P.sbuf("CT", [128, NJ, 2, 2], F32); b_CT = P.buf("CT")
    XT = P.sbuf("XT", [128, 8, T], F32); b_XT = P.bufs(8, "XT")
    HT = P.sbuf("HT", [128, 8, T], BF16); b_HT = P.bufs(8, "HT")
    Y32 = P.sbuf("Y32", [128, 8, T], F32); b_Y32 = P.bufs(8, "Y32")
    YT = P.sbuf("YT", [128, 8, T], BF16); b_YT = P.bufs(8, "YT")
    SQ = P.sbuf("SQ", [128, 3, T], BF16); r_SQ = Rot([(SQ[:, i, :], P.buf("SQ%d" % i)) for i in range(3)])
    TMP = P.sbuf("TMP", [128, 3, T], F32); r_TMP = Rot([(TMP[:, i, :], P.buf("TMP%d" % i)) for i in range(3)])
    SS = P.sbuf("SS", [128, 2, T], F32); r_SS = Rot([(SS[:, i, :], P.buf("SS%d" % i)) for i in range(2)])
    AR = P.sbuf("AR", [128, NJ * T], BF16)
    AT = AR[:].rearrange("p (j t) -> p j t", j=NJ); b_AT = P.bufs(NJ, "AT")
    ARf = AR[:].bitcast(F32)
    UU = ARf[:, 0:1536].rearrange("p (c t) -> p c t", c=3); b_UU = P.bufs(3, "UU")
    TH = ARf[:, 1536:2560].rearrange("p (c t) -> p c t", c=2); r_TH = Rot([(TH[:, i, :], P.buf("TH%d" % i)) for i in range(2)])
    PA = ARf[:, 2560:2560 + 2 * 527].rearrange("p (c t) -> p c t", c=2); b_PA = P.buf("PA")
    PB = ARf[:, 3616:3616 + 2 * 527].rearrange("p (c t) -> p c t", c=2); b_PB = P.buf("PB")
    DD = AR[:, 2 * 4672:2 * 4672 + 2 * T].rearrange("p (c t) -> p c t", c=2); b_DD = P.bufs(2, "DD")
    mix_scr = b_UU + [it[1] for it in r_TH.items] + [b_PA, b_PB] + b_DD
    for b in mix_scr:
        b.al = tuple(b_AT)
    for b in b_AT:
        b.al = tuple(mix_scr)
    VV4 = P.sbuf("VV4", [128, 4, 384], F32); b_VV4 = P.bufs(4, "VV4")
    VSQ = P.sbuf("VSQ", [128, 384], F32); b_VSQ = P.buf("VSQ")
    VST = P.sbuf("VST", [128, 5, 24], F32); b_VST = P.buf("VST"); b_VST2 = P.buf("VST2"); b_VST3 = P.buf("VST3")
    NH24 = P.sbuf("NH24", [128, 24], F32); b_NH24 = P.buf("NH24")
    VN = P.sbuf("VN", [128, 4, 384], BF16); b_VN = P.bufs(4, "VN")
    ACC = P.sbuf("ACC", [128, 5, 2, T], F32); r_ACC = Rot([(ACC[:, i], P.buf("ACC%d" % i)) for i in range(5)])
    GL = P.sbuf("GL", [128, 3, T], F32); r_GL = Rot([(GL[:, i, :], P.buf("GL%d" % i)) for i in range(3)])
    DG = P.sbuf("DG", [128, 3, 31, 128], BF16); b_DG = P.bufs(3, "DG")
    XA = DG[:].rearrange("p a k n -> p (a k n)").bitcast(F32)[:, 0:8 * T].rearrange("p (c t) -> p c t", c=8); b_XA = P.bufs(8, "XA")
    for b in b_XA:
        b.al = tuple(b_DG)
    for b in b_DG:
        b.al = tuple(b_XA)
    WA = P.sbuf("WA", [128, 4, 8, 128], BF16); b_WA = P.bufs(4, "WA")
    WU = P.sbuf("WU", [128, 3, 8, 256], BF16); b_WU = P.bufs(3, "WU")
    WD = P.sbuf("WD", [128, 2, 16, 128], BF16); b_WD = P.bufs(2, "WD")
    WE = P.sbuf("WE", [128, 2, NJ - 16, 128], BF16); b_WE = P.bufs(2, "WE")
    PSG = P.psum("PSG", [128, 6, T]); b_PSG = P.bufs(6, "PSG")
    PST = P.psum("PST", [128, 2, T]); b_PST = P.bufs(2, "PST")
    for b in b_PSG + b_PST:
        b.excl = True
    gen_i = [0]

    def gbank():
        i = gen_i[0] % 6
        gen_i[0] += 1
        return i

    pair_i = [0]

    def gpair():
        i = pair_i[0] % 3
        pair_i[0] += 1
        return 2 * i
    st_i = [0]

    def sbank():
        i = st_i[0] % 2
        st_i[0] += 1
        return i

    def ACT(out, in_, func, reads, writes, scale=1.0, bias=0.0):
        P.op(P.act, lambda e: e.activation(out=out, in_=in_, func=func, scale=scale, bias=bias), reads, writes)

    def TT(eng, out, in0, in1, op, reads, writes):
        P.op(eng, lambda e: e.tensor_tensor(out=out, in0=in0, in1=in1, op=op), reads, writes)

    def STT(out, in0, scalar, in1, op0, op1, reads, writes):
        P.op(P.dve, lambda e: e.scalar_tensor_tensor(out=out, in0=in0, scalar=scalar, in1=in1, op0=op0, op1=op1), reads, writes)

    def TS(out, in0, s1, s2, op0, op1, reads, writes):
        if s2 is None:
            P.op(P.dve, lambda e: e.tensor_scalar(out=out, in0=in0, scalar1=s1, scalar2=None, op0=op0), reads, writes)
        else:
            P.op(P.dve, lambda e: e.tensor_scalar(out=out, in0=in0, scalar1=s1, scalar2=s2, op0=op0, op1=op1), reads, writes)

    def CP(eng, out, in_, reads, writes):
        P.op(eng, lambda e: e.tensor_copy(out=out, in_=in_), reads, writes)

    def MM(out, lhsT, rhs, start, stop, reads, writes, signal=True):
        P.op(P.pe, lambda e: e.matmul(out, lhsT=lhsT, rhs=rhs, start=start, stop=stop), reads, writes, signal=signal)

    dsem = {}

    def DMA(eng, sem, out, in_, reads=(), writes=()):
        if sem not in dsem:
            dsem[sem] = P.dma_sem(sem)
        return P.dma(eng, dsem[sem], out, in_, reads, writes)

    def stage(name):
        if stop_after == name:
            raise _Stop()

    def dump(name, ap, bufs, shape):
        if name in dbg and name not in dbg_out:
            t = nc.dram_tensor("dbg_" + name, list(shape), ap.dtype, kind="ExternalOutput").ap()
            dbg_out[name] = t
            DMA(P.sp, "dbg_" + name, t, ap, reads=bufs)

    try:
        DMA(P.sp, "c0", PV[:], pv_d.rearrange("l p n -> p l n"), writes=[b_PV])
        DMA(P.sp, "c1", CF[:], c_fm, writes=[b_CF])
        DMA(P.sp, "c2", TRI[:], tri_d, writes=[b_TRI])
        DMA(P.sp, "c3", IC0[:], ic0_d, writes=[b_IC0])
        DMA(P.sp, "c4", INVW[:], invw_d, writes=[b_INVW])
        P.op(P.dve, lambda e: e.memset(onesb[:], 1.0), writes=[b_onesb])
        P.op(P.dve, lambda e: e.memset(onesf[:], 1.0), writes=[b_onesf])
        P.op(P.dve, lambda e: e.memset(NH4[:], -0.5), writes=[b_NH4])
        P.op(P.dve, lambda e: e.memset(DUM[:], 1.0), writes=[b_DUM])
        DMA(P.sp, "c5", identf[:], identf_d, writes=[b_identf])
        P.op(P.dve, lambda e: e.memset(NH24[:], -0.5), writes=[b_NH24])
        P.op(P.dve, lambda e: e.memset(HC[:].rearrange("p l c t -> p (l c t)"), 0.0), writes=[b for bl in b_HC for b in bl])
        P.op(P.dve, lambda e: e.memset(ZC[:].rearrange("p l c t -> p (l c t)"), 0.0), writes=[b for bl in b_ZC for b in bl])
        P.op(P.dve, lambda e: e.memset(ZT[:].rearrange("p l j a b -> p (l j a b)"), 0.0), writes=b_ZT)
        ACT(CF2[:], CF[:], AF.Tanh, [b_CF], [b_CF2], scale=0.5)
        STT(CF2[:], CF2[:], 1.0, CF[:], ALU.add, ALU.mult, [b_CF2, b_CF], [b_CF2])
        TS(SCb[:], CF2[:], 0.5, None, ALU.mult, None, [b_CF2], [b_SCb])
        stage('s_consts')
        MS = Y32[:].rearrange("p c t -> p (c t)").bitcast(BF16).rearrange("p (s j k n) -> p s j k n", s=2, j=4, k=8)
        b_MS = P.bufs(2, "MS")
        for b in b_MS:
            b.al = tuple(b_Y32)
        for b in b_Y32:
            b.al = tuple(b_MS)
        for li in range(L):
            for g in range(12):
                s = (li * 12 + g) % 2
                DMA(P.pool, "ms%d" % s, MS[:, s], mod_d[li, g], writes=[b_MS[s]])
                for jj in range(4):
                    j = g * 4 + jj
                    for kc in range(8):
                        MM(PST[:, 0, li * 48 + j:li * 48 + j + 1], MS[:, s, jj, kc, :], SCb[:, kc:kc + 1], kc == 0, kc == 7,
                           [b_MS[s], b_SCb], [b_PST[0]], signal=(kc == 7))
        for li in range(L):
            TT(P.dve, MOD[:, li, :], PST[:, 0, li * 48:(li + 1) * 48], PV[:, li, PV_MB:PV_MB + 48], ALU.add, [b_PST[0], b_PV], [b_MOD])
        stage('s_mod')
        DMA(P.pool, "ci", identh[:], ident_d, writes=[b_identh])
        DMA(P.pool, "cp", PWb[:], pwbd_d.rearrange("l p c n -> p l c n"), writes=[b_PWb])
        stage('s_casts')
        for li in range(L):
            STT(DV[:, li, 0:8], MOD[:, li, 8:16], 1.0, PV[:, li, PV_PRE:PV_PRE + 8], ALU.add, ALU.mult, [b_MOD, b_PV], [b_DV])
            TT(P.dve, DV[:, li, 8:16], MOD[:, li, 16:24], PV[:, li, PV_POST:PV_POST + 8], ALU.mult, [b_MOD, b_PV], [b_DV])
            STT(DV[:, li, 16:24], MOD[:, li, 32:40], 1.0, PV[:, li, PV_FPRE:PV_FPRE + 8], ALU.add, ALU.mult, [b_MOD, b_PV], [b_DV])
            TT(P.dve, DV[:, li, 24:32], MOD[:, li, 40:48], PV[:, li, PV_FPOST:PV_FPOST + 8], ALU.mult, [b_MOD, b_PV], [b_DV])
            TS(DV[:, li, 32:38], PV[:, li, PV_CNG:PV_CNG + 6], 0.5, None, ALU.mult, None, [b_PV], [b_DV])
        stage('s_derived')
        WSF = TMP[:].rearrange("p a t -> p (a t)")[:, 0:768].rearrange("p (h t) -> p h t", h=6)
        b_WSF = P.buf("WSF")
        b_WSF.al = tuple(it[1] for it in r_TMP.items)
        for it in r_TMP.items:
            it[1].al = (b_WSF,)
        SBB = SS[:].rearrange("p a t -> p (a t)")[:, 0:384].rearrange("p (c t) -> p c t", c=3)
        b_SBB = P.buf("SBB")
        b_SBB.al = tuple(it[1] for it in r_SS.items)
        for it in r_SS.items:
            it[1].al = (b_SBB,)
        for li in range(L):
            DMA(P.sp, "wsf", WSF, wst_d[li], writes=[b_WSF])
            for c in range(3):
                DMA(P.sp, "sbb", SBB[0:64, c, :], sgub_d[li, 2 * c:2 * c + 1, :].broadcast_to([64, 128]), writes=[b_SBB])
                DMA(P.sp, "sbb", SBB[64:128, c, :], sgub_d[li, 2 * c + 1:2 * c + 2, :].broadcast_to([64, 128]), writes=[b_SBB])
            TT(P.dve, WSF, WSF, TRI[:].unsqueeze(1).broadcast_to([128, 6, 128]), ALU.mult, [b_WSF, b_TRI], [b_WSF])
            CP(P.dve, WST[:, li], WSF, [b_WSF], [b_WSTl[li]])
            for h in range(6):
                MM(PSG[:, h // 4, (h % 4) * 128:(h % 4 + 1) * 128], onesf[:], WSF[:, h, :], True, True, [b_onesf, b_WSF], [b_PSG[h // 4]])
            for c in range(3):
                for hh in range(2):
                    h = 2 * c + hh
                    sl = slice(64 * hh, 64 * hh + 64)
                    STT(CC[sl, li, c, :], PSG[sl, h // 4, (h % 4) * 128:(h % 4 + 1) * 128], PV[sl, li, PV_SB + c:PV_SB + c + 1], SBB[sl, c, :],
                        ALU.mult, ALU.add, [b_PSG[h // 4], b_PV, b_SBB], [b_CCl[li]])
        stage('s_sgu')

        stream = []
        for ti in range(ntiles):
            for li in range(L):
                for j in range(14):
                    stream.append(("A", win_d[li, j], win_b[li, j], b_win[li][j], ti == 0))
                for m in range(8):
                    stream.append(("A", wout_d[li, m], wout_b[li, m], b_wout[li][m], ti == 0))
                for j in range(NJ):
                    stream.append(("U", up_d[li, j], up_b[li, j], b_up[li][j], ti == 0))
                order = [("D", m) for m in range(6)] + [("E", m) for m in range(6)] + [("D", 6), ("E", 6), ("D", 7), ("E", 7)]
                for r_, m in order:
                    if r_ == "D":
                        stream.append(("D", dn_d[li, m, :, 0:16, :], dn_b[li, m, :, 0:16, :], b_dn[li][m], ti == 0))
                    else:
                        stream.append(("E", dn_d[li, m, :, 16:NJ, :], dn_b[li, m, :, 16:NJ, :], b_dne[li][m], ti == 0))
        rings = {"A": (WA, b_WA, 4), "U": (WU, b_WU, 3), "D": (WD, b_WD, 2), "E": (WE, b_WE, 2)}
        issued = {"A": 0, "U": 0, "D": 0, "E": 0}
        consumed = {"A": 0, "U": 0, "D": 0, "E": 0}
        nxt = [0]

        def pump():
            while nxt[0] < len(stream):
                r, src32, scr, sb, first = stream[nxt[0]]
                ten, bl, ns = rings[r]
                if issued[r] - consumed[r] >= ns:
                    break
                s = issued[r] % ns
                if first:
                    DMA(P.pool, "wc%s%d" % (r, s), ten[:, s], src32, writes=[bl[s]])
                    if ntiles > 1:
                        DMA(P.sp, "ws%s%d" % (r, s), scr, ten[:, s], reads=[bl[s]], writes=[sb])
                else:
                    DMA(P.sp, "w%s%d" % (r, s), ten[:, s], scr, reads=[sb], writes=[bl[s]])
                issued[r] += 1
                nxt[0] += 1

        def wslot(r):
            ten, bl, ns = rings[r]
            assert consumed[r] < issued[r], "weight chunk not issued"
            s = consumed[r] % ns
            return ten[:, s], bl[s]

        def wdone(r):
            consumed[r] += 1
            pump()

        sc_i = [0]

        def preload(func):
            ACT(DUM[:, 1:2], DUM[:, 0:1], func, [b_DUM], [b_DUM])

        def tm_sum(chunks, fp32=False):
            col = 4 * (sc_i[0] % 16)
            sc_i[0] += 1
            oc = onesf[:, 0:1] if fp32 else onesb[:, 0:1]
            bo = b_onesf if fp32 else b_onesb
            n = len(chunks)
            for tb in range(4):
                for i, (ap, b) in enumerate(chunks):
                    MM(PST[:, 0, col + tb:col + tb + 1], ap[:, tb * 128:(tb + 1) * 128], oc, i == 0, i == n - 1, [b, bo], [b_PST[0]],
                       signal=(tb == 3 and i == n - 1))
            return col

        def small(col, scale, bias):
            a, ba = r_S4.get()
            ACT(a, PST[:, 0, col:col + 4], AF.Identity, [b_PST[0]], [ba], scale=scale, bias=bias)
            return a, ba

        def rpow(a, ba):
            r, br = r_S4.get()
            TT(P.pool, r, a, NH4[:], ALU.pow, [ba, b_NH4], [br])
            return r, br

        def bcast(a, ba):
            d4, bd4 = r_D4.get()
            TT(P.dve, d4, identf[:].unsqueeze(1).broadcast_to([128, 4, 128]), a.unsqueeze(2).broadcast_to([128, 4, 128]), ALU.mult,
               [b_identf, ba], [bd4])
            pb = gbank()
            MM(PSG[:, pb, :], onesf[:], d4.rearrange("p a b -> p (a b)"), True, True, [b_onesf, bd4], [b_PSG[pb]])
            return PSG[:, pb, :], b_PSG[pb]

        def rstd_of(chunks, scale, eps):
            n = len(chunks)
            for i, (ap, b) in enumerate(chunks):
                MM(PST[:, 1, :], onesb[:], ap, i == 0, i == n - 1, [b_onesb, b], [b_PST[1]], signal=(i == n - 1))
            r, br = r_SS.get()
            ACT(r, PST[:, 1, :], AF.Ln, [b_PST[1]], [br], scale=scale, bias=eps)
            ACT(r, r, AF.Exp, [br], [br], scale=-0.5)
            return r, br

        def prenorm(li, gcol, shcol, have_sq, xa=False, nxt=None):
            SQ8 = YT
            X, bX = (XA, b_XA) if xa else (XT, b_XT)
            if not have_sq:
                for c in range(8):
                    ACT(SQ8[:, c, :], X[:, c, :], AF.Square, [bX[c]], [b_YT[c]])
            r, br = rstd_of([(SQ8[:, c, :], b_YT[c]) for c in range(8)], 1.0 / D_MODEL, EPS)
            for c in range(8):
                t, bt = r_TMP.get()
                TT(P.dve, t, X[:, c, :], r, ALU.mult, [bX[c], br], [bt])
                ACT(HT[:, c, :], t, AF.Identity, [bt, b_DV, b_MOD], [b_HT[c]],
                    scale=DV[:, li, gcol + c:gcol + c + 1], bias=MOD[:, li, shcol + c:shcol + c + 1])
            if xa:
                for c in range(8):
                    ACT(XT[:, c, :], XA[:, c, :], AF.Copy, [b_XA[c]], [b_XT[c]])

        def postnorm_residual(li, ggcol, eps, produce, sq_next):
            SQ8 = HT
            for m in range(8):
                pb = produce(m)
                CP(P.dve, Y32[:, m, :], PSG[:, pb, :], [b_PSG[pb]], [b_Y32[m]])
                ACT(SQ8[:, m, :], Y32[:, m, :], AF.Square, [b_Y32[m]], [b_HT[m]])
            r, br = rstd_of([(SQ8[:, m, :], b_HT[m]) for m in range(8)], 1.0 / D_MODEL, eps)
            for m in range(8):
                t, bt = r_TMP.get()
                STT(t, Y32[:, m, :], DV[:, li, ggcol + m:ggcol + m + 1], r, ALU.mult, ALU.mult, [b_Y32[m], b_DV, br], [bt])
                TT(P.dve, XT[:, m, :], XT[:, m, :], t, ALU.add, [b_XT[m], bt], [b_XT[m]])
                if sq_next:
                    ACT(YT[:, m, :], XT[:, m, :], AF.Square, [b_XT[m]], [b_YT[m]])

        def mixer(ti, li, have_sq, xa=False):
            prenorm(li, 0, 0, have_sq, xa, AF.Tanh)
            dump("h1", HT[:], b_HT, [128, 8, T])
            hcl = HC[:, li]
            zcl = ZC[:, li]
            stage('m_prenorm')
            if ti > 0:
                ACT(hcl[:, :, 0:30], hcl[:, :, T:T + 30], AF.Copy, b_HC[li], b_HC[li])
                ACT(zcl[:, :, 0:15], zcl[:, :, T:T + 15], AF.Copy, b_ZC[li], b_ZC[li])
            def inproj_fm():
                w, bw = wslot("A")
                pb = gbank()
                for kc in range(8):
                    MM(PSG[:, pb, :], w[:, kc, :], HT[:, kc, :], kc == 0, kc == 7, [bw, b_HT[kc]], [b_PSG[pb]], signal=(kc == 7))
                wdone("A")
                return pb

            csq = []

            def conv_chunk(c):
                dg, bdg = dgs[c]
                pb = gbank()
                for k in range(31):
                    MM(PSG[:, pb, :], dg[:, k, :], hcl[:, c, k:k + T], k == 0, k == 30, [bdg, b_HC[li][c]], [b_PSG[pb]], signal=(k == 30))
                ACT(Y32[:, 3 + c, :], PSG[:, pb, :], AF.Identity, [b_PSG[pb], b_PV], [b_Y32[3 + c]], bias=PV[:, li, PV_CB + c:PV_CB + c + 1])
                q, bq = r_SQ.get()
                ACT(q, Y32[:, 3 + c, :], AF.Square, [b_Y32[3 + c]], [bq])
                csq.append((q, bq))

            def v_path_and_pool():
                def pool_sums():
                    TT(P.dve, PA[:, :, 1:527], zcl[:, :, 1:527], zcl[:, :, 0:526], ALU.add, b_ZC[li], [b_PA])
                    TT(P.dve, PB[64:128, 0, 3:527], PA[64:128, 0, 3:527], PA[64:128, 0, 1:525], ALU.add, [b_PA], [b_PB])
                    TT(P.dve, PB[:, 1, 3:527], PA[:, 1, 3:527], PA[:, 1, 1:525], ALU.add, [b_PA], [b_PB])
                    TT(P.dve, PA[:, 1, 7:527], PB[:, 1, 7:527], PB[:, 1, 3:523], ALU.add, [b_PB, b_PA], [b_PA])
                    TT(P.dve, PB[64:128, 1, 15:527], PA[64:128, 1, 15:527], PA[64:128, 1, 7:519], ALU.add, [b_PA, b_PB], [b_PB])
                st = VST[:]
                for tb in range(4):
                    v, bv = vvs[tb]
                    v3 = v.rearrange("p (h d) -> p h d", h=6)
                    P.op(P.dve, lambda e, v3=v3, tb=tb: e.tensor_reduce(out=st[:, 0, tb * 6:tb * 6 + 6], in_=v3, axis=AX.X, op=ALU.add), [bv], [b_VST])
                    ACT(VSQ[:], v, AF.Square, [bv], [b_VSQ])
                    P.op(P.dve, lambda e, tb=tb: e.tensor_reduce(out=st[:, 1, tb * 6:tb * 6 + 6], in_=VSQ[:].rearrange("p (h d) -> p h d", h=6), axis=AX.X,
                                                                  op=ALU.add), [b_VSQ], [b_VST])
                TS(st[:, 2, :], st[:, 0, :], 1.0 / 64, None, ALU.mult, None, [b_VST], [b_VST])
                TT(P.dve, st[:, 3, :], st[:, 2, :], st[:, 2, :], ALU.mult, [b_VST], [b_VST])
                STT(st[:, 3, :], st[:, 1, :], 1.0 / 64, st[:, 3, :], ALU.mult, ALU.subtract, [b_VST], [b_VST])
                TS(st[:, 3, :], st[:, 3, :], EPS, None, ALU.add, None, [b_VST], [b_VST])
                TT(P.pool, st[:, 4, :], st[:, 3, :], NH24[:], ALU.pow, [b_VST, b_NH24], [b_VST2])
                pool_sums()
                stage('m_v')
                srcs = [(PA, 0, 0), (PB, 0, 1), (PA, 1, 0), (PB, 1, 1)]
                for g in range(4):
                    ten, c, hh = srcs[g]
                    sl = slice(64 * hh, 64 * hh + 64)
                    STT(DD[sl, c, :], ten[sl, c, 15:527], INVW[sl, c:c + 1], zcl[sl, c, 15:527], ALU.mult, ALU.subtract,
                        [b_PA, b_PB, b_INVW, b_ZC[li][c]], [b_DD[c]])
                    if ti == 0:
                        t, bt = r_TMP.get()
                        TT(P.dve, t[sl, 0:16], ten[sl, c, 15:31], IC0[sl, c, :], ALU.mult, [b_PA, b_PB, b_IC0], [bt])
                        TT(P.dve, DD[sl, c, 0:16], t[sl, 0:16], zcl[sl, c, 15:31], ALU.subtract, [bt, b_ZC[li][c]], [b_DD[c]])

                for tb in range(4):
                    v, bv = vvs[tb]
                    v3 = v.rearrange("p (h d) -> p h d", h=6)
                    TT(P.dve, v3, v3, st[:, 2, tb * 6:tb * 6 + 6].unsqueeze(2).broadcast_to([128, 6, 64]), ALU.subtract, [bv, b_VST], [bv])
                    TT(P.dve, VN[:, tb, :].rearrange("p (h d) -> p h d", h=6), v3,
                       st[:, 4, tb * 6:tb * 6 + 6].unsqueeze(2).broadcast_to([128, 6, 64]), ALU.mult, [bv, b_VST2], [b_VN[tb]])

            def pool_mm():
                for c in range(2):
                    pb = gbank()
                    MM(PSG[:, pb, :], PWb[:, li, c, :], DD[:, c, :], True, True, [b_PWb, b_DD[c]], [b_PSG[pb]])
                    ACT(Y32[:, 6 + c, :], PSG[:, pb, :], AF.Identity, [b_PSG[pb], b_PV], [b_Y32[6 + c]], scale=PV[:, li, PV_PS + c:PV_PS + c + 1])
                dump("yc", Y32[:, 6:8, :], b_Y32[6:8], [128, 2, T])
                stage('m_pool')

            def sgu_mix():
                for c in range(3):
                    pb = gbank()
                    for tb in range(4):
                        for hh in range(2):
                            h = 2 * c + hh
                            MM(PSG[64 * hh:64 * hh + 64, pb, tb * 128:(tb + 1) * 128], VN[:, tb, h * 64:(h + 1) * 64], WST[:, li, h, :], True, True,
                               [b_VN[tb], b_WSTl[li]], [b_PSG[pb]], signal=(tb == 3 and hh == 1))
                    t, bt = r_TMP.get()
                    STT(t.rearrange("p (b t) -> p b t", b=4), PSG[:, pb, :].rearrange("p (b t) -> p b t", b=4), PV[:, li, PV_SG + c:PV_SG + c + 1],
                        CC[:, li, c, :].unsqueeze(1).broadcast_to([128, 4, 128]), ALU.mult, ALU.add, [b_PSG[pb], b_PV, b_CCl[li]], [bt])
                    TT(P.dve, Y32[:, c, :], UU[:, c, :], t, ALU.mult, [b_UU[c], bt], [b_Y32[c]])
                dump("ya", Y32[:, 0:3, :], b_Y32[0:3], [128, 3, T])
                stage('m_sgu')

            for c in range(3):
                pg = inproj_fm()
                th, bth = r_TH.get()
                ACT(th, PSG[:, pg, :], AF.Tanh, [b_PSG[pg]], [bth], scale=0.5)
                pa = inproj_fm()
                STT(hcl[:, c, 30:30 + T], th, 1.0, PSG[:, pa, :], ALU.add, ALU.mult, [bth, b_PSG[pa]], [b_HC[li][c]])
            stage('m_glu')
            dgs = []
            for c in range(3):
                if ti == 0:
                    TT(P.dve, DG[:, c], identh[:].unsqueeze(1).broadcast_to([128, 31, 128]),
                       PV[:, li, PV_CW + 31 * c:PV_CW + 31 * c + 31].unsqueeze(2).broadcast_to([128, 31, 128]), ALU.mult,
                       [b_identh, b_PV], [b_DG[c]])
                    if ntiles > 1:
                        DMA(P.sp, "dgw%d" % c, dg_b[li, c], DG[:, c], reads=[b_DG[c]], writes=[b_dgb[li][c]])
                else:
                    DMA(P.act, "dg%d" % c, DG[:, c], dg_b[li, c], reads=[b_dgb[li][c]], writes=[b_DG[c]])
                dgs.append((DG[:, c], b_DG[c]))
            vb = [gbank() for _ in range(4)]
            for jv in range(3):
                w, bw = wslot("A")
                for tb in range(4):
                    for kc in range(8):
                        MM(PSG[:, vb[tb], jv * 128:(jv + 1) * 128], HT[:, kc, tb * 128:(tb + 1) * 128], w[:, kc, :], kc == 0, kc == 7,
                           [bw, b_HT[kc]], [b_PSG[vb[tb]]], signal=(kc == 7))
                wdone("A")
            vvs = []
            for tb in range(4):
                v = VV4[:, tb, :]
                bv = b_VV4[tb]
                ACT(v, PSG[:, vb[tb], 0:384], AF.Gelu, [b_PSG[vb[tb]]], [bv])
                vvs.append((v, bv))
            for c in range(2):
                pb = inproj_fm()
                ACT(zcl[:, c, 15:15 + T], PSG[:, pb, :], AF.Copy, [b_PSG[pb]], [b_ZC[li][c]])
            stage('m_inproj')
            conv_chunk(0)
            v_path_and_pool()
            conv_chunk(1)
            conv_chunk(2)
            pm = gbank()
            for c in range(3):
                MM(PSG[:, pm, :], onesf[:], Y32[:, 3 + c, :], c == 0, c == 2, [b_onesf, b_Y32[3 + c]], [b_PSG[pm]], signal=(c == 2))
            for i, (q, bq) in enumerate(csq):
                MM(PST[:, 1, :], onesb[:], q, i == 0, i == 2, [b_onesb, bq], [b_PST[1]], signal=(i == 2))
            m2, bm2 = r_TMP.get()
            ACT(m2, PSG[:, pm, :], AF.Square, [b_PSG[pm]], [bm2], scale=1.0 / 384)
            ACT(CMN[:], PSG[:, pm, :], AF.Identity, [b_PSG[pm]], [b_CMN], scale=-1.0 / 384)
            rs, brs = CRS[:], b_CRS
            STT(rs, PST[:, 1, :], 1.0 / 384, m2, ALU.mult, ALU.subtract, [b_PST[1], bm2], [brs])
            ACT(rs, rs, AF.Ln, [brs], [brs], bias=EPS)
            ACT(rs, rs, AF.Exp, [brs], [brs], scale=-0.5)
            for c in range(3):
                pb = inproj_fm()
                ACT(UU[:, c, :], PSG[:, pb, :], AF.Gelu, [b_PSG[pb]], [b_UU[c]])
            stage('m_u')
            pool_mm()
            sgu_mix()

            def branch_norm(c0, c1, n, eps):
                sqs = []
                for c in range(c0, c1):
                    q, bq = r_SQ.get()
                    ACT(q, Y32[:, c, :], AF.Square, [b_Y32[c]], [bq])
                    sqs.append((q, bq))
                r, br = rstd_of(sqs, 1.0 / n, eps)
                for c in range(c0, c1):
                    STT(YT[:, c, :], Y32[:, c, :], PV[:, li, PV_BR + c:PV_BR + c + 1], r, ALU.mult, ALU.mult, [b_Y32[c], b_PV, br], [b_YT[c]])
            for c in range(3):
                t, bt = r_TMP.get()
                TT(P.dve, t, Y32[:, 3 + c, :], CMN[:], ALU.add, [b_Y32[3 + c], b_CMN], [bt])
                TT(P.dve, t, t, rs, ALU.mult, [bt, brs], [bt])
                th, bth = r_TH.get()
                ACT(th, t, AF.Tanh, [bt, b_DV], [bth], scale=DV[:, li, 32 + c:33 + c], bias=DV[:, li, 35 + c:36 + c])
                l_, bl_ = r_SS.get()
                ACT(l_, t, AF.Identity, [bt, b_PV], [bl_], scale=PV[:, li, PV_CNG + c:PV_CNG + c + 1], bias=PV[:, li, PV_CNB + c:PV_CNB + c + 1])
                STT(Y32[:, 3 + c, :], th, 1.0, l_, ALU.add, ALU.mult, [bth, bl_], [b_Y32[3 + c]])
            dump("yb", Y32[:, 3:6, :], b_Y32[3:6], [128, 3, T])
            stage('m_conv')
            branch_norm(3, 6, 384, 4 * EPS)
            branch_norm(6, 8, 256, EPS)
            branch_norm(0, 3, 384, EPS)
            dump("yT", YT[:], b_YT, [128, 8, T])
            stage('m_brnorm')

            def produce(m):
                w, bw = wslot("A")
                pb = gbank()
                for kc in range(8):
                    MM(PSG[:, pb, :], w[:, kc, :], YT[:, kc, :], kc == 0, kc == 7, [bw, b_YT[kc]], [b_PSG[pb]], signal=(kc == 7))
                wdone("A")
                return pb
            postnorm_residual(li, 8, EPS, produce, True)
            stage('m_out')
            dump("x1", XT[:], b_XT, [128, 8, T])

        def ffn(ti, li, last):
            prenorm(li, 16, 24, True, False, AF.Gelu)
            dump("h2", HT[:], b_HT, [128, 8, T])
            ztl = ZT[:, li]
            fw = lambda k, j: PV[:, li, PV_FW + 44 * k + j:PV_FW + 44 * k + j + 1]
            fwv = lambda k: PV[:, li, PV_FW + 44 * k:PV_FW + 44 * k + 44].rearrange("p (g j) -> p j g", g=2)
            TT(P.dve, CORR[:, :, :, 0], ztl[:, :, :, 1], fwv(1), ALU.mult, [b_ZT[li], b_PV], [b_CORR])
            TT(P.dve, CT[:, :, :, 0], ztl[:, :, :, 0], fwv(0), ALU.mult, [b_ZT[li], b_PV], [b_CT])
            TT(P.dve, CORR[:, :, :, 0], CORR[:, :, :, 0], CT[:, :, :, 0], ALU.add, [b_CORR, b_CT], [b_CORR])
            TT(P.dve, CORR[:, :, :, 1], ztl[:, :, :, 1], fwv(0), ALU.mult, [b_ZT[li], b_PV], [b_CORR])
            def fin_act(p):
                gl, bgl = r_GL.get()
                ACT(gl, p[1][:, 0, :], AF.Gelu, [p[2]], [bgl])
                return gl, bgl

            def fin_dve(p, glp):
                TT(P.dve, AT[:, p[0], :], glp[0], p[1][:, 1, :], ALU.mult, [glp[1], p[2]], [b_AT[p[0]]])
            pend = None
            for j in range(NJ):
                w, bw = wslot("U")
                p0 = gpair()
                for g in range(2):
                    for kc in range(8):
                        MM(PSG[:, p0 + g, :], w[:, kc, g * 128:(g + 1) * 128], HT[:, kc, :], kc == 0, kc == 7, [bw, b_HT[kc]], [b_PSG[p0 + g]],
                           signal=(kc == 7))
                wdone("U")
                acc, bacc = r_ACC.get()
                pbs = [b_PSG[p0], b_PSG[p0 + 1]]
                for g in range(2):
                    jj = j + 22 * g
                    ACT(acc[:, g, :], PSG[:, p0 + g, :], AF.Identity, [b_PSG[p0 + g], b_PV], [bacc], scale=fw(2, jj),
                        bias=PV[:, li, PV_FB + jj:PV_FB + jj + 1])
                if pend is not None:
                    glp = fin_act(pend)
                for g in range(2):
                    jj = j + 22 * g
                    STT(acc[:, g, 1:T], PSG[:, p0 + g, 0:T - 1], fw(1, jj), acc[:, g, 1:T], ALU.mult, ALU.add, [b_PSG[p0 + g], b_PV, bacc], [bacc])
                    STT(acc[:, g, 2:T], PSG[:, p0 + g, 0:T - 2], fw(0, jj), acc[:, g, 2:T], ALU.mult, ALU.add, [b_PSG[p0 + g], b_PV, bacc], [bacc])
                CP(P.dve, ztl[:, j, :, :], PSG[:, p0:p0 + 2, T - 2:T], pbs, [b_ZT[li]])
                TT(P.dve, acc[:, :, 0:2], acc[:, :, 0:2], CORR[:, j, :, :], ALU.add, [bacc, b_CORR], [bacc])
                if pend is not None:
                    fin_dve(pend, glp)
                pend = (j, acc, bacc)
            glp = fin_act(pend)
            fin_dve(pend, glp)
            stage('f_up')
            dump("aT", AT[:, 0:4, :], b_AT[0:4], [128, 4, T])

            first_pair = pair_i[0] % 3
            dbanks = [(2 * ((first_pair + i // 2) % 3) + i % 2) for i in range(6)]
            for m in range(6):
                w, bw = wslot("D")
                for kc in range(16):
                    MM(PSG[:, dbanks[m], :], w[:, kc, :], AT[:, kc, :], kc == 0, False, [bw, b_AT[kc]], [b_PSG[dbanks[m]]], signal=(kc == 15))
                wdone("D")

            def produce(m):
                if m < 6:
                    pb = dbanks[m]
                else:
                    pb = dbanks[m - 6]
                    w, bw = wslot("D")
                    for kc in range(16):
                        MM(PSG[:, pb, :], w[:, kc, :], AT[:, kc, :], kc == 0, False, [bw, b_AT[kc]], [b_PSG[pb]], signal=(kc == 15))
                    wdone("D")
                w, bw = wslot("E")
                for kc in range(16, NJ):
                    MM(PSG[:, pb, :], w[:, kc - 16, :], AT[:, kc, :], False, kc == NJ - 1, [bw, b_AT[kc]], [b_PSG[pb]], signal=(kc == NJ - 1))
                wdone("E")
                return pb
            postnorm_residual(li, 24, EPS, produce, not last)
            dump("x2", XT[:], b_XT, [128, 8, T])

        pump()
        for ti in range(ntiles):
            if ti == 0:
                DMA(P.sp, "xin", XT[:], x_fm[ti], writes=b_XT)
            for li in range(L):
                mixer(ti, li, li > 0, xa=(ti > 0 and li == 0))
                if li == L - 1 and ti + 1 < ntiles:
                    DMA(P.sp, "xin", XA, x_fm[ti + 1], writes=b_XA)
                ffn(ti, li, li == L - 1)
            DMA(P.sp, "xout", out_fm[ti], XT[:], reads=b_XT)

    except _Stop:
        pass
    P.barrier()
    print('sbuf bytes remaining', nc.sbuf_bytes_remaining)
    P.emit()
    return nc, dbg_out


def _fm(v, n):
    return np.ascontiguousarray(np.asarray(v, np.float32).reshape(n, 128).T)


def prep_shared(inp, layers):
    g = lambda k: np.asarray(inp[k], np.float32)
    pv, modc, winc, woutc, upc, dnc, wst, sgub, pwbd = [], [], [], [], [], [], [], [], []
    for l in layers:
        cols = [_fm(g("mix_pre_g")[l], 8), _fm(g("mix_post_g")[l], 8), _fm(g("branch_g")[l], 8), _fm(g("ffn_pre_g")[l], 8),
                _fm(g("ffn_post_g")[l], 8), _fm(g("sgu_norm_g")[l], 3), _fm(g("sgu_norm_b")[l], 3), _fm(g("conv_b")[l], 3),
                _fm(g("conv_norm_g")[l], 3), _fm(g("conv_norm_b")[l], 3), _fm(g("pool_scale")[l], 2)]
        cw = g("conv_w")[l].reshape(31, 3, 128).transpose(2, 1, 0).reshape(128, 93)
        fw = g("ffn_conv_w")[l].reshape(3, 44, 128).transpose(2, 0, 1).reshape(128, 132)
        cols += [cw, fw, _fm(g("ffn_conv_b")[l], 44), _fm(g("mod_b")[l], 48)]
        p = np.concatenate(cols, axis=1)
        assert p.shape == (128, NPV)
        pv.append(p)
        modc.append(g("mod_w")[l].reshape(8, 128, 12, 4, 128).transpose(2, 1, 3, 0, 4))
        winc.append(g("w_in")[l].reshape(8, 128, 14, 128).transpose(2, 1, 0, 3)[IN_ORDER])
        woutc.append(g("w_out")[l].reshape(8, 128, 8, 128).transpose(2, 1, 0, 3))
        upc.append(g("ffn_up")[l].reshape(8, 128, 2, NJ, 128).transpose(3, 1, 0, 2, 4).reshape(NJ, 128, 8, 256))
        dnc.append(g("ffn_down")[l].reshape(NJ, 128, 8, 128).transpose(2, 1, 0, 3))
        wst.append(g("sgu_w")[l].transpose(2, 0, 1))
        sgub.append(g("sgu_b")[l])
        pw = g("pool_w")[l]
        bd = np.zeros((128, 2, 128), np.float32)
        for c in range(2):
            bd[0:64, c, 0:64] = pw[2 * c]
            bd[64:128, c, 64:128] = pw[2 * c + 1]
        pwbd.append(bd)
    ca = lambda lst: np.ascontiguousarray(np.stack(lst, 0), dtype=np.float32)
    s_idx = np.arange(128)
    tri = (s_idx[None, :] >= s_idx[:, None]).astype(np.float32)
    wins = np.array([2, 4, 8, 16], np.float32)
    ic0 = np.zeros((128, 2, 16), np.float32)
    invw = np.zeros((128, 2), np.float32)
    for c in range(2):
        for hh in range(2):
            w_ = wins[2 * c + hh]
            ic0[64 * hh:64 * hh + 64, c, :] = 1.0 / np.minimum(np.arange(1, 17, dtype=np.float32), w_)
            invw[64 * hh:64 * hh + 64, c] = 1.0 / w_
    return {"pv": ca(pv), "mod_c": ca(modc), "w_in_c": ca(winc), "w_out_c": ca(woutc), "up_c": ca(upc), "down_c": ca(dnc),
            "wst": ca(wst), "sgub": ca(sgub), "pwbd": ca(pwbd), "identh": (0.5 * np.eye(128)).astype(np.float32), "identf": np.eye(128, dtype=np.float32), "tri": tri,
            "ic0": ic0, "invw": invw}


def x_to_fm(xb, ntiles=NT):
    return np.ascontiguousarray(np.asarray(xb, np.float32).reshape(ntiles, T, 8, 128).transpose(0, 3, 2, 1))


def fm_to_x(o):
    nt = o.shape[0]
    return np.ascontiguousarray(o.transpose(0, 3, 2, 1).reshape(nt * T, D_MODEL))


_CACHE = {}


def _get_prog(layers):
    key = tuple(layers)
    if key not in _CACHE:
        _CACHE[key] = build(list(range(len(layers))))[0]
    return _CACHE[key]


def run_layers(x, inputs, layers):
    shared = prep_shared(inputs, layers)
    c = np.asarray(inputs["c"], np.float32)
    nc = _get_prog(layers)
    in_maps = []
    for b in range(BATCH):
        m = dict(shared)
        m["x_fm"] = x_to_fm(x[b])
        m["c_fm"] = _fm(c[b], 8)
        in_maps.append(m)
    res = run_bass_kernel_spmd(nc, in_maps, core_ids=list(range(BATCH)))
    return np.stack([fm_to_x(res.results[b]["out_fm"]) for b in range(BATCH)], 0)


FUSED = True


def kernel(**inputs):
    x = np.asarray(inputs["x"], np.float32)
    if FUSED:
        out = run_layers(x, inputs, list(range(DEPTH)))
    else:
        out = x
        for l in range(DEPTH):
            out = run_layers(out, inputs, [l])
    return out.astype(np.float32)
```
